# Optimizing a Trainium2 kernel written in Bass

```python
import math
import jax, jax.numpy as jnp
from jax import lax
import numpy as np

D_MODEL = 1024
BATCH = 8
SEQ = 2048
DEPTH = 1
DEC_BATCH = 16
DEC_SEQ = 32
PAST_LEN = 2048

CHUNK = 64
Q_BLOCK = 128
DA_HEADS = 8
DA_HD = 64
DA_V = 2 * DA_HD
LRU_W = D_MODEL
LRU_BLOCKS = 8
LRU_BW = LRU_W // LRU_BLOCKS
CONV_W = 4
LRU_C = 8.0
XA_HEADS = 4
XA_HD = 256
N_MEM = 256
N_BUCKETS = 32
MAX_DISTANCE = 128
D_FF = ((8 * D_MODEL + 3 * 256 - 1) // (3 * 256)) * 256
N_BRANCH = 3
EPS = 1e-6
DA_QK_W = DA_HEADS * 2 * DA_HD
DA_V_W = DA_HEADS * DA_V
XA_W = XA_HEADS * XA_HD
IN_SPLITS = (DA_QK_W, 2 * DA_QK_W, 2 * DA_QK_W + DA_V_W, 2 * DA_QK_W + DA_V_W + LRU_W, 2 * DA_QK_W + DA_V_W + 2 * LRU_W)
IN_W = 2 * DA_QK_W + DA_V_W + 2 * LRU_W + XA_W

kernel_name = "hybrid_diffattn_rglru_streaming_step"


def rms_norm(x, g):
    x32 = x.astype(jnp.float32)
    y = x32 * lax.rsqrt(jnp.mean(x32 * x32, axis=-1, keepdims=True) + EPS)
    return (y * g.astype(jnp.float32)).astype(x.dtype)


def rel_bucket(rel):
    half = N_BUCKETS // 2
    max_exact = half // 2
    n = jnp.abs(rel)
    nf = jnp.maximum(n, 1).astype(jnp.float32)
    large = max_exact + (jnp.log(nf / max_exact) / math.log(MAX_DISTANCE / max_exact) * (half - max_exact)).astype(jnp.int32)
    large = jnp.minimum(large, half - 1)
    return jnp.where(rel > 0, half, 0) + jnp.where(n < max_exact, n, large)


def rel_bias(q_pos, k_pos, rel_table):
    b = rel_bucket(k_pos[None, :] - q_pos[:, None])
    return jnp.moveaxis(rel_table[b], -1, 0).astype(jnp.float32)


def diff_attention(q, k, v, q_pos, k_pos, rel_table, lam, subln_g, lambda_init):
    logits = jnp.einsum("bqhcd,bkhcd->bhcqk", q, k).astype(jnp.float32) * (DA_HD ** -0.5)
    logits = logits + rel_bias(q_pos, k_pos, rel_table)[None, :, None]
    visible = (k_pos[None, :] // CHUNK) <= (q_pos[:, None] // CHUNK)
    logits = jnp.where(visible, logits, -1e30)
    p = jax.nn.softmax(logits, axis=-1)
    w = p[:, :, 0] - lam * p[:, :, 1]
    out = jnp.einsum("bhqk,bkhe->bqhe", w.astype(v.dtype), v)
    return rms_norm(out, subln_g) * (1.0 - lambda_init)


def blocked_queries(attend, q, q_pos):
    B, S = q.shape[:2]
    nb = S // Q_BLOCK
    qb = jnp.moveaxis(q.reshape((B, nb, Q_BLOCK) + q.shape[2:]), 1, 0)
    pb = q_pos.reshape(nb, Q_BLOCK)
    out = lax.map(lambda a: attend(a[0], a[1]), (qb, pb))
    return jnp.moveaxis(out, 0, 1).reshape((B, S) + out.shape[3:])


def lru_combine(left, right):
    a_l, b_l = left
    a_r, b_r = right
    return a_l * a_r, a_r * b_l + b_r


def rglru_branch(xb, gb, conv_state, h0, conv_w, conv_b, w_rg_a, b_rg_a, w_rg_x, b_rg_x, rg_lambda):
    B, S, _ = xb.shape
    xpad = jnp.concatenate([conv_state.astype(xb.dtype), xb], axis=1)
    xc = conv_b + sum(xpad[:, j:j + S] * conv_w[j] for j in range(CONV_W))
    new_conv = xpad[:, -(CONV_W - 1):]
    xh = xc.reshape(B, S, LRU_BLOCKS, LRU_BW)
    r = jax.nn.sigmoid((jnp.einsum("bsni,nij->bsnj", xh, w_rg_a).reshape(B, S, LRU_W) + b_rg_a).astype(jnp.float32))
    i = jax.nn.sigmoid((jnp.einsum("bsni,nij->bsnj", xh, w_rg_x).reshape(B, S, LRU_W) + b_rg_x).astype(jnp.float32))
    log_a = -LRU_C * r * jax.nn.softplus(-rg_lambda.astype(jnp.float32))
    a = jnp.exp(log_a)
    mult = jnp.sqrt(-jnp.expm1(2.0 * log_a))
    u = mult * (i * xc.astype(jnp.float32))
    u = u.at[:, 0].add(a[:, 0] * h0.astype(jnp.float32))
    _, h = lax.associative_scan(lru_combine, (a, u), axis=1)
    out = h.astype(xb.dtype) * jax.nn.gelu(gb)
    return out, new_conv, h[:, -1]


def memory_kv(mem, g, w_mem_kv):
    B, M, _ = mem.shape
    kv = (rms_norm(mem, g) @ w_mem_kv).reshape(B, M, 2, XA_HEADS, XA_HD)
    return kv[:, :, 0], kv[:, :, 1]


def cross_attention(q, mem_k, mem_v):
    logits = jnp.einsum("bqhd,bmhd->bhqm", q, mem_k).astype(jnp.float32) * (XA_HD ** -0.5)
    p = jax.nn.softmax(logits, axis=-1)
    return jnp.einsum("bhqm,bmhd->bqhd", p.astype(mem_v.dtype), mem_v)


def layer_forward(x, pos, past_k, past_v, past_pos, conv_state, h0, mem_k, mem_v, rel_table, lp, lambda_init):
    B, S, _ = x.shape
    h = rms_norm(x, lp["norm_mix"])
    q, k, v, xb, gb, qc = jnp.split(h @ lp["w_in"], IN_SPLITS, axis=-1)
    q = q.reshape(B, S, DA_HEADS, 2, DA_HD)
    k = k.reshape(B, S, DA_HEADS, 2, DA_HD)
    v = v.reshape(B, S, DA_HEADS, DA_V)
    if past_k is None:
        k_all, v_all, k_pos = k, v, pos
    else:
        k_all = jnp.concatenate([past_k.astype(k.dtype), k], axis=1)
        v_all = jnp.concatenate([past_v.astype(v.dtype), v], axis=1)
        k_pos = jnp.concatenate([past_pos, pos])
    lam = (jnp.exp(jnp.sum(lp["lambda_q1"] * lp["lambda_k1"]).astype(jnp.float32))
           - jnp.exp(jnp.sum(lp["lambda_q2"] * lp["lambda_k2"]).astype(jnp.float32)) + lambda_init)

    def attend(qq, pp):
        return diff_attention(qq, k_all, v_all, pp, k_pos, rel_table, lam, lp["subln_g"], lambda_init)

    if S > Q_BLOCK:
        a_out = blocked_queries(attend, q, pos)
    else:
        a_out = attend(q, pos)
    if conv_state is None:
        conv_state = jnp.zeros((B, CONV_W - 1, LRU_W), xb.dtype)
        h0 = jnp.zeros((B, LRU_W), jnp.float32)
    b_out, new_conv, h_last = rglru_branch(xb, gb, conv_state, h0, lp["conv_w"], lp["conv_b"], lp["w_rg_a"],
                                           lp["b_rg_a"], lp["w_rg_x"], lp["b_rg_x"], lp["rg_lambda"])
    c_out = cross_attention(qc.reshape(B, S, XA_HEADS, XA_HD), mem_k.astype(x.dtype), mem_v.astype(x.dtype))
    ya = a_out.reshape(B, S, DA_V_W) @ lp["w_proj_a"]
    yb = b_out @ lp["w_proj_b"]
    yc = c_out.reshape(B, S, XA_W) @ lp["w_proj_c"]
    gates = jax.nn.sigmoid((h @ lp["w_gate"] + lp["b_gate"]).astype(jnp.float32)).astype(x.dtype)
    gates = gates.reshape(B, S, N_BRANCH, D_MODEL)
    merged = gates[:, :, 0] * ya + gates[:, :, 1] * yb + gates[:, :, 2] * yc
    x = x + merged @ lp["w_out"]
    h2 = rms_norm(x, lp["norm_ffn"])
    g_ff, u_ff = jnp.split(h2 @ lp["w_ffn_in"], 2, axis=-1)
    x = x + (jax.nn.silu(g_ff) * u_ff) @ lp["w_ffn_out"]
    return x, k, v, new_conv, h_last


def setup_inputs(seed: int = 0) -> dict:
    key = jax.random.key(seed)
    ks = jax.random.split(key, 40)
    f32 = jnp.float32
    nrm = lambda i, shape, s: jax.random.normal(ks[i], shape, f32) * s
    gain = lambda i, shape: 1.0 + 0.05 * jax.random.normal(ks[i], shape, f32)
    u = jax.random.uniform(ks[39], (DEPTH, LRU_W), f32, 0.9, 0.999)
    s = u ** (1.0 / LRU_C)
    return {
        "x_prompt": nrm(0, (BATCH, SEQ, D_MODEL), 1.0),
        "x_sample": nrm(1, (DEC_BATCH, DEC_SEQ, D_MODEL), 1.0),
        "mem_prompt": nrm(2, (BATCH, N_MEM, D_MODEL), 1.0),
        "cache_k": nrm(3, (DEPTH, DEC_BATCH, PAST_LEN, DA_HEADS, 2, DA_HD), 1.0),
        "cache_v": nrm(4, (DEPTH, DEC_BATCH, PAST_LEN, DA_HEADS, DA_V), 1.0),
        "state_conv": nrm(5, (DEPTH, DEC_BATCH, CONV_W - 1, LRU_W), 1.0),
        "state_lru": nrm(6, (DEPTH, DEC_BATCH, LRU_W), 0.5),
        "cache_mem_k": nrm(7, (DEPTH, DEC_BATCH, N_MEM, XA_HEADS, XA_HD), 1.0),
        "cache_mem_v": nrm(8, (DEPTH, DEC_BATCH, N_MEM, XA_HEADS, XA_HD), 1.0),
        "rel_table": nrm(9, (N_BUCKETS, DA_HEADS), 0.5),
        "norm_mix": gain(10, (DEPTH, D_MODEL)),
        "w_in": nrm(11, (DEPTH, D_MODEL, IN_W), D_MODEL ** -0.5),
        "lambda_q1": nrm(12, (DEPTH, DA_HD), 0.1),
        "lambda_k1": nrm(13, (DEPTH, DA_HD), 0.1),
        "lambda_q2": nrm(14, (DEPTH, DA_HD), 0.1),
        "lambda_k2": nrm(15, (DEPTH, DA_HD), 0.1),
        "subln_g": gain(16, (DEPTH, DA_V)),
        "conv_w": nrm(17, (DEPTH, CONV_W, LRU_W), CONV_W ** -0.5),
        "conv_b": nrm(18, (DEPTH, LRU_W), 0.01),
        "w_rg_a": nrm(19, (DEPTH, LRU_BLOCKS, LRU_BW, LRU_BW), LRU_BW ** -0.5),
        "b_rg_a": nrm(20, (DEPTH, LRU_W), 0.01),
        "w_rg_x": nrm(21, (DEPTH, LRU_BLOCKS, LRU_BW, LRU_BW), LRU_BW ** -0.5),
        "b_rg_x": nrm(22, (DEPTH, LRU_W), 0.01),
        "rg_lambda": jnp.log(s / (1.0 - s)),
        "norm_mem": gain(23, (DEPTH, D_MODEL)),
        "w_mem_kv": nrm(24, (DEPTH, D_MODEL, 2 * XA_W), D_MODEL ** -0.5),
        "w_proj_a": nrm(25, (DEPTH, DA_V_W, D_MODEL), DA_V_W ** -0.5),
        "w_proj_b": nrm(26, (DEPTH, LRU_W, D_MODEL), LRU_W ** -0.5),
        "w_proj_c": nrm(27, (DEPTH, XA_W, D_MODEL), XA_W ** -0.5),
        "w_gate": nrm(28, (DEPTH, D_MODEL, N_BRANCH * D_MODEL), D_MODEL ** -0.5),
        "b_gate": nrm(29, (DEPTH, N_BRANCH * D_MODEL), 0.01),
        "w_out": nrm(30, (DEPTH, D_MODEL, D_MODEL), D_MODEL ** -0.5),
        "norm_ffn": gain(31, (DEPTH, D_MODEL)),
        "w_ffn_in": nrm(32, (DEPTH, D_MODEL, 2 * D_FF), D_MODEL ** -0.5),
        "w_ffn_out": nrm(33, (DEPTH, D_FF, D_MODEL), D_FF ** -0.5),
        "norm_final": gain(34, (D_MODEL,)),
    }


def reference(x_prompt, x_sample, mem_prompt, cache_k, cache_v, state_conv, state_lru, cache_mem_k, cache_mem_v,
              rel_table, norm_mix, w_in, lambda_q1, lambda_k1, lambda_q2, lambda_k2, subln_g, conv_w, conv_b,
              w_rg_a, b_rg_a, w_rg_x, b_rg_x, rg_lambda, norm_mem, w_mem_kv, w_proj_a, w_proj_b, w_proj_c,
              w_gate, b_gate, w_out, norm_ffn, w_ffn_in, w_ffn_out, norm_final):
    past_len = cache_k.shape[2]
    pos_p = jnp.arange(x_prompt.shape[1], dtype=jnp.int32)
    past_pos = jnp.arange(past_len, dtype=jnp.int32)
    pos_s = past_len + jnp.arange(x_sample.shape[1], dtype=jnp.int32)
    xp, xs = x_prompt, x_sample
    kp_l, vp_l, cp_l, hp_l, mkp_l, mvp_l = [], [], [], [], [], []
    ks_l, vs_l, cs_l, hs_l = [], [], [], []
    for l in range(DEPTH):
        lambda_init = 0.8 - 0.6 * math.exp(-0.3 * l)
        lp = dict(norm_mix=norm_mix[l], w_in=w_in[l], lambda_q1=lambda_q1[l], lambda_k1=lambda_k1[l],
                  lambda_q2=lambda_q2[l], lambda_k2=lambda_k2[l], subln_g=subln_g[l], conv_w=conv_w[l],
                  conv_b=conv_b[l], w_rg_a=w_rg_a[l], b_rg_a=b_rg_a[l], w_rg_x=w_rg_x[l], b_rg_x=b_rg_x[l],
                  rg_lambda=rg_lambda[l], w_proj_a=w_proj_a[l], w_proj_b=w_proj_b[l], w_proj_c=w_proj_c[l],
                  w_gate=w_gate[l], b_gate=b_gate[l], w_out=w_out[l], norm_ffn=norm_ffn[l],
                  w_ffn_in=w_ffn_in[l], w_ffn_out=w_ffn_out[l])
        mk, mv = memory_kv(mem_prompt, norm_mem[l], w_mem_kv[l])
        xp, kp, vp, cp, hp = layer_forward(xp, pos_p, None, None, None, None, None, mk, mv, rel_table, lp, lambda_init)
        kp_l.append(kp); vp_l.append(vp); cp_l.append(cp); hp_l.append(hp); mkp_l.append(mk); mvp_l.append(mv)
        xs, ks_, vs_, cs_, hs_ = layer_forward(xs, pos_s, cache_k[l], cache_v[l], past_pos, state_conv[l],
                                               state_lru[l], cache_mem_k[l], cache_mem_v[l], rel_table, lp, lambda_init)
        ks_l.append(ks_); vs_l.append(vs_); cs_l.append(cs_); hs_l.append(hs_)
    y_prompt = rms_norm(xp, norm_final)
    y_sample = rms_norm(xs, norm_final)
    return (y_prompt, y_sample,
            jnp.stack(kp_l), jnp.stack(vp_l), jnp.stack(cp_l), jnp.stack(hp_l), jnp.stack(mkp_l), jnp.stack(mvp_l),
            jnp.stack(ks_l), jnp.stack(vs_l), jnp.stack(cs_l), jnp.stack(hs_l))
```

```python
import os
from contextlib import ExitStack
import numpy as np
import concourse.bass as bass
import concourse.mybir as mybir
from concourse.bass_utils import run_bass_kernel_spmd

F32 = mybir.dt.float32
BF16 = mybir.dt.bfloat16
U8 = mybir.dt.uint8
AF = mybir.ActivationFunctionType
ALU = mybir.AluOpType

D = 1024
S = 2048
T = 2112
NCORES = 8
DFF = 2816
EPS = 1e-6
LAMBDA_INIT = 0.2
NEG = -30000.0


class Buf:
    __slots__ = ("name", "w", "r", "dsem", "dval")

    def __init__(self, name):
        self.name = name
        self.w = []
        self.r = []
        self.dsem = None
        self.dval = 0


class Sched:
    ENG = ("pe", "act", "dve", "pool", "sp")

    def __init__(self, nc, stack):
        self.nc = nc
        self.stack = stack
        self.e = {"pe": nc.tensor, "act": nc.scalar, "dve": nc.vector, "pool": nc.gpsimd, "sp": nc.sync}
        self.sem = {k: stack.enter_context(nc.semaphore("s_" + k)) for k in self.ENG}
        self.cnt = {k: 0 for k in self.ENG}
        self.pending = {k: False for k in self.ENG}
        self.seen = {k: {} for k in self.ENG}
        self.dma_sems = []
        self.all_dma = []

    def _wait(self, eng, tok):
        kind = tok[0]
        if kind == "e":
            _, src, idx = tok
            if src == eng and src == "pe":
                return
            key = src
            if self.seen[eng].get(key, 0) >= idx:
                return
            self.e[eng].wait_ge(self.sem[src], idx)
            self.seen[eng][key] = idx
        else:
            _, buf, val = tok
            key = ("d", id(buf))
            if self.seen[eng].get(key, 0) >= val:
                return
            self.e[eng].wait_ge(buf.dsem, val)
            self.seen[eng][key] = val

    def _deps(self, eng, r, w):
        toks = []
        for b in r:
            toks += b.w
        for b in w:
            toks += b.w + b.r
        for t in toks:
            self._wait(eng, t)

    def op(self, eng, fn, r=(), w=(), signal=True):
        self._deps(eng, r, w)
        inst = fn()
        if signal:
            self.cnt[eng] += 1
            inst.then_inc(self.sem[eng], 1)
            tok = ("e", eng, self.cnt[eng])
            self.pending[eng] = False
        else:
            tok = ("e", eng, self.cnt[eng] + 1)
            self.pending[eng] = True
        for b in r:
            b.r.append(tok)
        for b in w:
            b.w = [tok]
            b.r = []
        return inst

    def dma(self, q, pairs, r=(), w=(), **kw):
        self._deps(q, r, w)
        owner = w[0] if w else r[0]
        if owner.dsem is None:
            owner.dsem = self.stack.enter_context(self.nc.semaphore("d_" + owner.name))
        for (o, i) in pairs:
            self.e[q].dma_start(out=o, in_=i, **kw).then_inc(owner.dsem, 16)
            owner.dval += 16
        tok = ("d", owner, owner.dval)
        for b in r:
            b.r.append(tok)
        for b in w:
            b.w = [tok]
            b.r = []
        self.all_dma.append(tok)

    def alias(self, new, olds):
        for o in olds:
            new.r += o.w + o.r

    def finish(self):
        for tok in self.all_dma:
            self._wait("sp", tok)


def _rel_bucket_np(rel):
    n = np.abs(rel)
    nf = np.maximum(n, 1).astype(np.float32)
    large = 8 + (np.log(nf / np.float32(8)) / np.float32(np.log(16.0)) * np.float32(8)).astype(np.int32)
    large = np.minimum(large, 15)
    return np.where(rel > 0, 16, 0) + np.where(n < 8, n, large)


def _bias_onehot():
    j = np.arange(384)
    rel = j - 255
    b = _rel_bucket_np(rel)
    oh = np.zeros((32, 384), np.float32)
    oh[b, j] = 1.0
    oh[15, :] -= 1.0
    oh[:, 383] = 0.0
    return oh


def build_program(phase_limit=99):
    nc = bass.Bass("TRN2", target_bir_lowering=False)

    def din(name, shape):
        return nc.dram_tensor(name, list(shape), F32, kind="ExternalInput")

    def dout(name, shape):
        return nc.dram_tensor(name, list(shape), F32, kind="ExternalOutput")

    xp = din("xp", [S, D]).ap()
    xs = din("xs", [64, D]).ap()
    mem = din("mem", [256, D]).ap()
    ck = din("ck", [2, S, D]).ap()
    cv = din("cv", [2, S, D]).ap()
    sconv = din("sconv", [2, 3, D]).ap()
    slru = din("slru", [2, D]).ap()
    cmk = din("cmk", [2, 256, D]).ap()
    cmv = din("cmv", [2, 256, D]).ap()
    rel_table = din("rel_table", [32, 8]).ap()
    boh = din("boh", [32, 384]).ap()
    ident_in = din("ident", [128, 128]).ap()
    aident_in = din("aident", [128, 128]).ap()
    norm_mix = din("norm_mix", [1, D]).ap()
    w_in = din("w_in", [D, 6144]).ap()
    lq1 = din("lq1", [1, 64]).ap()
    lk1 = din("lk1", [1, 64]).ap()
    lq2 = din("lq2", [1, 64]).ap()
    lk2 = din("lk2", [1, 64]).ap()
    subln_g = din("subln_g", [1, 128]).ap()
    conv_w = din("conv_w", [4, D]).ap()
    conv_b = din("conv_b", [1, D]).ap()
    w_rg_a = din("w_rg_a", [8, 128, 128]).ap()
    b_rg_a = din("b_rg_a", [1, D]).ap()
    w_rg_x = din("w_rg_x", [8, 128, 128]).ap()
    b_rg_x = din("b_rg_x", [1, D]).ap()
    rg_lambda = din("rg_lambda", [1, D]).ap()
    norm_mem = din("norm_mem", [1, D]).ap()
    w_mem_kv = din("w_mem_kv", [D, 2048]).ap()
    w_proj_a = din("w_proj_a", [D, D]).ap()
    w_proj_b = din("w_proj_b", [D, D]).ap()
    w_proj_c = din("w_proj_c", [D, D]).ap()
    w_gate = din("w_gate", [D, 3072]).ap()
    b_gate = din("b_gate", [1, 3072]).ap()
    w_out = din("w_out", [D, D]).ap()
    norm_ffn = din("norm_ffn", [1, D]).ap()
    w_ffn_in = din("w_ffn_in", [D, 2 * DFF]).ap()
    w_ffn_out = din("w_ffn_out", [DFF, D]).ap()
    norm_final = din("norm_final", [1, D]).ap()

    yp = dout("yp", [S, D]).ap()
    ys = dout("ys", [64, D]).ap()
    nk = dout("nk", [S, D]).ap()
    nv = dout("out_v", [S, D]).ap()
    nconv = dout("nconv", [3, D]).ap()
    nlru = dout("nlru", [1, D]).ap()
    nmk = dout("nmk", [256, D]).ap()
    nmv = dout("nmv", [256, D]).ap()
    nks = dout("nks", [64, D]).ap()
    nvs = dout("out_vs", [64, D]).ap()
    nconvs = dout("nconvs", [2, 3, D]).ap()
    nlrus = dout("nlrus", [2, D]).ap()
    tsc_h = nc.dram_tensor("tsc", [8, 384], F32, kind="Internal")
    DEBUG = os.environ.get("MK_DEBUG", "")
    dbg = nc.dram_tensor("debug_out", [128, 16896], BF16, kind="ExternalOutput").ap() if DEBUG else None
    dbg2 = nc.dram_tensor("debug_x2", [2112, 1024], F32, kind="ExternalOutput").ap() if DEBUG else None

    stack = ExitStack()
    with stack:
        arena = stack.enter_context(nc.sbuf_tensor("arena", [128, 212800], U8))
        psum = [stack.enter_context(nc.psum_tensor("ps%d" % i, [128, 512], F32)) for i in range(8)]
        stack.enter_context(nc.Block())
        sc = Sched(nc, stack)
        E = sc.e

        def region(off, nbytes, dt, pattern=None, **kw):
            ap = arena[:, off:off + nbytes].bitcast(dt)
            if pattern:
                ap = ap.rearrange(pattern, **kw)
            return ap

        O_CONST = 0
        O_SLAB = 14336
        O_HT = O_SLAB + 3 * 16384
        O_R1 = O_HT + 33792
        O_A = O_R1 + 67072
        O_TMP = O_A + 33792
        TMP_SZ = 212800 - O_TMP
        assert TMP_SZ >= 14656, TMP_SZ

        psb = [Buf("psum%d" % i) for i in range(8)]

        def ps_f32(i):
            return psum[i][:]

        def ps_bf(i):
            return psum[i][:].bitcast(BF16)

        co = [O_CONST]

        def calloc(nbytes, dt, pattern=None, **kw):
            off = co[0]
            co[0] += (nbytes + 31) // 32 * 32
            assert co[0] <= O_SLAB
            return region(off, nbytes, dt, pattern, **kw)

        identb = calloc(256, BF16)
        identf = calloc(512, F32)
        bias_t = calloc(8 * 2 * 2 * 256, BF16, "p (h k s q) -> p h k s q", h=8, k=2, s=2)
        colv = calloc(64 * 4, F32)
        cB = Buf("consts")

        C_NLAM, C_GSUB, C_SP4, C_ONE = 0, 1, 2, 10
        C_BA, C_BX, C_CB, C_CW = 11, 19, 27, 35
        colv2 = calloc(64 * 4, F32)
        C2_BG = 0
        colv_b = Buf("colv")

        sc.dma("sp", [(identf, ident_in)], w=[cB])
        sc.op("dve", lambda: E["dve"].tensor_copy(out=identb, in_=identf), r=[cB], w=[cB])


        hT = region(O_HT, 33792, BF16, "p (c t) -> p c t", c=8)
        hT_b = [Buf("hT%d" % i) for i in range(18)]
        TILES = [(i * 128, 128) for i in range(16)] + [(2048, 32), (2080, 32)]

        def tile_src(tt):
            t0, rows = TILES[tt]
            if tt < 16:
                return xp[t0:t0 + rows, :]
            return xs[(t0 - 2048):(t0 - 2048) + rows, :]

        grow = region(O_A, 4096, F32)
        grow_b = Buf("grow")
        sc.dma("sp", [(grow, norm_mix.partition_broadcast(128))], w=[grow_b])

        xst = [region(O_R1 + i * 4096, 4096, F32) for i in range(3)]
        xst_b = [Buf("xst%d" % i) for i in range(3)]
        xn = [region(O_R1 + 12288 + i * 2048, 2048, BF16) for i in range(2)]
        xn_b = [Buf("xn%d" % i) for i in range(2)]
        junk = region(O_R1 + 16384, 2048, BF16)
        junk_b = Buf("junk")
        ssb = region(O_R1 + 18432, 18 * 4 * 3, F32, "p (k t) -> p k t", k=3)
        ss_b = Buf("ss")

        def rms_tile(src_ap, rows, st, st_b, ssi, g_ap, out_bf, out_b):
            sc.op("act", lambda: E["act"].activation(out=junk[:rows], in_=st[:rows], func=AF.Square,
                                                    accum_out=ssb[:rows, 0, ssi:ssi + 1]),
                  r=[st_b], w=[junk_b, ss_b])
            sc.op("act", lambda: E["act"].activation(out=ssb[:rows, 1, ssi:ssi + 1], in_=ssb[:rows, 0, ssi:ssi + 1],
                                                    func=AF.Ln, scale=1.0 / D, bias=EPS), r=[ss_b], w=[ss_b])
            sc.op("act", lambda: E["act"].activation(out=ssb[:rows, 2, ssi:ssi + 1], in_=ssb[:rows, 1, ssi:ssi + 1],
                                                    func=AF.Exp, scale=-0.5), r=[ss_b], w=[ss_b])
            sc.op("dve", lambda: E["dve"].scalar_tensor_tensor(
                out=out_bf[:rows], in0=st[:rows], scalar=ssb[:rows, 2, ssi:ssi + 1], in1=g_ap[:rows],
                op0=ALU.mult, op1=ALU.mult), r=[st_b, ss_b, grow_b], w=[out_b])

        def transpose_to_T(src_bf, src_b, rows, dstT, dst_b, t0, pbank, evac_eng):
            pv = ps_bf(pbank).rearrange("p (c t) -> p c t", c=8)
            for c in range(8):
                sc.op("pe", lambda c=c: E["pe"].transpose(out=pv[:, c, :rows], in_=src_bf[:rows, c * 128:(c + 1) * 128],
                                                          identity=identb[:rows, :rows]),
                      r=[src_b, cB], w=[psb[pbank]], signal=(c == 7))
            if evac_eng == "act":
                sc.op("act", lambda: E["act"].copy(out=dstT[:, :, t0:t0 + rows], in_=pv[:, :, :rows]),
                      r=[psb[pbank]], w=[dst_b])
            else:
                sc.op("dve", lambda: E["dve"].tensor_copy(out=dstT[:, :, t0:t0 + rows], in_=pv[:, :, :rows]),
                      r=[psb[pbank]], w=[dst_b])

        for tt in range(18):
            t0, rows = TILES[tt]
            st, st_b = xst[tt % 3], xst_b[tt % 3]
            sc.dma("sp", [(st[:rows], tile_src(tt))], w=[st_b])
            rms_tile(None, rows, st, st_b, tt, grow, xn[tt % 2], xn_b[tt % 2])
            transpose_to_T(xn[tt % 2], xn_b[tt % 2], rows, hT, hT_b[tt], t0, tt % 2, "act")

        slab = [region(O_SLAB + i * 16384, 16384, BF16, "p (c n) -> p c n", c=8) for i in range(2)]
        slab_b = [Buf("slab%d" % i) for i in range(2)]
        slab_i = [0]
        O_TMP2 = O_SLAB + 2 * 16384

        def load_slab(w_ap, c0, ncols, kc0=0, nkc=8):
            i = slab_i[0] % 2
            slab_i[0] += 1
            src = w_ap[kc0 * 128:(kc0 + nkc) * 128, c0:c0 + ncols].rearrange("(c p) n -> p c n", p=128)
            pairs = []
            step = 2
            for k0 in range(0, nkc, step):
                k1 = min(nkc, k0 + step)
                pairs.append((slab[i][:, k0:k1, 0:ncols], src[:, k0:k1, :]))
            sc.dma("pool", pairs, w=[slab_b[i]])
            return slab[i], slab_b[i]

        stg = [region(O_A + 4096 + i * 4096, 4096, F32) for i in range(4)]
        stg_b = [Buf("stg%d" % i) for i in range(4)]
        stg_i = [0]

        def kv_phase(srcT, srcT_b, tiles, wk_sl, wv_sl, krows, vrows, kT_dst, kT_dst_b, vdst, vdst_b, vh, ve):
            for which, (wsl, wsl_b) in enumerate((wk_sl, wv_sl)):
                for tt, (t0, rows) in enumerate(tiles):
                    sg, sg_b = stg[stg_i[0] % 4], stg_b[stg_i[0] % 4]
                    stg_i[0] += 1
                    for half in range(2):
                        pb = (2 * tt + half) % 4
                        for kc in range(8):
                            sc.op("pe", lambda kc=kc, half=half, pb=pb: E["pe"].matmul(
                                ps_f32(pb)[:rows, :], lhsT=srcT[:, kc, t0:t0 + rows], rhs=wsl[:, kc, half * 512:(half + 1) * 512],
                                start=(kc == 0), stop=(kc == 7)),
                                r=[srcT_b[tt], wsl_b], w=[psb[pb]], signal=(kc == 7))
                        sc.op("act", lambda half=half, pb=pb: E["act"].copy(
                            out=sg[:rows, half * 512:(half + 1) * 512], in_=ps_f32(pb)[:rows, :]),
                            r=[psb[pb]], w=[sg_b])
                        if which == 1:
                            for hv in range(vh):
                                sc.op("dve", lambda half=half, hv=hv: E["dve"].tensor_copy(
                                    out=vdst(tt)[:rows, half * vh + hv, 0:ve],
                                    in_=sg[:rows, half * 512 + hv * ve:half * 512 + (hv + 1) * ve]),
                                    r=[sg_b], w=[vdst_b[tt]])
                    if which == 0:
                        sc.dma("sp", [(krows(tt), sg[:rows])], r=[sg_b])
                        for half in range(2):
                            pb = 4 + (2 * tt + half) % 4
                            pv = ps_f32(pb).rearrange("p (c t) -> p c t", c=4)
                            for c in range(4):
                                hh = half * 4 + c
                                sc.op("pe", lambda c=c, hh=hh, pv=pv: E["pe"].transpose(
                                    out=pv[:, c, :rows], in_=sg[:rows, hh * 128:(hh + 1) * 128], identity=identf[:rows, :rows]),
                                    r=[sg_b, cB], w=[psb[pb]], signal=(c == 3))
                            sc.op("dve", lambda half=half, pv=pv: E["dve"].tensor_copy(
                                out=kT_dst[:, half * 4:(half + 1) * 4, t0:t0 + rows], in_=pv[:, :, :rows]),
                                r=[psb[pb]], w=[kT_dst_b[tt]])
                    else:
                        sc.dma("sp", [(vrows(tt), sg[:rows])], r=[sg_b])

        kT = region(O_R1, 33792, BF16, "p (c t) -> p c t", c=8)
        kT_b = [Buf("kT%d" % i) for i in range(18)]
        v_aug = region(O_R1 + 33792, 16 * 8 * 130 * 2, BF16, "p (t h e) -> p t h e", t=16, h=8)
        sv_aug = calloc(2 * 8 * 130 * 2, BF16, "p (t h e) -> p t h e", t=2, h=8)
        v_b = [Buf("v%d" % i) for i in range(18)]
        for b_ in kT_b + v_b:
            sc.alias(b_, xst_b + xn_b + [junk_b, ss_b])
        sc.op("pool", lambda: E["pool"].memset(v_aug[:, :, :, 128:129], 1.0), w=v_b[:16])
        sc.op("pool", lambda: E["pool"].memset(sv_aug[:, :, :, 128:129], 1.0), w=v_b[16:])

        def out_rows(dst_p, dst_s, tt):
            t0, rows = TILES[tt]
            if tt < 16:
                return dst_p[t0:t0 + rows, :]
            return dst_s[t0 - 2048:t0 - 2048 + rows, :]

        wk_sl = load_slab(w_in, 1024, 1024)
        wv_sl = load_slab(w_in, 2048, 1024)
        kv_phase(hT, hT_b, TILES, wk_sl, wv_sl,
                 lambda tt: out_rows(nk, nks, tt), lambda tt: out_rows(nv, nvs, tt),
                 kT, kT_b, lambda tt: (v_aug[:, tt] if tt < 16 else sv_aug[:, tt - 16]), v_b, 4, 128)

        if phase_limit == 210:
            sc.finish()
            return nc
        scr = region(O_TMP2, 16384, F32)
        scr_b = Buf("scr")
        lam4 = scr[:, 0:256].rearrange("p (k d) -> p k d", k=4)
        sc.dma("sp", [(lam4[:, 0, :], lq1.partition_broadcast(128)), (lam4[:, 1, :], lk1.partition_broadcast(128)),
                      (lam4[:, 2, :], lq2.partition_broadcast(128)), (lam4[:, 3, :], lk2.partition_broadcast(128))],
               w=[scr_b])
        lt = scr[:, 256:512]
        sc.op("dve", lambda: E["dve"].tensor_tensor(out=lt[:, 0:64], in0=lam4[:, 0, :], in1=lam4[:, 1, :], op=ALU.mult), r=[scr_b], w=[scr_b])
        sc.op("dve", lambda: E["dve"].tensor_tensor(out=lt[:, 64:128], in0=lam4[:, 2, :], in1=lam4[:, 3, :], op=ALU.mult), r=[scr_b], w=[scr_b])
        sc.op("dve", lambda: E["dve"].reduce_sum(out=lt[:, 128:130], in_=lt[:, 0:128].rearrange("p (k d) -> p k d", k=2),
                                                 axis=mybir.AxisListType.X), r=[scr_b], w=[scr_b])
        sc.op("act", lambda: E["act"].activation(out=lt[:, 130:132], in_=lt[:, 128:130], func=AF.Exp), r=[scr_b], w=[scr_b])
        sc.op("dve", lambda: E["dve"].scalar_tensor_tensor(out=colv[:, C_NLAM:C_NLAM + 1], in0=lt[:, 131:132], scalar=-LAMBDA_INIT,
                                                           in1=lt[:, 130:131], op0=ALU.add, op1=ALU.subtract), r=[scr_b], w=[colv_b])
        with nc.allow_non_contiguous_dma(reason="tiny per-channel columns"):
            sc.dma("sp", [(lt[:, 132:133], subln_g.rearrange("o e -> e o"))], w=[scr_b])
        sc.op("dve", lambda: E["dve"].tensor_scalar(colv[:, C_GSUB:C_GSUB + 1], lt[:, 132:133], 1.0 - LAMBDA_INIT, None, op0=ALU.mult),
              r=[scr_b], w=[colv_b])

        if phase_limit != 200:
            rt = scr[0:32, 512:520]
            bo = scr[0:32, 1024:1408]
            sc.dma("sp", [(rt, rel_table), (bo, boh)], w=[scr_b])
            sc.op("pe", lambda: E["pe"].matmul(ps_f32(7)[0:8, 0:384], lhsT=rt, rhs=bo, start=True, stop=True), r=[scr_b], w=[psb[7]])
            tms = scr[0:8, 1536:1920]
            sc.op("dve", lambda: E["dve"].tensor_copy(out=tms, in_=ps_f32(7)[0:8, 0:384]), r=[psb[7]], w=[scr_b])
            tsc_b = Buf("tsc")
            sc.dma("sp", [(tsc_h.ap(), tms)], r=[scr_b], w=[tsc_b])
            if phase_limit == 201:
                sc.finish()
                return nc
            btf = scr[:, 2048:4096].rearrange("p (h k q) -> p h k q", h=8, k=2)
            ghk = region(O_A + 20480, 8192, F32, "p (h k q) -> p h k q", h=8, k=2)
            aid = region(O_A + 28672, 512, F32)
            ghk_b = Buf("ghk")
            pairs = [(aid, aident_in)]
            for h in range(8):
                for k_, base in ((0, 128), (1, 0)):
                    pairs.append((ghk[:, h, k_, :], bass.AP(tsc_h, h * 384 + base, [[1, 128], [1, 128]])))
            sc.dma("sp", pairs, r=[tsc_b], w=[ghk_b])
            for h in range(8):
                for k_ in range(2):
                    sc.op("pe", lambda h=h, k_=k_: E["pe"].matmul(ps_f32(7)[:, k_ * 128:(k_ + 1) * 128], lhsT=ghk[:, h, k_, :], rhs=aid,
                                                               start=True, stop=True), r=[ghk_b], w=[psb[7]])
                sc.op("dve", lambda h=h: E["dve"].tensor_copy(out=btf[:, h, :, :], in_=ps_f32(7)[:, 0:256].rearrange("p (k q) -> p k q", k=2)),
                      r=[psb[7]], w=[scr_b])
            sc.op("pool", lambda: E["pool"].memset(btf[64:128, :, 0, 0:64], NEG), r=[scr_b], w=[scr_b])
            sc.op("dve", lambda: E["dve"].tensor_copy(out=bias_t[:, :, :, 0, :], in_=btf), r=[scr_b], w=[cB])
            sc.op("dve", lambda: E["dve"].tensor_tensor(out=btf, in0=btf, in1=bias_t[:, :, :, 0, :], op=ALU.subtract), r=[scr_b, cB], w=[scr_b])
            sc.op("dve", lambda: E["dve"].tensor_copy(out=bias_t[:, :, :, 1, :], in_=btf), r=[scr_b], w=[cB])

        if phase_limit >= 3:
            a_outT = region(O_A, 33792, BF16, "p (c t) -> p c t", c=8)
            aT_b = [Buf("aT%d" % i) for i in range(8)]
            for b_ in aT_b:
                sc.alias(b_, stg_b + [grow_b])
            qT = [region(O_TMP + i * 4224, 4224, BF16) for i in range(2)]
            qT_b = [Buf("qT%d" % i) for i in range(2)]
            PT = [region(O_TMP + 8448 + i * 1024, 1024, BF16) for i in range(4)]
            PT_b = [Buf("PT%d" % i) for i in range(4)]
            qTs = region(O_TMP + 12544, 1024, BF16, "p (h t) -> p h t", h=8)
            qTs_b = Buf("qTs")
            accS = region(O_TMP2, 2 * 4 * 129 * 4, F32, "p (c j e) -> p c j e", c=2, j=4)
            accS2 = region(O_TMP2, 2 * 4 * 129 * 4, F32, "p (c x) -> p c x", c=2)
            t0s = region(O_TMP2 + 4160, 2048, F32, "p (j e) -> p j e", j=4)
            t1s = region(O_TMP2 + 6208, 2048, F32, "p (j e) -> p j e", j=4)
            tns = region(O_TMP2 + 8256, 1024, BF16, "p (j e) -> p j e", j=4)
            rec = region(O_TMP2 + 9280, 32, F32, "p (c j) -> p c j", c=2)
            ssq = region(O_TMP2 + 9312, 48, F32, "p (k j) -> p k j", k=3)
            junk2 = region(O_TMP2 + 9376, 256, BF16)
            ep_b = Buf("ep")
            sc.alias(ep_b, [scr_b])
            BQ = [(0, 512), (512, 512), (1024, 512), (1536, 512), (2048, 64)]
            wq_sl, wq_b = load_slab(w_in, 0, 1024)

            def q_proj(h):
                slot = h % 2
                for bi, (q0, n) in enumerate(BQ):
                    for kc in range(8):
                        sc.op("pe", lambda kc=kc: E["pe"].matmul(
                            ps_f32(7)[:, :n], lhsT=wq_sl[:, kc, h * 128:(h + 1) * 128], rhs=hT[:, kc, q0:q0 + n],
                            start=(kc == 0), stop=(kc == 7)), r=[wq_b] + hT_b, w=[psb[7]], signal=(kc == 7))
                    if bi < 4:
                        sc.op("act", lambda: E["act"].activation(out=qT[slot][:, q0:q0 + n], in_=ps_f32(7)[:, :n],
                                                                func=AF.Copy, scale=0.125), r=[psb[7]], w=[qT_b[slot]])
                    else:
                        sc.op("act", lambda: E["act"].activation(out=qTs[:, h, :], in_=ps_f32(7)[:, :n],
                                                                func=AF.Copy, scale=0.125), r=[psb[7]], w=[qTs_b])

            def epilogue(h, acc_list, nr, nj, dst_cols, defer=None):
                for (ap_, bb, c, j0, n) in acc_list:
                    sc.op("dve", lambda ap_=ap_, c=c, j0=j0, n=n: E["dve"].tensor_copy(
                        out=accS2[:nr, c, j0 * 129:(j0 + n) * 129], in_=ap_), r=[bb], w=[ep_b])
                sc.op("dve", lambda: E["dve"].reciprocal(out=rec[:nr, :, :nj], in_=accS[:nr, :, :nj, 128]), r=[ep_b], w=[ep_b])
                sc.op("dve", lambda: E["dve"].tensor_scalar(rec[:nr, 1, :nj], rec[:nr, 1, :nj], colv[:nr, C_NLAM:C_NLAM + 1], None,
                                                            op0=ALU.mult), r=[ep_b, colv_b], w=[ep_b])
                sc.op("dve", lambda: E["dve"].tensor_tensor(out=t0s[:nr, :nj, :], in0=accS[:nr, 0, :nj, 0:128],
                                                            in1=rec[:nr, 0, :nj].unsqueeze(2).to_broadcast([nr, nj, 128]), op=ALU.mult),
                      r=[ep_b], w=[ep_b])
                sc.op("dve", lambda: E["dve"].tensor_tensor(out=t1s[:nr, :nj, :], in0=accS[:nr, 1, :nj, 0:128],
                                                            in1=rec[:nr, 1, :nj].unsqueeze(2).to_broadcast([nr, nj, 128]), op=ALU.mult),
                      r=[ep_b], w=[ep_b])
                sc.op("dve", lambda: E["dve"].tensor_tensor(out=t0s[:nr, :nj, :], in0=t0s[:nr, :nj, :], in1=t1s[:nr, :nj, :], op=ALU.add),
                      r=[ep_b], w=[ep_b])
                for j in range(nj):
                    sc.op("act", lambda j=j: E["act"].activation(out=junk2[:nr, :], in_=t0s[:nr, j, :], func=AF.Square,
                                                                accum_out=ssq[:nr, 0, j:j + 1]), r=[ep_b], w=[ep_b])
                sc.op("act", lambda: E["act"].activation(out=ssq[:nr, 1, :nj], in_=ssq[:nr, 0, :nj], func=AF.Ln, scale=1.0 / 128, bias=EPS),
                      r=[ep_b], w=[ep_b])
                sc.op("act", lambda: E["act"].activation(out=ssq[:nr, 2, :nj], in_=ssq[:nr, 1, :nj], func=AF.Exp, scale=-0.5),
                      r=[ep_b], w=[ep_b])
                sc.op("dve", lambda: E["dve"].tensor_tensor(out=tns[:nr, :nj, :], in0=t0s[:nr, :nj, :],
                                                            in1=ssq[:nr, 2, :nj].unsqueeze(2).to_broadcast([nr, nj, 128]), op=ALU.mult),
                      r=[ep_b], w=[ep_b])
                def part2():
                    pv = ps_bf(7).rearrange("p (j t) -> p j t", j=8)
                    for j in range(nj):
                        sc.op("pe", lambda j=j: E["pe"].transpose(out=pv[:, j, :nr], in_=tns[:nr, j, :], identity=identb[:nr, :nr]),
                              r=[ep_b, cB], w=[psb[7]], signal=(j == nj - 1))
                    for j in range(nj):
                        c0 = dst_cols(j)
                        sc.op("dve", lambda j=j, c0=c0: E["dve"].tensor_scalar(a_outT[:, h, c0:c0 + nr], pv[:, j, :nr],
                                                                              colv[:, C_GSUB:C_GSUB + 1], None, op0=ALU.mult),
                              r=[psb[7], colv_b], w=[aT_b[h]])
                if defer is None:
                    part2()
                else:
                    defer.append(part2)

            def attn_prompt(h):
                slot = h % 2
                steps = []
                for qb in range(4):
                    last_kt = 4 * qb + 3
                    for kt in range(0, last_kt + 1):
                        j0 = max(0, kt - 4 * qb)
                        for c in range(2):
                            near = []
                            for j in range(j0, 4):
                                d = (4 * qb + j) - kt
                                if d == 0:
                                    near.append((j, 0))
                                elif d == 1:
                                    near.append((j, 1))
                            steps.append(dict(qb=qb, kt=kt, c=c, j0=j0, ncols=512 - 128 * j0, qs=qb * 512 + 128 * j0, near=near,
                                              sb=len(steps) % 4, last=(kt == last_kt and c == 1)))
                started = {}

                def emit_S(st):
                    sb, c, kt, j0, ncols, qs, near = st["sb"], st["c"], st["kt"], st["j0"], st["ncols"], st["qs"], st["near"]
                    sc.op("pe", lambda: E["pe"].matmul(
                        ps_f32(sb)[:, :ncols], lhsT=kT[c * 64:(c + 1) * 64, h, kt * 128:(kt + 1) * 128],
                        rhs=qT[slot][c * 64:(c + 1) * 64, qs:qs + ncols], start=True, stop=True),
                        r=[kT_b[kt], qT_b[slot]], w=[psb[sb]], signal=(len(near) == 0))
                    for ni, (j, kind) in enumerate(near):
                        for hl in range(2):
                            lastb = (ni == len(near) - 1 and hl == 1)
                            sc.op("pe", lambda j=j, kind=kind, hl=hl, lastb=lastb: E["pe"].matmul(
                                ps_f32(sb)[:, (j - j0) * 128:(j - j0 + 1) * 128], lhsT=identb, rhs=bias_t[:, h, kind, hl, :],
                                start=False, stop=True, skip_group_check=True), r=[cB], w=[psb[sb]], signal=lastb)
                    sc.op("act", lambda: E["act"].activation(out=PT[sb][:, :ncols], in_=ps_f32(sb)[:, :ncols], func=AF.Exp),
                          r=[psb[sb]], w=[PT_b[sb]])

                def emit_PV(st):
                    sb, c, kt, j0, qb = st["sb"], st["c"], st["kt"], st["j0"], st["qb"]
                    stt = started.setdefault(qb, set())
                    for j in range(j0, 4):
                        if j < 3:
                            bank, col = 4 + c, j * 129
                        else:
                            bank, col = 6, c * 129
                        first = bank not in stt
                        stt.add(bank)
                        fin = (kt == 4 * qb + j)
                        sc.op("pe", lambda j=j, bank=bank, col=col, first=first, fin=fin: E["pe"].matmul(
                            ps_f32(bank)[:, col:col + 129], lhsT=PT[sb][:, (j - j0) * 128:(j - j0 + 1) * 128],
                            rhs=v_aug[:, kt, h, 0:129], start=first, stop=fin, skip_group_check=True),
                            r=[PT_b[sb], v_b[kt]], w=[psb[bank]], signal=(j == 3))
                    if st["last"]:
                        acc_list = [(ps_f32(4)[:, 0:387], psb[4], 0, 0, 3), (ps_f32(5)[:, 0:387], psb[5], 1, 0, 3),
                                    (ps_f32(6)[:, 0:129], psb[6], 0, 3, 1), (ps_f32(6)[:, 129:258], psb[6], 1, 3, 1)]
                        epilogue(h, acc_list, 128, 4, lambda j, qb=qb: qb * 512 + j * 128, defer=pending)
                        pend_at[0] = cur_i[0] + 5

                LA = 3
                pending = []
                pend_at = [None]
                cur_i = [0]
                for i in range(len(steps) + LA):
                    cur_i[0] = i
                    if i < len(steps):
                        emit_S(steps[i])
                    if pending and pend_at[0] is not None and i >= pend_at[0]:
                        pending.pop(0)()
                        pend_at[0] = None
                    if i - LA >= 0:
                        emit_PV(steps[i - LA])
                while pending:
                    pending.pop(0)()

            q_proj(0)
            for h in range(8):
                if h + 1 < 8:
                    q_proj(h + 1)
                attn_prompt(h)

            KcT = region(O_R1, 32768, BF16, "p (h t) -> p h t", h=8)
            KcT_b = Buf("KcT")
            Vc = region(O_R1 + 33792, 16 * 8 * 130 * 2, BF16, "p (t h e) -> p t h e", t=16, h=8)
            Vc_b = Buf("Vc")
            kTs = region(O_TMP + 13568, 1024, BF16, "p (h t) -> p h t", h=8)
            kTs_b = Buf("kTs")
            sc.op("dve", lambda: E["dve"].tensor_copy(out=kTs, in_=kT[:, :, 2048:2112]), r=kT_b[16:18], w=[kTs_b])
            sc.alias(KcT_b, kT_b)
            sc.alias(Vc_b, v_b[:16])
            kst = [region(O_TMP + i * 2048, 2048, BF16) for i in range(2)]
            kst_b = [Buf("kst%d" % i) for i in range(2)]
            for b_ in kst_b:
                sc.alias(b_, qT_b)
            for s_ in range(2):
                for kt in range(16):
                    sc.dma("pool", [(Vc[:, kt, :, 0:128], cv[s_, kt * 128:(kt + 1) * 128, :].rearrange("p (h e) -> p h e", h=8))],
                           w=[Vc_b])
                for kt in range(16):
                    ks, ks_b = kst[kt % 2], kst_b[kt % 2]
                    sc.dma("pool", [(ks, ck[s_, kt * 128:(kt + 1) * 128, :])], w=[ks_b])
                    pb = kt % 2
                    pv = ps_bf(pb).rearrange("p (c t) -> p c t", c=8)
                    for c in range(8):
                        sc.op("pe", lambda c=c, pv=pv: E["pe"].transpose(out=pv[:, c, :], in_=ks[:, c * 128:(c + 1) * 128], identity=identb),
                              r=[ks_b, cB], w=[psb[pb]], signal=(c == 7))
                    sc.op("dve", lambda pv=pv: E["dve"].tensor_copy(out=KcT[:, :, kt * 128:(kt + 1) * 128], in_=pv),
                          r=[psb[pb]], w=[KcT_b])
                tnew = 16 + s_
                c0n = 2048 + 32 * s_
                def s_stage(h, c):
                    sb = 2 + c
                    qsl = qTs[c * 64:(c + 1) * 64, h, 32 * s_:32 * s_ + 32]
                    for kt in range(16):
                        sc.op("pe", lambda kt=kt: E["pe"].matmul(
                            ps_f32(sb)[:, kt * 32:(kt + 1) * 32], lhsT=KcT[c * 64:(c + 1) * 64, h, kt * 128:(kt + 1) * 128], rhs=qsl,
                            start=(kt == 0), stop=True, skip_group_check=True), r=[KcT_b, qTs_b], w=[psb[sb]], signal=False)
                    for hl in range(2):
                        sc.op("pe", lambda hl=hl: E["pe"].matmul(
                            ps_f32(sb)[:, 480:512], lhsT=identb, rhs=bias_t[:, h, 1, hl, 0:32], start=False, stop=True,
                            skip_group_check=True), r=[cB], w=[psb[sb]], signal=(hl == 1))
                    nb = 6
                    ncol = ((h % 2) * 2 + c) * 32
                    sc.op("pe", lambda: E["pe"].matmul(
                        ps_f32(nb)[0:32, ncol:ncol + 32], lhsT=kTs[c * 64:(c + 1) * 64, h, 32 * s_:32 * s_ + 32], rhs=qsl,
                        start=True, stop=True, skip_group_check=True), r=[kTs_b, qTs_b], w=[psb[nb]], signal=False)
                    for hl in range(2):
                        sc.op("pe", lambda hl=hl: E["pe"].matmul(
                            ps_f32(nb)[0:32, ncol:ncol + 32], lhsT=identb[0:32, 0:32], rhs=bias_t[0:32, h, 0, hl, 0:32],
                            start=False, stop=True, skip_group_check=True), r=[cB], w=[psb[nb]], signal=(hl == 1))
                    sc.op("act", lambda: E["act"].activation(out=PT[sb][:, :512], in_=ps_f32(sb)[:, :512], func=AF.Exp),
                          r=[psb[sb]], w=[PT_b[sb]])
                    sc.op("act", lambda: E["act"].activation(out=PT[c][0:32, 0:32], in_=ps_f32(nb)[0:32, ncol:ncol + 32], func=AF.Exp),
                          r=[psb[nb]], w=[PT_b[c]])

                def pv_stage(h, c):
                    sb = 2 + c
                    for kt in range(16):
                        sc.op("pe", lambda kt=kt: E["pe"].matmul(
                            ps_f32(4 + c)[0:32, 0:129], lhsT=PT[sb][:, kt * 32:(kt + 1) * 32], rhs=Vc[:, kt, h, 0:129],
                            start=(kt == 0), stop=False), r=[PT_b[sb], Vc_b], w=[psb[4 + c]], signal=False)
                    sc.op("pe", lambda: E["pe"].matmul(
                        ps_f32(4 + c)[0:32, 0:129], lhsT=PT[c][0:32, 0:32], rhs=sv_aug[0:32, s_, h, 0:129],
                        start=False, stop=True), r=[PT_b[c], v_b[tnew]], w=[psb[4 + c]], signal=True)
                    if c == 1:
                        acc_list = [(ps_f32(4)[0:32, 0:129], psb[4], 0, 0, 1), (ps_f32(5)[0:32, 0:129], psb[5], 1, 0, 1)]
                        epilogue(h, acc_list, 32, 1, lambda j, c0n=c0n: c0n)

                hc = [(h, c) for h in range(8) for c in range(2)]
                s_stage(*hc[0])
                for i in range(len(hc)):
                    if i + 1 < len(hc):
                        s_stage(*hc[i + 1])
                    pv_stage(*hc[i])

        def load_w(dst, dst_b, w_ap, c0, ncols, kc0=0, nkc=8):
            src = w_ap[kc0 * 128:(kc0 + nkc) * 128, c0:c0 + ncols].rearrange("(c p) n -> p c n", p=128)
            pairs = []
            for k0 in range(0, nkc, 2):
                k1 = min(nkc, k0 + 2)
                pairs.append((dst[:, k0:k1, 0:ncols], src[:, k0:k1, :]))
            sc.dma("pool", pairs, w=[dst_b])

        if phase_limit >= 4:
            x2 = region(O_R1, 65536, F32, "p (t d) -> p t d", t=16)
            x2s = region(O_TMP + 10560, 4096, F32)
            x2_b = [Buf("x2_%d" % i) for i in range(17)]
            for b_ in x2_b[:16]:
                sc.alias(b_, [KcT_b, Vc_b, kTs_b] + kT_b + v_b)
            sc.alias(x2_b[16], [qTs_b, kTs_b] + PT_b + kst_b + qT_b)
            for tt in range(16):
                sc.dma("sp", [(x2[:, tt, :], xp[tt * 128:(tt + 1) * 128, :])], w=[x2_b[tt]])
            sc.dma("sp", [(x2s[0:64, :], xs[0:64, :])], w=[x2_b[16]])
            with nc.allow_non_contiguous_dma(reason="tiny per-channel columns"):
                sc.dma("sp", [(colv2[:, C2_BG:C2_BG + 24], b_gate.rearrange("o (c p) -> p (o c)", p=128))], w=[colv_b])
            wo = region(O_TMP2, 16384, BF16, "p (c n) -> p c n", c=8)
            wo_b = Buf("wo")
            sc.alias(wo_b, [ep_b, scr_b])
            mT = region(O_TMP, 8192, BF16, "p (c t) -> p c t", c=8)
            mT_b = Buf("mT")
            gs = region(O_TMP + 8192, 2048, F32)
            gs_b = Buf("gs")
            sc.alias(mT_b, qT_b + kst_b + PT_b)
            sc.alias(gs_b, qT_b + kst_b + PT_b)

            def merge_branch(bi, srcT, srcT_bufs, wproj_ap):
                wp, wp_b = load_slab(wproj_ap, 0, 1024)
                wg, wg_b = load_slab(w_gate, bi * 1024, 1024)
                load_w(wo, wo_b, w_out, 0, 1024)
                cnt = [0]
                for (q0, n) in BQ:
                    for m in range(8):
                        pa, pg = cnt[0] % 2, 2 + cnt[0] % 2
                        cnt[0] += 1
                        for kc in range(8):
                            sc.op("pe", lambda kc=kc: E["pe"].matmul(ps_f32(pa)[:, :n], lhsT=wp[:, kc, m * 128:(m + 1) * 128],
                                                                    rhs=srcT[:, kc, q0:q0 + n], start=(kc == 0), stop=(kc == 7)),
                                  r=[wp_b] + srcT_bufs, w=[psb[pa]], signal=(kc == 7))
                        for kc in range(8):
                            sc.op("pe", lambda kc=kc: E["pe"].matmul(ps_f32(pg)[:, :n], lhsT=wg[:, kc, m * 128:(m + 1) * 128],
                                                                    rhs=hT[:, kc, q0:q0 + n], start=(kc == 0), stop=(kc == 7)),
                                  r=[wg_b] + hT_b, w=[psb[pg]], signal=(kc == 7))
                        sc.op("act", lambda: E["act"].activation(out=gs[:, :n], in_=ps_f32(pg)[:, :n], func=AF.Sigmoid,
                                                                bias=colv2[:, C2_BG + bi * 8 + m:C2_BG + bi * 8 + m + 1]),
                              r=[psb[pg], colv_b], w=[gs_b])
                        sc.op("dve", lambda: E["dve"].tensor_tensor(out=mT[:, m, :n], in0=gs[:, :n], in1=ps_f32(pa)[:, :n], op=ALU.mult),
                              r=[gs_b, psb[pa]], w=[mT_b])
                    ntile = (n + 127) // 128
                    for ti in range(ntile):
                        rows = min(128, n - ti * 128)
                        if q0 < 2048:
                            tt = q0 // 128 + ti
                            xt_ = x2[:, tt, :]
                        else:
                            tt = 16
                            xt_ = x2s
                        for half in range(2):
                            pb = 4 + (2 * ti + half) % 4
                            for kc in range(8):
                                sc.op("pe", lambda kc=kc: E["pe"].matmul(
                                    ps_f32(pb)[:rows, :], lhsT=mT[:, kc, ti * 128:ti * 128 + rows], rhs=wo[:, kc, half * 512:(half + 1) * 512],
                                    start=(kc == 0), stop=(kc == 7)), r=[mT_b, wo_b], w=[psb[pb]], signal=(kc == 7))
                            sc.op("dve", lambda: E["dve"].tensor_tensor(
                                out=xt_[:rows, half * 512:(half + 1) * 512], in0=xt_[:rows, half * 512:(half + 1) * 512],
                                in1=ps_f32(pb)[:rows, :], op=ALU.add), r=[psb[pb]], w=[x2_b[tt]])

            merge_branch(0, a_outT, aT_b, w_proj_a)

        if phase_limit >= 5:
            c_outT = region(O_A, 33792, BF16, "p (c t) -> p c t", c=8)
            cT_b = [Buf("cT%d" % i) for i in range(4)]
            for b_ in cT_b + stg_b:
                sc.alias(b_, aT_b)
            grow2 = region(O_A + 20480, 4096, F32)
            grow2_b = Buf("grow2")
            mst = region(O_A + 24576, 4096, F32)
            mst_b = Buf("mst")
            mxn = region(O_A + 28672, 2048, BF16)
            mxn_b = Buf("mxn")
            junkm = region(O_A + 30720, 2048, BF16)
            ssm = region(O_A + 32768, 3 * 4 * 4, F32, "p (k t) -> p k t", k=3)
            mjs_b = Buf("mjs")
            for b_ in (grow2_b, mst_b, mxn_b, mjs_b):
                sc.alias(b_, aT_b)
            memT = region(O_TMP2, 4096, BF16, "p (c t) -> p c t", c=8)
            memT_b = [Buf("memT%d" % i) for i in range(2)]
            for b_ in memT_b:
                sc.alias(b_, [wo_b])
            memkT = region(O_TMP, 4096, BF16, "p (c t) -> p c t", c=8)
            memkT_b = [Buf("memkT%d" % i) for i in range(2)]
            memv = region(O_TMP + 4096, 4128, BF16, "p (t h e) -> p t h e", t=2, h=4)
            memv_b = [Buf("memv%d" % i) for i in range(2)]
            qcTs = region(O_TMP + 8224, 1024, BF16, "p (h c t) -> p h c t", h=4, c=2)
            qcTs_b = Buf("qcTs")
            for b_ in memkT_b + memv_b + [qcTs_b]:
                sc.alias(b_, [mT_b, gs_b])
            sc.dma("sp", [(grow2, norm_mem.partition_broadcast(128))], w=[grow2_b])
            sc.op("pool", lambda: E["pool"].memset(memv[:, :, :, 256:257], 1.0), w=memv_b)
            MT = [(0, 128), (128, 128)]
            for mt in range(2):
                sc.dma("sp", [(mst, mem[mt * 128:(mt + 1) * 128, :])], w=[mst_b])
                sc.op("act", lambda: E["act"].activation(out=junkm, in_=mst, func=AF.Square, accum_out=ssm[:, 0, mt:mt + 1]),
                      r=[mst_b], w=[mjs_b])
                sc.op("act", lambda: E["act"].activation(out=ssm[:, 1, mt:mt + 1], in_=ssm[:, 0, mt:mt + 1], func=AF.Ln, scale=1.0 / D, bias=EPS),
                      r=[mjs_b], w=[mjs_b])
                sc.op("act", lambda: E["act"].activation(out=ssm[:, 2, mt:mt + 1], in_=ssm[:, 1, mt:mt + 1], func=AF.Exp, scale=-0.5),
                      r=[mjs_b], w=[mjs_b])
                sc.op("dve", lambda: E["dve"].scalar_tensor_tensor(out=mxn, in0=mst, scalar=ssm[:, 2, mt:mt + 1], in1=grow2,
                                                                   op0=ALU.mult, op1=ALU.mult), r=[mst_b, mjs_b, grow2_b], w=[mxn_b])
                transpose_to_T(mxn, mxn_b, 128, memT, memT_b[mt], mt * 128, mt % 2, "act")
            wmk_sl = load_slab(w_mem_kv, 0, 1024)
            wmv_sl = load_slab(w_mem_kv, 1024, 1024)
            kv_phase(memT, memT_b, MT, wmk_sl, wmv_sl,
                     lambda tt: nmk[tt * 128:(tt + 1) * 128, :], lambda tt: nmv[tt * 128:(tt + 1) * 128, :],
                     memkT, memkT_b, lambda tt: memv[:, tt], memv_b, 2, 256)

            for b_ in cT_b:
                sc.alias(b_, stg_b)
            qcT = region(O_TMP2, 8448, BF16, "p (c t) -> p c t", c=2)
            qcT_b = Buf("qcT")
            sc.alias(qcT_b, memT_b)
            PTc = [region(O_TMP2 + 8448 + i * 1024, 1024, BF16) for i in range(4)]
            PTc_b = [Buf("PTc%d" % i) for i in range(4)]
            cn = [region(O_TMP2 + 12544 + i * 512, 512, BF16) for i in range(2)]
            cn_b = [Buf("cn%d" % i) for i in range(2)]
            recc = region(O_TMP2 + 13568, 32, F32)
            recc_b = Buf("recc")
            cks = region(O_TMP2 + 13600, 2048, BF16)
            cks_b = Buf("cks")
            for b_ in PTc_b + cn_b + [recc_b, cks_b]:
                sc.alias(b_, [wo_b])
            wqc, wqc_b = load_slab(w_in, 5120, 1024)
            ccount = [0]

            def c_epilogue(h, accbank, nr, c0):
                i = ccount[0] % 2
                ccount[0] += 1
                sc.op("dve", lambda: E["dve"].reciprocal(out=recc[:nr, i:i + 1], in_=ps_f32(accbank)[:nr, 256:257]), r=[psb[accbank]], w=[recc_b])
                sc.op("dve", lambda: E["dve"].tensor_scalar(cn[i][:nr, :], ps_f32(accbank)[:nr, 0:256], recc[:nr, i:i + 1], None, op0=ALU.mult),
                      r=[psb[accbank], recc_b], w=[cn_b[i]])
                tb = 2 + i
                pv = ps_bf(tb).rearrange("p (c t) -> p c t", c=8)
                for dcx in range(2):
                    sc.op("pe", lambda dcx=dcx: E["pe"].transpose(out=pv[:, dcx, :nr], in_=cn[i][:nr, dcx * 128:(dcx + 1) * 128],
                                                                 identity=identb[:nr, :nr]), r=[cn_b[i], cB], w=[psb[tb]], signal=(dcx == 1))
                sc.op("act", lambda: E["act"].copy(out=c_outT[:, 2 * h:2 * h + 2, c0:c0 + nr], in_=pv[:, 0:2, :nr]), r=[psb[tb]], w=[cT_b[h]])

            for h in range(4):
                for dc in range(2):
                    for bi, (q0, n) in enumerate(BQ):
                        tb = 2 + (bi % 2)
                        for kc in range(8):
                            sc.op("pe", lambda kc=kc: E["pe"].matmul(
                                ps_f32(tb)[:, :n], lhsT=wqc[:, kc, h * 256 + dc * 128:h * 256 + (dc + 1) * 128], rhs=hT[:, kc, q0:q0 + n],
                                start=(kc == 0), stop=(kc == 7)), r=[wqc_b] + hT_b, w=[psb[tb]], signal=(kc == 7))
                        if bi < 4:
                            sc.op("act", lambda: E["act"].activation(out=qcT[:, dc, q0:q0 + n], in_=ps_f32(tb)[:, :n], func=AF.Copy, scale=0.0625),
                                  r=[psb[tb]], w=[qcT_b])
                        else:
                            sc.op("act", lambda: E["act"].activation(out=qcTs[:, h, dc, :], in_=ps_f32(tb)[:, :n], func=AF.Copy, scale=0.0625),
                                  r=[psb[tb]], w=[qcTs_b])
                for qb in range(4):
                    q0 = qb * 512
                    for mt in range(2):
                        sb = mt
                        for dc in range(2):
                            sc.op("pe", lambda dc=dc: E["pe"].matmul(
                                ps_f32(sb)[:, :512], lhsT=memkT[:, 2 * h + dc, mt * 128:(mt + 1) * 128], rhs=qcT[:, dc, q0:q0 + 512],
                                start=(dc == 0), stop=(dc == 1)), r=[memkT_b[mt], qcT_b], w=[psb[sb]], signal=(dc == 1))
                        pi = 2 * (qb % 2) + mt
                        sc.op("act", lambda: E["act"].activation(out=PTc[pi][:, :512], in_=ps_f32(sb)[:, :512], func=AF.Exp),
                              r=[psb[sb]], w=[PTc_b[pi]])
                        for j in range(4):
                            sc.op("pe", lambda j=j: E["pe"].matmul(
                                ps_f32(4 + j)[:, 0:257], lhsT=PTc[pi][:, j * 128:(j + 1) * 128], rhs=memv[:, mt, h, 0:257],
                                start=(mt == 0), stop=(mt == 1)), r=[PTc_b[pi], memv_b[mt]], w=[psb[4 + j]], signal=True)
                    for j in range(4):
                        c_epilogue(h, 4 + j, 128, q0 + j * 128)

            for s_ in range(2):
                for mt in range(2):
                    sc.dma("pool", [(memv[:, mt, :, 0:256], cmv[s_, mt * 128:(mt + 1) * 128, :].rearrange("p (h e) -> p h e", h=4))],
                           w=[memv_b[mt]])
                    sc.dma("pool", [(cks, cmk[s_, mt * 128:(mt + 1) * 128, :])], w=[cks_b])
                    pv = ps_bf(mt).rearrange("p (c t) -> p c t", c=8)
                    for c in range(8):
                        sc.op("pe", lambda c=c, pv=pv: E["pe"].transpose(out=pv[:, c, :], in_=cks[:, c * 128:(c + 1) * 128], identity=identb),
                              r=[cks_b, cB], w=[psb[mt]], signal=(c == 7))
                    sc.op("dve", lambda pv=pv: E["dve"].tensor_copy(out=memkT[:, :, mt * 128:(mt + 1) * 128], in_=pv), r=[psb[mt]], w=[memkT_b[mt]])
                for h in range(4):
                    sb = h % 2
                    first = True
                    for mt in range(2):
                        for dc in range(2):
                            sc.op("pe", lambda mt=mt, dc=dc, first=first: E["pe"].matmul(
                                ps_f32(sb)[:, mt * 32:(mt + 1) * 32], lhsT=memkT[:, 2 * h + dc, mt * 128:(mt + 1) * 128],
                                rhs=qcTs[:, h, dc, 32 * s_:32 * s_ + 32], start=first, stop=(dc == 1), skip_group_check=True),
                                r=memkT_b + [qcTs_b], w=[psb[sb]], signal=(mt == 1 and dc == 1))
                            first = False
                    pi = h % 4
                    sc.op("act", lambda: E["act"].activation(out=PTc[pi][:, :64], in_=ps_f32(sb)[:, :64], func=AF.Exp), r=[psb[sb]], w=[PTc_b[pi]])
                    ab = 4 + h
                    for mt in range(2):
                        sc.op("pe", lambda mt=mt: E["pe"].matmul(
                            ps_f32(ab)[0:32, 0:257], lhsT=PTc[pi][:, mt * 32:(mt + 1) * 32], rhs=memv[:, mt, h, 0:257],
                            start=(mt == 0), stop=(mt == 1)), r=[PTc_b[pi]] + memv_b, w=[psb[ab]], signal=(mt == 1))
                    c_epilogue(h, ab, 32, 2048 + 32 * s_)

            if DEBUG == "cT":
                sc.dma("sp", [(dbg[:, c * 2112:(c + 1) * 2112], c_outT[:, c, :]) for c in range(8)], r=cT_b)
            if phase_limit >= 6:
                merge_branch(2, c_outT, cT_b, w_proj_c)

        if phase_limit >= 7:
            b_outT = region(O_A, 33792, BF16, "p (c t) -> p c t", c=8)
            bT_b = [Buf("bT%d" % i) for i in range(8)]
            for b_ in bT_b:
                sc.alias(b_, cT_b + stg_b + [grow2_b, mst_b, mxn_b, mjs_b])
            colL = calloc(80 * 4, F32)
            colL_b = Buf("colL")
            with nc.allow_non_contiguous_dma(reason="tiny per-channel columns"):
                sc.dma("sp", [(colL[:, 0:8], b_rg_a.rearrange("o (c p) -> p (o c)", p=128)),
                              (colL[:, 8:16], b_rg_x.rearrange("o (c p) -> p (o c)", p=128)),
                              (colL[:, 16:24], conv_b.rearrange("o (c p) -> p (o c)", p=128)),
                              (colL[:, 24:56].rearrange("p (j c) -> p j c", j=4), conv_w.rearrange("j (c p) -> p j c", p=128)),
                              (colL[:, 72:80], rg_lambda.rearrange("o (c p) -> p (o c)", p=128))], w=[colL_b])
            sc.op("dve", lambda: E["dve"].tensor_scalar(colL[:, 0:16], colL[:, 0:16], -1.0, None, op0=ALU.mult), r=[colL_b], w=[colL_b])
            sc.op("act", lambda: E["act"].activation(out=colL[:, 72:80], in_=colL[:, 72:80], func=AF.Exp, scale=-1.0), r=[colL_b], w=[colL_b])
            sc.op("act", lambda: E["act"].activation(out=colL[:, 72:80], in_=colL[:, 72:80], func=AF.Ln, bias=1.0), r=[colL_b], w=[colL_b])
            sc.op("dve", lambda: E["dve"].tensor_scalar(colL[:, 56:64], colL[:, 72:80], -8.0, None, op0=ALU.mult), r=[colL_b], w=[colL_b])
            sc.op("dve", lambda: E["dve"].tensor_scalar(colL[:, 64:72], colL[:, 72:80], -16.0, None, op0=ALU.mult), r=[colL_b], w=[colL_b])

            lru_old = [wo_b, mT_b, gs_b, qcT_b, recc_b, cks_b, qcTs_b] + PTc_b + cn_b + memkT_b + memv_b + memT_b
            def lbuf(name):
                b_ = Buf(name)
                sc.alias(b_, lru_old)
                return b_
            wrga = region(O_TMP2, 2048, BF16, "p (n j) -> p n j", n=8)
            wrgx = region(O_TMP2 + 2048, 2048, BF16, "p (n j) -> p n j", n=8)
            wrg_b = lbuf("wrg")
            sc.dma("pool", [(wrga, w_rg_a.rearrange("n i j -> i n j")), (wrgx, w_rg_x.rearrange("n i j -> i n j"))], w=[wrg_b])
            LW = 256
            def tset(i):
                base = (O_TMP2 + 4096) if i == 0 else O_TMP
                o = [base]
                def mk(nbytes, dt, name):
                    ap = region(o[0], nbytes, dt)
                    o[0] += nbytes
                    return ap, lbuf(name + str(i))
                d = {}
                d["xpad"] = mk(1040, F32, "xpad")
                d["xc"] = mk(1024, F32, "xc")
                d["xcb"] = mk(512, BF16, "xcb")
                d["tRr"] = mk(1024, F32, "tRr")
                d["tA"] = mk(1024, F32, "tA")
                d["tA2"] = mk(1024, F32, "tA2")
                d["tI"] = mk(1024, F32, "tI")
                d["hh"] = mk(1024, F32, "hh")
                d["gbs"] = mk(1024, F32, "gbs")
                d["tG"] = mk(1024, F32, "tG")
                return d
            TS = [tset(0), tset(1)]
            hst = region(O_TMP + 9744, 16, F32); hst_b = lbuf("hst")
            xpad_b = [TS[0]["xpad"][1], TS[1]["xpad"][1]]
            xc_b, xcb_b, tA_b, tA2_b = TS[0]["xc"][1], TS[0]["xcb"][1], TS[0]["tA"][1], TS[0]["tA2"][1]
            tI_b, hh_b, gbs_b, tG_b, tRr_b = TS[1]["tI"][1], TS[1]["hh"][1], TS[1]["gbs"][1], TS[1]["tG"][1], TS[1]["tRr"][1]
            lru_all_b = [v_[1] for d_ in TS for v_ in d_.values()] + [hst_b]

            wxb, wxb_b = load_slab(w_in, 3072, 1024)
            wgb, wgb_b = load_slab(w_in, 4096, 1024)
            LB = [(i * LW, LW, -1) for i in range(2048 // LW)] + [(2048, 32, 0), (2080, 32, 1)]
            NPB = 2048 // LW
            blocks = [(n, li) for n in range(8) for li in range(len(LB))]

            def sig3(buf_ap, buf_b, src_ap, src_bufs, w, scale, bias):
                kw = {} if bias is None else {"bias": bias}
                sc.op("act", lambda: E["act"].activation(out=buf_ap[:, :w], in_=src_ap, func=AF.Exp, scale=-scale, **kw),
                      r=src_bufs + [colL_b], w=[buf_b])
                sc.op("act", lambda: E["act"].activation(out=buf_ap[:, :w], in_=buf_ap[:, :w], func=AF.Ln, bias=1.0), r=[buf_b], w=[buf_b])
                sc.op("act", lambda: E["act"].activation(out=buf_ap[:, :w], in_=buf_ap[:, :w], func=AF.Exp, scale=-1.0), r=[buf_b], w=[buf_b])

            def lru_stage1(k_):
                n, li = blocks[k_]
                q0, w, sq = LB[li]
                S_, P_ = TS[k_ % 2], TS[(k_ + 1) % 2]
                cur, cur_b = S_["xpad"]
                prv, prv_b = P_["xpad"]
                xc, xc_b_ = S_["xc"]
                xcb, xcb_b_ = S_["xcb"]
                px, pg, pa, pi_ = k_ % 2, 2 + k_ % 2, 4 + k_ % 2, 6 + k_ % 2
                for kc in range(8):
                    sc.op("pe", lambda kc=kc: E["pe"].matmul(ps_f32(px)[:, :w], lhsT=wxb[:, kc, n * 128:(n + 1) * 128], rhs=hT[:, kc, q0:q0 + w],
                                                            start=(kc == 0), stop=(kc == 7)), r=[wxb_b] + hT_b, w=[psb[px]], signal=(kc == 7))
                for kc in range(8):
                    sc.op("pe", lambda kc=kc: E["pe"].matmul(ps_f32(pg)[:, :w], lhsT=wgb[:, kc, n * 128:(n + 1) * 128], rhs=hT[:, kc, q0:q0 + w],
                                                            start=(kc == 0), stop=(kc == 7)), r=[wgb_b] + hT_b, w=[psb[pg]], signal=(kc == 7))
                if li == 0:
                    sc.op("dve", lambda: E["dve"].memset(cur[:, 0:3], 0.0), w=[cur_b])
                elif sq < 0:
                    sc.op("dve", lambda: E["dve"].tensor_copy(out=cur[:, 0:3], in_=prv[:, LW:LW + 3]), r=[prv_b], w=[cur_b])
                else:
                    with nc.allow_non_contiguous_dma(reason="tiny conv state"):
                        sc.dma("sp", [(cur[:, 0:3], sconv[sq, :, n * 128:(n + 1) * 128].rearrange("j c -> c j"))], w=[cur_b])
                sc.op("dve", lambda: E["dve"].tensor_copy(out=cur[:, 3:3 + w], in_=ps_f32(px)[:, :w]), r=[psb[px]], w=[cur_b])
                if li == NPB - 1 or sq >= 0:
                    dst = nconv if sq < 0 else nconvs[sq]
                    with nc.allow_non_contiguous_dma(reason="tiny conv state"):
                        sc.dma("sp", [(dst[:, n * 128:(n + 1) * 128].rearrange("j c -> c j"), cur[:, w:w + 3])], r=[cur_b])
                sc.op("dve", lambda: E["dve"].tensor_scalar(xc[:, :w], cur[:, 3:3 + w], colL[:, 24 + 3 * 8 + n:24 + 3 * 8 + n + 1],
                                                            colL[:, 16 + n:17 + n], op0=ALU.mult, op1=ALU.add), r=[cur_b, colL_b], w=[xc_b_])
                for j in range(3):
                    sc.op("dve", lambda j=j: E["dve"].scalar_tensor_tensor(out=xc[:, :w], in0=cur[:, j:j + w],
                                                                          scalar=colL[:, 24 + j * 8 + n:24 + j * 8 + n + 1], in1=xc[:, :w],
                                                                          op0=ALU.mult, op1=ALU.add), r=[cur_b, colL_b, xc_b_], w=[xc_b_])
                sc.op("dve", lambda: E["dve"].tensor_copy(out=xcb[:, :w], in_=xc[:, :w]), r=[xc_b_], w=[xcb_b_])
                sc.op("pe", lambda: E["pe"].matmul(ps_f32(pa)[:, :w], lhsT=wrga[:, n, :], rhs=xcb[:, :w], start=True, stop=True),
                      r=[wrg_b, xcb_b_], w=[psb[pa]])
                sc.op("pe", lambda: E["pe"].matmul(ps_f32(pi_)[:, :w], lhsT=wrgx[:, n, :], rhs=xcb[:, :w], start=True, stop=True),
                      r=[wrg_b, xcb_b_], w=[psb[pi_]])

            def lru_stage2(k_):
                n, li = blocks[k_]
                q0, w, sq = LB[li]
                S_ = TS[k_ % 2]
                xc, xc_b_ = S_["xc"]
                tRr, tRr_b_ = S_["tRr"]
                tA, tA_b_ = S_["tA"]
                tA2, tA2_b_ = S_["tA2"]
                tI, tI_b_ = S_["tI"]
                hh, hh_b_ = S_["hh"]
                gbs, gbs_b_ = S_["gbs"]
                tG, tG_b_ = S_["tG"]
                pg, pa, pi_ = 2 + k_ % 2, 4 + k_ % 2, 6 + k_ % 2
                sc.op("act", lambda: E["act"].copy(out=gbs[:, :w], in_=ps_f32(pg)[:, :w]), r=[psb[pg]], w=[gbs_b_])
                sc.op("act", lambda: E["act"].activation(out=tG[:, :w], in_=ps_f32(pg)[:, :w], func=AF.Square), r=[psb[pg]], w=[tG_b_])
                sc.op("dve", lambda: E["dve"].tensor_scalar(tG[:, :w], tG[:, :w], 0.044715, 1.0, op0=ALU.mult, op1=ALU.add), r=[tG_b_], w=[tG_b_])
                sc.op("dve", lambda: E["dve"].tensor_tensor(out=tG[:, :w], in0=tG[:, :w], in1=gbs[:, :w], op=ALU.mult), r=[tG_b_, gbs_b_], w=[tG_b_])
                def sig_ops(buf_ap, buf_b, src_ap, src_bufs, scale, bias):
                    kw = {} if bias is None else {"bias": bias}
                    return [
                        lambda: sc.op("act", lambda: E["act"].activation(out=buf_ap[:, :w], in_=src_ap, func=AF.Exp, scale=-scale, **kw),
                                      r=src_bufs + [colL_b], w=[buf_b]),
                        lambda: sc.op("act", lambda: E["act"].activation(out=buf_ap[:, :w], in_=buf_ap[:, :w], func=AF.Ln, bias=1.0),
                                      r=[buf_b], w=[buf_b]),
                        lambda: sc.op("act", lambda: E["act"].activation(out=buf_ap[:, :w], in_=buf_ap[:, :w], func=AF.Exp, scale=-1.0),
                                      r=[buf_b], w=[buf_b]),
                    ]
                R_ = sig_ops(tRr, tRr_b_, ps_f32(pa)[:, :w], [psb[pa]], 1.0, colL[:, n:n + 1])
                I_ = sig_ops(tI, tI_b_, ps_f32(pi_)[:, :w], [psb[pi_]], 1.0, colL[:, 8 + n:9 + n])
                G_ = sig_ops(tG, tG_b_, tG[:, :w], [tG_b_], 1.5957691216057308, None)
                op_a = lambda: sc.op("act", lambda: E["act"].activation(out=tA[:, :w], in_=tRr[:, :w], func=AF.Exp, scale=colL[:, 56 + n:57 + n]),
                                     r=[tRr_b_, colL_b], w=[tA_b_])
                op_a2 = lambda: sc.op("act", lambda: E["act"].activation(out=tA2[:, :w], in_=tRr[:, :w], func=AF.Exp, scale=colL[:, 64 + n:65 + n]),
                                      r=[tRr_b_, colL_b], w=[tA2_b_])
                op_lnm = lambda: sc.op("act", lambda: E["act"].activation(out=tA2[:, :w], in_=tA2[:, :w], func=AF.Ln, scale=-1.0, bias=1.0),
                                       r=[tA2_b_], w=[tA2_b_])
                op_expm = lambda: sc.op("act", lambda: E["act"].activation(out=tA2[:, :w], in_=tA2[:, :w], func=AF.Exp, scale=0.5),
                                        r=[tA2_b_], w=[tA2_b_])
                for f_ in (R_[0], I_[0], G_[0], R_[1], I_[1], G_[1], R_[2], I_[2], op_a2, op_a, op_lnm, G_[2], op_expm):
                    f_()
                sc.op("dve", lambda: E["dve"].tensor_tensor(out=tI[:, :w], in0=tI[:, :w], in1=tA2[:, :w], op=ALU.mult), r=[tI_b_, tA2_b_], w=[tI_b_])
                sc.op("dve", lambda: E["dve"].tensor_tensor(out=tI[:, :w], in0=tI[:, :w], in1=xc[:, :w], op=ALU.mult), r=[tI_b_, xc_b_], w=[tI_b_])
                if li == 0:
                    init = 0.0
                    rr = []
                elif sq < 0:
                    init = hst[:, 0:1]
                    rr = [hst_b]
                else:
                    with nc.allow_non_contiguous_dma(reason="tiny lru state"):
                        sc.dma("sp", [(hst[:, 1 + sq:2 + sq], slru[sq:sq + 1, n * 128:(n + 1) * 128].rearrange("o c -> c o"))], w=[hst_b])
                    init = hst[:, 1 + sq:2 + sq]
                    rr = [hst_b]
                sc.op("dve", lambda: E["dve"].tensor_tensor_scan(out=hh[:, :w], data0=tA[:, :w], data1=tI[:, :w], initial=init,
                                                                 op0=ALU.mult, op1=ALU.add), r=[tA_b_, tI_b_] + rr, w=[hh_b_])
                if sq < 0 and li < NPB - 1:
                    sc.op("dve", lambda: E["dve"].tensor_copy(out=hst[:, 0:1], in_=hh[:, w - 1:w]), r=[hh_b_], w=[hst_b])
                if li == NPB - 1 or sq >= 0:
                    dst = nlru[0:1, n * 128:(n + 1) * 128] if sq < 0 else nlrus[sq:sq + 1, n * 128:(n + 1) * 128]
                    with nc.allow_non_contiguous_dma(reason="tiny lru state"):
                        sc.dma("sp", [(dst.rearrange("o c -> c o"), hh[:, w - 1:w])], r=[hh_b_])
                sc.op("dve", lambda: E["dve"].tensor_tensor(out=tG[:, :w], in0=tG[:, :w], in1=gbs[:, :w], op=ALU.mult), r=[tG_b_, gbs_b_], w=[tG_b_])
                sc.op("dve", lambda: E["dve"].tensor_tensor(out=b_outT[:, n, q0:q0 + w], in0=tG[:, :w], in1=hh[:, :w], op=ALU.mult),
                      r=[tG_b_, hh_b_], w=[bT_b[n]])

            lru_stage1(0)
            for k_ in range(len(blocks)):
                if k_ + 1 < len(blocks):
                    lru_stage1(k_ + 1)
                lru_stage2(k_)

            if DEBUG == "bT":
                sc.dma("sp", [(dbg[:, c * 2112:(c + 1) * 2112], b_outT[:, c, :]) for c in range(8)], r=bT_b)
            if phase_limit >= 8:
                merge_branch(1, b_outT, bT_b, w_proj_b)

        if phase_limit >= 9:
            T17 = [(i * 128, 128, x2[:, i, :], x2_b[i]) for i in range(16)] + [(2048, 64, x2s, x2_b[16])]
            fin_old = [wo_b, wrg_b] + lru_all_b
            grow3 = region(O_TMP2, 4096, F32); grow3_b = Buf("grow3"); sc.alias(grow3_b, fin_old)
            xn3 = [region(O_TMP2 + 4096 + i * 2048, 2048, BF16) for i in range(2)]
            xn3_b = [Buf("xn3_%d" % i) for i in range(2)]
            junk3 = region(O_TMP2 + 8192, 2048, BF16)
            ss3 = region(O_TMP2 + 10240, 3 * 17 * 4, F32, "p (k t) -> p k t", k=3)
            js3_b = Buf("js3")
            for b_ in xn3_b + [js3_b]:
                sc.alias(b_, fin_old)
            h2T = hT
            h2T_b = [Buf("h2T%d" % i) for i in range(17)]
            for b_ in h2T_b:
                sc.alias(b_, hT_b)
            sc.dma("sp", [(grow3, norm_ffn.partition_broadcast(128))], w=[grow3_b])

            def rms2(st, st_b, rows, ssi, g_ap, g_b, out_ap, out_b, junk_ap, ss_ap, js_b):
                sc.op("act", lambda: E["act"].activation(out=junk_ap[:rows], in_=st[:rows], func=AF.Square,
                                                        accum_out=ss_ap[:rows, 0, ssi:ssi + 1]), r=[st_b], w=[js_b])
                sc.op("act", lambda: E["act"].activation(out=ss_ap[:rows, 1, ssi:ssi + 1], in_=ss_ap[:rows, 0, ssi:ssi + 1],
                                                        func=AF.Ln, scale=1.0 / D, bias=EPS), r=[js_b], w=[js_b])
                sc.op("act", lambda: E["act"].activation(out=ss_ap[:rows, 2, ssi:ssi + 1], in_=ss_ap[:rows, 1, ssi:ssi + 1],
                                                        func=AF.Exp, scale=-0.5), r=[js_b], w=[js_b])
                sc.op("dve", lambda: E["dve"].scalar_tensor_tensor(
                    out=out_ap[:rows], in0=st[:rows], scalar=ss_ap[:rows, 2, ssi:ssi + 1], in1=g_ap[:rows],
                    op0=ALU.mult, op1=ALU.mult), r=[st_b, js_b, g_b], w=[out_b])

            for ti, (t0, rows, xt_, xb_) in enumerate(T17):
                rms2(xt_, xb_, rows, ti, grow3, grow3_b, xn3[ti % 2], xn3_b[ti % 2], junk3, ss3, js3_b)
                transpose_to_T(xn3[ti % 2], xn3_b[ti % 2], rows, h2T, h2T_b[ti], t0, ti % 2, "act")

            actT = region(O_A, 33792, BF16, "p (c t) -> p c t", c=8)
            actT_b = Buf("actT")
            sc.alias(actT_b, bT_b)
            wo2 = region(O_TMP2, 16384, BF16, "p (c n) -> p c n", c=8)
            wo2_b = Buf("wo2")
            sc.alias(wo2_b, [grow3_b, js3_b] + xn3_b)
            tS = [region(O_TMP + i * 2048, 2048, F32) for i in range(2)]
            tS_b = [Buf("tS%d" % i) for i in range(2)]
            for b_ in tS_b:
                sc.alias(b_, [mT_b, gs_b] + lru_all_b)
            fc = [0]
            for (c0, cnt) in ((0, 8), (8, 8), (16, 6)):
                wgs, wgs_b = load_slab(w_ffn_in, c0 * 128, cnt * 128)
                wus, wus_b = load_slab(w_ffn_in, DFF + c0 * 128, cnt * 128)
                load_w(wo2, wo2_b, w_ffn_out, 0, 1024, kc0=c0, nkc=cnt)
                for (q0, n) in BQ:
                    for j in range(cnt):
                        k_ = fc[0]
                        fc[0] += 1
                        pg, pu = k_ % 2, 2 + k_ % 2
                        for kc in range(8):
                            sc.op("pe", lambda kc=kc: E["pe"].matmul(ps_f32(pg)[:, :n], lhsT=wgs[:, kc, j * 128:(j + 1) * 128],
                                                                    rhs=h2T[:, kc, q0:q0 + n], start=(kc == 0), stop=(kc == 7)),
                                  r=[wgs_b] + h2T_b, w=[psb[pg]], signal=(kc == 7))
                        for kc in range(8):
                            sc.op("pe", lambda kc=kc: E["pe"].matmul(ps_f32(pu)[:, :n], lhsT=wus[:, kc, j * 128:(j + 1) * 128],
                                                                    rhs=h2T[:, kc, q0:q0 + n], start=(kc == 0), stop=(kc == 7)),
                                  r=[wus_b] + h2T_b, w=[psb[pu]], signal=(kc == 7))
                        sc.op("act", lambda: E["act"].activation(out=tS[k_ % 2][:, :n], in_=ps_f32(pg)[:, :n], func=AF.Silu),
                              r=[psb[pg]], w=[tS_b[k_ % 2]])
                        sc.op("dve", lambda: E["dve"].tensor_tensor(out=actT[:, j, q0:q0 + n], in0=tS[k_ % 2][:, :n], in1=ps_f32(pu)[:, :n],
                                                                    op=ALU.mult), r=[tS_b[k_ % 2], psb[pu]], w=[actT_b])
                    ntile = (n + 127) // 128
                    for ti in range(ntile):
                        rows = min(128, n - ti * 128)
                        if q0 < 2048:
                            tt = q0 // 128 + ti
                            xt_ = x2[:, tt, :]
                        else:
                            tt = 16
                            xt_ = x2s
                        for half in range(2):
                            pb = 4 + (2 * ti + half) % 4
                            for j in range(cnt):
                                sc.op("pe", lambda j=j: E["pe"].matmul(
                                    ps_f32(pb)[:rows, :], lhsT=actT[:, j, q0 + ti * 128:q0 + ti * 128 + rows],
                                    rhs=wo2[:, j, half * 512:(half + 1) * 512], start=(j == 0), stop=(j == cnt - 1)),
                                    r=[actT_b, wo2_b], w=[psb[pb]], signal=(j == cnt - 1))
                            sc.op("dve", lambda: E["dve"].tensor_tensor(
                                out=xt_[:rows, half * 512:(half + 1) * 512], in0=xt_[:rows, half * 512:(half + 1) * 512],
                                in1=ps_f32(pb)[:rows, :], op=ALU.add), r=[psb[pb]], w=[x2_b[tt]])

            growF = region(O_A, 4096, F32); growF_b = Buf("growF"); sc.alias(growF_b, [actT_b])
            yst = [region(O_A + 4096 + i * 4096, 4096, F32) for i in range(2)]
            yst_b = [Buf("yst%d" % i) for i in range(2)]
            junkF = region(O_A + 12288, 2048, BF16)
            ssF = region(O_A + 14336, 3 * 17 * 4, F32, "p (k t) -> p k t", k=3)
            jsF_b = Buf("jsF")
            for b_ in yst_b + [jsF_b]:
                sc.alias(b_, [actT_b])
            sc.dma("sp", [(growF, norm_final.partition_broadcast(128))], w=[growF_b])
            for ti, (t0, rows, xt_, xb_) in enumerate(T17):
                rms2(xt_, xb_, rows, ti, growF, growF_b, yst[ti % 2], yst_b[ti % 2], junkF, ssF, jsF_b)
                dst = yp[t0:t0 + rows, :] if ti < 16 else ys[0:64, :]
                sc.dma("sp", [(dst, yst[ti % 2][:rows])], r=[yst_b[ti % 2]])

        if DEBUG == "x2":
            sc.dma("sp", [(dbg2[tt * 128:(tt + 1) * 128, :], x2[:, tt, :]) for tt in range(16)] + [(dbg2[2048:2112, :], x2s[0:64, :])],
                   r=x2_b)
        if DEBUG == "aT" and phase_limit >= 3:
            sc.dma("sp", [(dbg[:, c * 2112:(c + 1) * 2112], a_outT[:, c, :]) for c in range(8)], r=aT_b)
        if DEBUG == "hT":
            sc.dma("sp", [(dbg[:, c * 2112:(c + 1) * 2112], hT[:, c, :]) for c in range(8)], r=hT_b)
        sc.finish()
    return nc


_NC_CACHE = {}
_DBG = None


def _get_nc():
    lim = int(os.environ.get("MK_PHASE_LIMIT", "99"))
    if lim not in _NC_CACHE:
        _NC_CACHE[lim] = build_program(lim)
    return _NC_CACHE[lim]


def kernel(x_prompt, x_sample, mem_prompt, cache_k, cache_v, state_conv, state_lru, cache_mem_k, cache_mem_v,
           rel_table, norm_mix, w_in, lambda_q1, lambda_k1, lambda_q2, lambda_k2, subln_g, conv_w, conv_b,
           w_rg_a, b_rg_a, w_rg_x, b_rg_x, rg_lambda, norm_mem, w_mem_kv, w_proj_a, w_proj_b, w_proj_c,
           w_gate, b_gate, w_out, norm_ffn, w_ffn_in, w_ffn_out, norm_final):
    f = lambda a: np.ascontiguousarray(np.asarray(a, dtype=np.float32))
    nc = _get_nc()
    shared = {
        "rel_table": f(rel_table), "boh": _bias_onehot(), "ident": np.eye(128, dtype=np.float32), "aident": np.ascontiguousarray(np.eye(128, dtype=np.float32)[::-1]), "norm_mix": f(norm_mix), "w_in": f(w_in)[0],
        "lq1": f(lambda_q1), "lk1": f(lambda_k1), "lq2": f(lambda_q2), "lk2": f(lambda_k2),
        "subln_g": f(subln_g), "conv_w": f(conv_w)[0], "conv_b": f(conv_b), "w_rg_a": f(w_rg_a)[0],
        "b_rg_a": f(b_rg_a), "w_rg_x": f(w_rg_x)[0], "b_rg_x": f(b_rg_x), "rg_lambda": f(rg_lambda),
        "norm_mem": f(norm_mem), "w_mem_kv": f(w_mem_kv)[0], "w_proj_a": f(w_proj_a)[0],
        "w_proj_b": f(w_proj_b)[0], "w_proj_c": f(w_proj_c)[0], "w_gate": f(w_gate)[0], "b_gate": f(b_gate),
        "w_out": f(w_out)[0], "norm_ffn": f(norm_ffn), "w_ffn_in": f(w_ffn_in)[0], "w_ffn_out": f(w_ffn_out)[0],
        "norm_final": f(norm_final).reshape(1, D),
    }
    x_prompt = f(x_prompt); x_sample = f(x_sample); mem_prompt = f(mem_prompt)
    cache_k = f(cache_k); cache_v = f(cache_v); state_conv = f(state_conv); state_lru = f(state_lru)
    cache_mem_k = f(cache_mem_k); cache_mem_v = f(cache_mem_v)
    in_maps = []
    for c in range(NCORES):
        m = dict(shared)
        m["xp"] = x_prompt[c]
        m["xs"] = x_sample[2 * c:2 * c + 2].reshape(64, D)
        m["mem"] = mem_prompt[c]
        m["ck"] = cache_k[0, 2 * c:2 * c + 2].reshape(2, S, D)
        m["cv"] = cache_v[0, 2 * c:2 * c + 2].reshape(2, S, D)
        m["sconv"] = state_conv[0, 2 * c:2 * c + 2]
        m["slru"] = state_lru[0, 2 * c:2 * c + 2]
        m["cmk"] = cache_mem_k[0, 2 * c:2 * c + 2].reshape(2, 256, D)
        m["cmv"] = cache_mem_v[0, 2 * c:2 * c + 2].reshape(2, 256, D)
        in_maps.append(m)
    res = run_bass_kernel_spmd(nc, in_maps, core_ids=list(range(NCORES)))
    R = res.results
    global _DBG
    _DBG = R[0].get("debug_out") if isinstance(R[0], dict) else None
    global _DBG2
    _DBG2 = R[0].get("debug_x2") if isinstance(R[0], dict) else None
    cat = lambda k: np.stack([np.asarray(R[c][k], dtype=np.float32) for c in range(NCORES)])
    y_prompt = cat("yp")
    y_sample = cat("ys").reshape(16, 32, D)
    new_k_p = cat("nk").reshape(1, 8, S, 8, 2, 64)
    new_v_p = cat("out_v").reshape(1, 8, S, 8, 128)
    new_conv_p = cat("nconv").reshape(1, 8, 3, D)
    new_lru_p = cat("nlru").reshape(1, 8, D)
    new_mk = cat("nmk").reshape(1, 8, 256, 4, 256)
    new_mv = cat("nmv").reshape(1, 8, 256, 4, 256)
    new_k_s = cat("nks").reshape(1, 16, 32, 8, 2, 64)
    new_v_s = cat("out_vs").reshape(1, 16, 32, 8, 128)
    new_conv_s = cat("nconvs").reshape(1, 16, 3, D)
    new_lru_s = cat("nlrus").reshape(1, 16, D)
    return (y_prompt, y_sample, new_k_p, new_v_p, new_conv_p, new_lru_p, new_mk, new_mv,
            new_k_s, new_v_s, new_conv_s, new_lru_s)
```

```python
import os
from contextlib import ExitStack
import numpy as np
import concourse.bass as bass
import concourse.mybir as mybir
from concourse.bass_utils import run_bass_kernel_spmd

F32 = mybir.dt.float32
BF16 = mybir.dt.bfloat16
U8 = mybir.dt.uint8
AF = mybir.ActivationFunctionType
ALU = mybir.AluOpType

D = 1024
S = 2048
T = 2112
NCORES = 8
DFF = 2816
EPS = 1e-6
LAMBDA_INIT = 0.2
NEG = -30000.0


class Buf:
    __slots__ = ("name", "w", "r", "dsem", "dval")

    def __init__(self, name):
        self.name = name
        self.w = []
        self.r = []
        self.dsem = None
        self.dval = 0


class Sched:
    ENG = ("pe", "act", "dve", "pool", "sp")

    def __init__(self, nc, stack):
        self.nc = nc
        self.stack = stack
        self.e = {"pe": nc.tensor, "act": nc.scalar, "dve": nc.vector, "pool": nc.gpsimd, "sp": nc.sync}
        self.sem = {k: stack.enter_context(nc.semaphore("s_" + k)) for k in self.ENG}
        self.cnt = {k: 0 for k in self.ENG}
        self.pending = {k: False for k in self.ENG}
        self.seen = {k: {} for k in self.ENG}
        self.dma_sems = []
        self.all_dma = []

    def _wait(self, eng, tok):
        kind = tok[0]
        if kind == "e":
            _, src, idx = tok
            if src == eng and src == "pe":
                return
            key = src
            if self.seen[eng].get(key, 0) >= idx:
                return
            self.e[eng].wait_ge(self.sem[src], idx)
            self.seen[eng][key] = idx
        else:
            _, buf, val = tok
            key = ("d", id(buf))
            if self.seen[eng].get(key, 0) >= val:
                return
            self.e[eng].wait_ge(buf.dsem, val)
            self.seen[eng][key] = val

    def _deps(self, eng, r, w):
        toks = []
        for b in r:
            toks += b.w
        for b in w:
            toks += b.w + b.r
        for t in toks:
            self._wait(eng, t)

    def op(self, eng, fn, r=(), w=(), signal=True):
        self._deps(eng, r, w)
        inst = fn()
        if signal:
            self.cnt[eng] += 1
            inst.then_inc(self.sem[eng], 1)
            tok = ("e", eng, self.cnt[eng])
            self.pending[eng] = False
        else:
            tok = ("e", eng, self.cnt[eng] + 1)
            self.pending[eng] = True
        for b in r:
            b.r.append(tok)
        for b in w:
            b.w = [tok]
            b.r = []
        return inst

    def dma(self, q, pairs, r=(), w=(), **kw):
        self._deps(q, r, w)
        owner = w[0] if w else r[0]
        if owner.dsem is None:
            owner.dsem = self.stack.enter_context(self.nc.semaphore("d_" + owner.name))
        for (o, i) in pairs:
            self.e[q].dma_start(out=o, in_=i, **kw).then_inc(owner.dsem, 16)
            owner.dval += 16
        tok = ("d", owner, owner.dval)
        for b in r:
            b.r.append(tok)
        for b in w:
            b.w = [tok]
            b.r = []
        self.all_dma.append(tok)

    def alias(self, new, olds):
        for o in olds:
            new.r += o.w + o.r

    def finish(self):
        for tok in self.all_dma:
            self._wait("sp", tok)


def _rel_bucket_np(rel):
    n = np.abs(rel)
    nf = np.maximum(n, 1).astype(np.float32)
    large = 8 + (np.log(nf / np.float32(8)) / np.float32(np.log(16.0)) * np.float32(8)).astype(np.int32)
    large = np.minimum(large, 15)
    return np.where(rel > 0, 16, 0) + np.where(n < 8, n, large)


def _bias_onehot():
    j = np.arange(384)
    rel = j - 255
    b = _rel_bucket_np(rel)
    oh = np.zeros((32, 384), np.float32)
    oh[b, j] = 1.0
    oh[15, :] -= 1.0
    oh[:, 383] = 0.0
    return oh


def build_program(phase_limit=99):
    nc = bass.Bass("TRN2", target_bir_lowering=False)

    def din(name, shape):
        return nc.dram_tensor(name, list(shape), F32, kind="ExternalInput")

    def dout(name, shape):
        return nc.dram_tensor(name, list(shape), F32, kind="ExternalOutput")

    xp = din("xp", [S, D]).ap()
    xs = din("xs", [64, D]).ap()
    mem = din("mem", [256, D]).ap()
    ck = din("ck", [2, S, D]).ap()
    cv = din("cv", [2, S, D]).ap()
    sconv = din("sconv", [2, 3, D]).ap()
    slru = din("slru", [2, D]).ap()
    cmk = din("cmk", [2, 256, D]).ap()
    cmv = din("cmv", [2, 256, D]).ap()
    rel_table = din("rel_table", [32, 8]).ap()
    boh = din("boh", [32, 384]).ap()
    ident_in = din("ident", [128, 128]).ap()
    aident_in = din("aident", [128, 128]).ap()
    norm_mix = din("norm_mix", [1, D]).ap()
    w_in = din("w_in", [D, 6144]).ap()
    lq1 = din("lq1", [1, 64]).ap()
    lk1 = din("lk1", [1, 64]).ap()
    lq2 = din("lq2", [1, 64]).ap()
    lk2 = din("lk2", [1, 64]).ap()
    subln_g = din("subln_g", [1, 128]).ap()
    conv_w = din("conv_w", [4, D]).ap()
    conv_b = din("conv_b", [1, D]).ap()
    w_rg_a = din("w_rg_a", [8, 128, 128]).ap()
    b_rg_a = din("b_rg_a", [1, D]).ap()
    w_rg_x = din("w_rg_x", [8, 128, 128]).ap()
    b_rg_x = din("b_rg_x", [1, D]).ap()
    rg_lambda = din("rg_lambda", [1, D]).ap()
    norm_mem = din("norm_mem", [1, D]).ap()
    w_mem_kv = din("w_mem_kv", [D, 2048]).ap()
    w_proj_a = din("w_proj_a", [D, D]).ap()
    w_proj_b = din("w_proj_b", [D, D]).ap()
    w_proj_c = din("w_proj_c", [D, D]).ap()
    w_gate = din("w_gate", [D, 3072]).ap()
    b_gate = din("b_gate", [1, 3072]).ap()
    w_out = din("w_out", [D, D]).ap()
    norm_ffn = din("norm_ffn", [1, D]).ap()
    w_ffn_in = din("w_ffn_in", [D, 2 * DFF]).ap()
    w_ffn_out = din("w_ffn_out", [DFF, D]).ap()
    norm_final = din("norm_final", [1, D]).ap()

    yp = dout("yp", [S, D]).ap()
    ys = dout("ys", [64, D]).ap()
    nk = dout("nk", [S, D]).ap()
    nv = dout("out_v", [S, D]).ap()
    nconv = dout("nconv", [3, D]).ap()
    nlru = dout("nlru", [1, D]).ap()
    nmk = dout("nmk", [256, D]).ap()
    nmv = dout("nmv", [256, D]).ap()
    nks = dout("nks", [64, D]).ap()
    nvs = dout("out_vs", [64, D]).ap()
    nconvs = dout("nconvs", [2, 3, D]).ap()
    nlrus = dout("nlrus", [2, D]).ap()
    tsc_h = nc.dram_tensor("tsc", [8, 384], F32, kind="Internal")
    DEBUG = os.environ.get("MK_DEBUG", "")
    dbg = nc.dram_tensor("debug_out", [128, 16896], BF16, kind="ExternalOutput").ap() if DEBUG else None
    dbg2 = nc.dram_tensor("debug_x2", [2112, 1024], F32, kind="ExternalOutput").ap() if DEBUG else None

    stack = ExitStack()
    with stack:
        arena = stack.enter_context(nc.sbuf_tensor("arena", [128, 212800], U8))
        psum = [stack.enter_context(nc.psum_tensor("ps%d" % i, [128, 512], F32)) for i in range(8)]
        stack.enter_context(nc.Block())
        sc = Sched(nc, stack)
        E = sc.e

        def region(off, nbytes, dt, pattern=None, **kw):
            ap = arena[:, off:off + nbytes].bitcast(dt)
            if pattern:
                ap = ap.rearrange(pattern, **kw)
            return ap

        O_CONST = 0
        O_SLAB = 14336
        O_HT = O_SLAB + 3 * 16384
        O_R1 = O_HT + 33792
        O_A = O_R1 + 67072
        O_TMP = O_A + 33792
        TMP_SZ = 212800 - O_TMP
        assert TMP_SZ >= 14656, TMP_SZ

        psb = [Buf("psum%d" % i) for i in range(8)]

        def ps_f32(i):
            return psum[i][:]

        def ps_bf(i):
            return psum[i][:].bitcast(BF16)

        co = [O_CONST]

        def calloc(nbytes, dt, pattern=None, **kw):
            off = co[0]
            co[0] += (nbytes + 31) // 32 * 32
            assert co[0] <= O_SLAB
            return region(off, nbytes, dt, pattern, **kw)

        identb = calloc(256, BF16)
        identf = calloc(512, F32)
        bias_t = calloc(8 * 2 * 2 * 256, BF16, "p (h k s q) -> p h k s q", h=8, k=2, s=2)
        colv = calloc(64 * 4, F32)
        cB = Buf("consts")

        C_NLAM, C_GSUB, C_SP4, C_ONE = 0, 1, 2, 10
        C_BA, C_BX, C_CB, C_CW = 11, 19, 27, 35
        colv2 = calloc(64 * 4, F32)
        C2_BG = 0
        colv_b = Buf("colv")

        sc.dma("sp", [(identf, ident_in)], w=[cB])
        sc.op("dve", lambda: E["dve"].tensor_copy(out=identb, in_=identf), r=[cB], w=[cB])


        hT = region(O_HT, 33792, BF16, "p (c t) -> p c t", c=8)
        hT_b = [Buf("hT%d" % i) for i in range(18)]
        TILES = [(i * 128, 128) for i in range(16)] + [(2048, 32), (2080, 32)]

        def tile_src(tt):
            t0, rows = TILES[tt]
            if tt < 16:
                return xp[t0:t0 + rows, :]
            return xs[(t0 - 2048):(t0 - 2048) + rows, :]

        grow = region(O_A, 4096, F32)
        grow_b = Buf("grow")
        sc.dma("sp", [(grow, norm_mix.partition_broadcast(128))], w=[grow_b])

        xst = [region(O_R1 + i * 4096, 4096, F32) for i in range(3)]
        xst_b = [Buf("xst%d" % i) for i in range(3)]
        xn = [region(O_R1 + 12288 + i * 2048, 2048, BF16) for i in range(2)]
        xn_b = [Buf("xn%d" % i) for i in range(2)]
        junk = region(O_R1 + 16384, 2048, BF16)
        junk_b = Buf("junk")
        ssb = region(O_R1 + 18432, 18 * 4 * 3, F32, "p (k t) -> p k t", k=3)
        ss_b = Buf("ss")

        def rms_tile(src_ap, rows, st, st_b, ssi, g_ap, out_bf, out_b):
            sc.op("act", lambda: E["act"].activation(out=junk[:rows], in_=st[:rows], func=AF.Square,
                                                    accum_out=ssb[:rows, 0, ssi:ssi + 1]),
                  r=[st_b], w=[junk_b, ss_b])
            sc.op("act", lambda: E["act"].activation(out=ssb[:rows, 1, ssi:ssi + 1], in_=ssb[:rows, 0, ssi:ssi + 1],
                                                    func=AF.Ln, scale=1.0 / D, bias=EPS), r=[ss_b], w=[ss_b])
            sc.op("act", lambda: E["act"].activation(out=ssb[:rows, 2, ssi:ssi + 1], in_=ssb[:rows, 1, ssi:ssi + 1],
                                                    func=AF.Exp, scale=-0.5), r=[ss_b], w=[ss_b])
            sc.op("dve", lambda: E["dve"].scalar_tensor_tensor(
                out=out_bf[:rows], in0=st[:rows], scalar=ssb[:rows, 2, ssi:ssi + 1], in1=g_ap[:rows],
                op0=ALU.mult, op1=ALU.mult), r=[st_b, ss_b, grow_b], w=[out_b])

        def transpose_to_T(src_bf, src_b, rows, dstT, dst_b, t0, pbank, evac_eng):
            pv = ps_bf(pbank).rearrange("p (c t) -> p c t", c=8)
            for c in range(8):
                sc.op("pe", lambda c=c: E["pe"].transpose(out=pv[:, c, :rows], in_=src_bf[:rows, c * 128:(c + 1) * 128],
                                                          identity=identb[:rows, :rows]),
                      r=[src_b, cB], w=[psb[pbank]], signal=(c == 7))
            if evac_eng == "act":
                sc.op("act", lambda: E["act"].copy(out=dstT[:, :, t0:t0 + rows], in_=pv[:, :, :rows]),
                      r=[psb[pbank]], w=[dst_b])
            else:
                sc.op("dve", lambda: E["dve"].tensor_copy(out=dstT[:, :, t0:t0 + rows], in_=pv[:, :, :rows]),
                      r=[psb[pbank]], w=[dst_b])

        for tt in range(18):
            t0, rows = TILES[tt]
            st, st_b = xst[tt % 3], xst_b[tt % 3]
            sc.dma("sp", [(st[:rows], tile_src(tt))], w=[st_b])
            rms_tile(None, rows, st, st_b, tt, grow, xn[tt % 2], xn_b[tt % 2])
            transpose_to_T(xn[tt % 2], xn_b[tt % 2], rows, hT, hT_b[tt], t0, tt % 2, "act")

        slab = [region(O_SLAB + i * 16384, 16384, BF16, "p (c n) -> p c n", c=8) for i in range(2)]
        slab_b = [Buf("slab%d" % i) for i in range(2)]
        slab_i = [0]
        O_TMP2 = O_SLAB + 2 * 16384

        def load_slab(w_ap, c0, ncols, kc0=0, nkc=8):
            i = slab_i[0] % 2
            slab_i[0] += 1
            src = w_ap[kc0 * 128:(kc0 + nkc) * 128, c0:c0 + ncols].rearrange("(c p) n -> p c n", p=128)
            pairs = []
            step = 2
            for k0 in range(0, nkc, step):
                k1 = min(nkc, k0 + step)
                pairs.append((slab[i][:, k0:k1, 0:ncols], src[:, k0:k1, :]))
            sc.dma("pool", pairs, w=[slab_b[i]])
            return slab[i], slab_b[i]

        stg = [region(O_A + 4096 + i * 4096, 4096, F32) for i in range(4)]
        stg_b = [Buf("stg%d" % i) for i in range(4)]
        stg_i = [0]

        def kv_phase(srcT, srcT_b, tiles, wk_sl, wv_sl, krows, vrows, kT_dst, kT_dst_b, vdst, vdst_b, vh, ve):
            for which, (wsl, wsl_b) in enumerate((wk_sl, wv_sl)):
                for tt, (t0, rows) in enumerate(tiles):
                    sg, sg_b = stg[stg_i[0] % 4], stg_b[stg_i[0] % 4]
                    stg_i[0] += 1
                    for half in range(2):
                        pb = (2 * tt + half) % 4
                        for kc in range(8):
                            sc.op("pe", lambda kc=kc, half=half, pb=pb: E["pe"].matmul(
                                ps_f32(pb)[:rows, :], lhsT=srcT[:, kc, t0:t0 + rows], rhs=wsl[:, kc, half * 512:(half + 1) * 512],
                                start=(kc == 0), stop=(kc == 7)),
                                r=[srcT_b[tt], wsl_b], w=[psb[pb]], signal=(kc == 7))
                        sc.op("act", lambda half=half, pb=pb: E["act"].copy(
                            out=sg[:rows, half * 512:(half + 1) * 512], in_=ps_f32(pb)[:rows, :]),
                            r=[psb[pb]], w=[sg_b])
                        if which == 1:
                            for hv in range(vh):
                                sc.op("dve", lambda half=half, hv=hv: E["dve"].tensor_copy(
                                    out=vdst(tt)[:rows, half * vh + hv, 0:ve],
                                    in_=sg[:rows, half * 512 + hv * ve:half * 512 + (hv + 1) * ve]),
                                    r=[sg_b], w=[vdst_b[tt]])
                    if which == 0:
                        sc.dma("sp", [(krows(tt), sg[:rows])], r=[sg_b])
                        for half in range(2):
                            pb = 4 + (2 * tt + half) % 4
                            pv = ps_f32(pb).rearrange("p (c t) -> p c t", c=4)
                            for c in range(4):
                                hh = half * 4 + c
                                sc.op("pe", lambda c=c, hh=hh, pv=pv: E["pe"].transpose(
                                    out=pv[:, c, :rows], in_=sg[:rows, hh * 128:(hh + 1) * 128], identity=identf[:rows, :rows]),
                                    r=[sg_b, cB], w=[psb[pb]], signal=(c == 3))
                            sc.op("dve", lambda half=half, pv=pv: E["dve"].tensor_copy(
                                out=kT_dst[:, half * 4:(half + 1) * 4, t0:t0 + rows], in_=pv[:, :, :rows]),
                                r=[psb[pb]], w=[kT_dst_b[tt]])
                    else:
                        sc.dma("sp", [(vrows(tt), sg[:rows])], r=[sg_b])

        kT = region(O_R1, 33792, BF16, "p (c t) -> p c t", c=8)
        kT_b = [Buf("kT%d" % i) for i in range(18)]
        v_aug = region(O_R1 + 33792, 16 * 8 * 130 * 2, BF16, "p (t h e) -> p t h e", t=16, h=8)
        sv_aug = calloc(2 * 8 * 130 * 2, BF16, "p (t h e) -> p t h e", t=2, h=8)
        v_b = [Buf("v%d" % i) for i in range(18)]
        for b_ in kT_b + v_b:
            sc.alias(b_, xst_b + xn_b + [junk_b, ss_b])
        sc.op("pool", lambda: E["pool"].memset(v_aug[:, :, :, 128:129], 1.0), w=v_b[:16])
        sc.op("pool", lambda: E["pool"].memset(sv_aug[:, :, :, 128:129], 1.0), w=v_b[16:])

        def out_rows(dst_p, dst_s, tt):
            t0, rows = TILES[tt]
            if tt < 16:
                return dst_p[t0:t0 + rows, :]
            return dst_s[t0 - 2048:t0 - 2048 + rows, :]

        wk_sl = load_slab(w_in, 1024, 1024)
        wv_sl = load_slab(w_in, 2048, 1024)
        kv_phase(hT, hT_b, TILES, wk_sl, wv_sl,
                 lambda tt: out_rows(nk, nks, tt), lambda tt: out_rows(nv, nvs, tt),
                 kT, kT_b, lambda tt: (v_aug[:, tt] if tt < 16 else sv_aug[:, tt - 16]), v_b, 4, 128)

        if phase_limit == 210:
            sc.finish()
            return nc
        scr = region(O_TMP2, 16384, F32)
        scr_b = Buf("scr")
        lam4 = scr[:, 0:256].rearrange("p (k d) -> p k d", k=4)
        sc.dma("sp", [(lam4[:, 0, :], lq1.partition_broadcast(128)), (lam4[:, 1, :], lk1.partition_broadcast(128)),
                      (lam4[:, 2, :], lq2.partition_broadcast(128)), (lam4[:, 3, :], lk2.partition_broadcast(128))],
               w=[scr_b])
        lt = scr[:, 256:512]
        sc.op("dve", lambda: E["dve"].tensor_tensor(out=lt[:, 0:64], in0=lam4[:, 0, :], in1=lam4[:, 1, :], op=ALU.mult), r=[scr_b], w=[scr_b])
        sc.op("dve", lambda: E["dve"].tensor_tensor(out=lt[:, 64:128], in0=lam4[:, 2, :], in1=lam4[:, 3, :], op=ALU.mult), r=[scr_b], w=[scr_b])
        sc.op("dve", lambda: E["dve"].reduce_sum(out=lt[:, 128:130], in_=lt[:, 0:128].rearrange("p (k d) -> p k d", k=2),
                                                 axis=mybir.AxisListType.X), r=[scr_b], w=[scr_b])
        sc.op("act", lambda: E["act"].activation(out=lt[:, 130:132], in_=lt[:, 128:130], func=AF.Exp), r=[scr_b], w=[scr_b])
        sc.op("dve", lambda: E["dve"].scalar_tensor_tensor(out=colv[:, C_NLAM:C_NLAM + 1], in0=lt[:, 131:132], scalar=-LAMBDA_INIT,
                                                           in1=lt[:, 130:131], op0=ALU.add, op1=ALU.subtract), r=[scr_b], w=[colv_b])
        with nc.allow_non_contiguous_dma(reason="tiny per-channel columns"):
            sc.dma("sp", [(lt[:, 132:133], subln_g.rearrange("o e -> e o"))], w=[scr_b])
        sc.op("dve", lambda: E["dve"].tensor_scalar(colv[:, C_GSUB:C_GSUB + 1], lt[:, 132:133], 1.0 - LAMBDA_INIT, None, op0=ALU.mult),
              r=[scr_b], w=[colv_b])

        if phase_limit != 200:
            rt = scr[0:32, 512:520]
            bo = scr[0:32, 1024:1408]
            sc.dma("sp", [(rt, rel_table), (bo, boh)], w=[scr_b])
            sc.op("pe", lambda: E["pe"].matmul(ps_f32(7)[0:8, 0:384], lhsT=rt, rhs=bo, start=True, stop=True), r=[scr_b], w=[psb[7]])
            tms = scr[0:8, 1536:1920]
            sc.op("dve", lambda: E["dve"].tensor_copy(out=tms, in_=ps_f32(7)[0:8, 0:384]), r=[psb[7]], w=[scr_b])
            tsc_b = Buf("tsc")
            sc.dma("sp", [(tsc_h.ap(), tms)], r=[scr_b], w=[tsc_b])
            if phase_limit == 201:
                sc.finish()
                return nc
            btf = scr[:, 2048:4096].rearrange("p (h k q) -> p h k q", h=8, k=2)
            ghk = region(O_A + 20480, 8192, F32, "p (h k q) -> p h k q", h=8, k=2)
            aid = region(O_A + 28672, 512, F32)
            ghk_b = Buf("ghk")
            pairs = [(aid, aident_in)]
            for h in range(8):
                for k_, base in ((0, 128), (1, 0)):
                    pairs.append((ghk[:, h, k_, :], bass.AP(tsc_h, h * 384 + base, [[1, 128], [1, 128]])))
            sc.dma("sp", pairs, r=[tsc_b], w=[ghk_b])
            for h in range(8):
                for k_ in range(2):
                    sc.op("pe", lambda h=h, k_=k_: E["pe"].matmul(ps_f32(7)[:, k_ * 128:(k_ + 1) * 128], lhsT=ghk[:, h, k_, :], rhs=aid,
                                                               start=True, stop=True), r=[ghk_b], w=[psb[7]])
                sc.op("dve", lambda h=h: E["dve"].tensor_copy(out=btf[:, h, :, :], in_=ps_f32(7)[:, 0:256].rearrange("p (k q) -> p k q", k=2)),
                      r=[psb[7]], w=[scr_b])
            sc.op("pool", lambda: E["pool"].memset(btf[64:128, :, 0, 0:64], NEG), r=[scr_b], w=[scr_b])
            sc.op("dve", lambda: E["dve"].tensor_copy(out=bias_t[:, :, :, 0, :], in_=btf), r=[scr_b], w=[cB])
            sc.op("dve", lambda: E["dve"].tensor_tensor(out=btf, in0=btf, in1=bias_t[:, :, :, 0, :], op=ALU.subtract), r=[scr_b, cB], w=[scr_b])
            sc.op("dve", lambda: E["dve"].tensor_copy(out=bias_t[:, :, :, 1, :], in_=btf), r=[scr_b], w=[cB])

        if phase_limit >= 3:
            a_outT = region(O_A, 33792, BF16, "p (c t) -> p c t", c=8)
            aT_b = [Buf("aT%d" % i) for i in range(8)]
            for b_ in aT_b:
                sc.alias(b_, stg_b + [grow_b])
            qT = [region(O_TMP + i * 4224, 4224, BF16) for i in range(2)]
            qT_b = [Buf("qT%d" % i) for i in range(2)]
            PT = [region(O_TMP + 8448 + i * 1024, 1024, BF16) for i in range(4)]
            PT_b = [Buf("PT%d" % i) for i in range(4)]
            qTs = region(O_TMP + 12544, 1024, BF16, "p (h t) -> p h t", h=8)
            qTs_b = Buf("qTs")
            accS = region(O_TMP2, 2 * 4 * 129 * 4, F32, "p (c j e) -> p c j e", c=2, j=4)
            accS2 = region(O_TMP2, 2 * 4 * 129 * 4, F32, "p (c x) -> p c x", c=2)
            t0s = region(O_TMP2 + 4160, 2048, F32, "p (j e) -> p j e", j=4)
            t1s = region(O_TMP2 + 6208, 2048, F32, "p (j e) -> p j e", j=4)
            tns = region(O_TMP2 + 8256, 1024, BF16, "p (j e) -> p j e", j=4)
            rec = region(O_TMP2 + 9280, 32, F32, "p (c j) -> p c j", c=2)
            ssq = region(O_TMP2 + 9312, 48, F32, "p (k j) -> p k j", k=3)
            junk2 = region(O_TMP2 + 9376, 256, BF16)
            ep_b = Buf("ep")
            sc.alias(ep_b, [scr_b])
            BQ = [(0, 512), (512, 512), (1024, 512), (1536, 512), (2048, 64)]
            wq_sl, wq_b = load_slab(w_in, 0, 1024)

            def q_proj(h):
                slot = h % 2
                for bi, (q0, n) in enumerate(BQ):
                    for kc in range(8):
                        sc.op("pe", lambda kc=kc: E["pe"].matmul(
                            ps_f32(7)[:, :n], lhsT=wq_sl[:, kc, h * 128:(h + 1) * 128], rhs=hT[:, kc, q0:q0 + n],
                            start=(kc == 0), stop=(kc == 7)), r=[wq_b] + hT_b, w=[psb[7]], signal=(kc == 7))
                    if bi < 4:
                        sc.op("act", lambda: E["act"].activation(out=qT[slot][:, q0:q0 + n], in_=ps_f32(7)[:, :n],
                                                                func=AF.Copy, scale=0.125), r=[psb[7]], w=[qT_b[slot]])
                    else:
                        sc.op("act", lambda: E["act"].activation(out=qTs[:, h, :], in_=ps_f32(7)[:, :n],
                                                                func=AF.Copy, scale=0.125), r=[psb[7]], w=[qTs_b])

            def epilogue(h, acc_list, nr, nj, dst_cols, defer=None):
                for (ap_, bb, c, j0, n) in acc_list:
                    sc.op("dve", lambda ap_=ap_, c=c, j0=j0, n=n: E["dve"].tensor_copy(
                        out=accS2[:nr, c, j0 * 129:(j0 + n) * 129], in_=ap_), r=[bb], w=[ep_b])
                sc.op("dve", lambda: E["dve"].reciprocal(out=rec[:nr, :, :nj], in_=accS[:nr, :, :nj, 128]), r=[ep_b], w=[ep_b])
                sc.op("dve", lambda: E["dve"].tensor_scalar(rec[:nr, 1, :nj], rec[:nr, 1, :nj], colv[:nr, C_NLAM:C_NLAM + 1], None,
                                                            op0=ALU.mult), r=[ep_b, colv_b], w=[ep_b])
                sc.op("dve", lambda: E["dve"].tensor_tensor(out=t0s[:nr, :nj, :], in0=accS[:nr, 0, :nj, 0:128],
                                                            in1=rec[:nr, 0, :nj].unsqueeze(2).to_broadcast([nr, nj, 128]), op=ALU.mult),
                      r=[ep_b], w=[ep_b])
                sc.op("dve", lambda: E["dve"].tensor_tensor(out=t1s[:nr, :nj, :], in0=accS[:nr, 1, :nj, 0:128],
                                                            in1=rec[:nr, 1, :nj].unsqueeze(2).to_broadcast([nr, nj, 128]), op=ALU.mult),
                      r=[ep_b], w=[ep_b])
                sc.op("dve", lambda: E["dve"].tensor_tensor(out=t0s[:nr, :nj, :], in0=t0s[:nr, :nj, :], in1=t1s[:nr, :nj, :], op=ALU.add),
                      r=[ep_b], w=[ep_b])
                for j in range(nj):
                    sc.op("act", lambda j=j: E["act"].activation(out=junk2[:nr, :], in_=t0s[:nr, j, :], func=AF.Square,
                                                                accum_out=ssq[:nr, 0, j:j + 1]), r=[ep_b], w=[ep_b])
                sc.op("act", lambda: E["act"].activation(out=ssq[:nr, 1, :nj], in_=ssq[:nr, 0, :nj], func=AF.Ln, scale=1.0 / 128, bias=EPS),
                      r=[ep_b], w=[ep_b])
                sc.op("act", lambda: E["act"].activation(out=ssq[:nr, 2, :nj], in_=ssq[:nr, 1, :nj], func=AF.Exp, scale=-0.5),
                      r=[ep_b], w=[ep_b])
                sc.op("dve", lambda: E["dve"].tensor_tensor(out=tns[:nr, :nj, :], in0=t0s[:nr, :nj, :],
                                                            in1=ssq[:nr, 2, :nj].unsqueeze(2).to_broadcast([nr, nj, 128]), op=ALU.mult),
                      r=[ep_b], w=[ep_b])
                def part2():
                    pv = ps_bf(7).rearrange("p (j t) -> p j t", j=8)
                    for j in range(nj):
                        sc.op("pe", lambda j=j: E["pe"].transpose(out=pv[:, j, :nr], in_=tns[:nr, j, :], identity=identb[:nr, :nr]),
                              r=[ep_b, cB], w=[psb[7]], signal=(j == nj - 1))
                    for j in range(nj):
                        c0 = dst_cols(j)
                        sc.op("dve", lambda j=j, c0=c0: E["dve"].tensor_scalar(a_outT[:, h, c0:c0 + nr], pv[:, j, :nr],
                                                                              colv[:, C_GSUB:C_GSUB + 1], None, op0=ALU.mult),
                              r=[psb[7], colv_b], w=[aT_b[h]])
                if defer is None:
                    part2()
                else:
                    defer.append(part2)

            def attn_prompt(h):
                slot = h % 2
                steps = []
                for qb in range(4):
                    last_kt = 4 * qb + 3
                    for kt in range(0, last_kt + 1):
                        j0 = max(0, kt - 4 * qb)
                        for c in range(2):
                            near = []
                            for j in range(j0, 4):
                                d = (4 * qb + j) - kt
                                if d == 0:
                                    near.append((j, 0))
                                elif d == 1:
                                    near.append((j, 1))
                            steps.append(dict(qb=qb, kt=kt, c=c, j0=j0, ncols=512 - 128 * j0, qs=qb * 512 + 128 * j0, near=near,
                                              sb=len(steps) % 4, last=(kt == last_kt and c == 1)))
                started = {}

                def emit_S(st):
                    sb, c, kt, j0, ncols, qs, near = st["sb"], st["c"], st["kt"], st["j0"], st["ncols"], st["qs"], st["near"]
                    sc.op("pe", lambda: E["pe"].matmul(
                        ps_f32(sb)[:, :ncols], lhsT=kT[c * 64:(c + 1) * 64, h, kt * 128:(kt + 1) * 128],
                        rhs=qT[slot][c * 64:(c + 1) * 64, qs:qs + ncols], start=True, stop=True),
                        r=[kT_b[kt], qT_b[slot]], w=[psb[sb]], signal=(len(near) == 0))
                    for ni, (j, kind) in enumerate(near):
                        for hl in range(2):
                            lastb = (ni == len(near) - 1 and hl == 1)
                            sc.op("pe", lambda j=j, kind=kind, hl=hl, lastb=lastb: E["pe"].matmul(
                                ps_f32(sb)[:, (j - j0) * 128:(j - j0 + 1) * 128], lhsT=identb, rhs=bias_t[:, h, kind, hl, :],
                                start=False, stop=True, skip_group_check=True), r=[cB], w=[psb[sb]], signal=lastb)
                    sc.op("act", lambda: E["act"].activation(out=PT[sb][:, :ncols], in_=ps_f32(sb)[:, :ncols], func=AF.Exp),
                          r=[psb[sb]], w=[PT_b[sb]])

                def emit_PV(st):
                    sb, c, kt, j0, qb = st["sb"], st["c"], st["kt"], st["j0"], st["qb"]
                    stt = started.setdefault(qb, set())
                    for j in range(j0, 4):
                        if j < 3:
                            bank, col = 4 + c, j * 129
                        else:
                            bank, col = 6, c * 129
                        first = bank not in stt
                        stt.add(bank)
                        fin = (kt == 4 * qb + j)
                        sc.op("pe", lambda j=j, bank=bank, col=col, first=first, fin=fin: E["pe"].matmul(
                            ps_f32(bank)[:, col:col + 129], lhsT=PT[sb][:, (j - j0) * 128:(j - j0 + 1) * 128],
                            rhs=v_aug[:, kt, h, 0:129], start=first, stop=fin, skip_group_check=True),
                            r=[PT_b[sb], v_b[kt]], w=[psb[bank]], signal=(j == 3))
                    if st["last"]:
                        acc_list = [(ps_f32(4)[:, 0:387], psb[4], 0, 0, 3), (ps_f32(5)[:, 0:387], psb[5], 1, 0, 3),
                                    (ps_f32(6)[:, 0:129], psb[6], 0, 3, 1), (ps_f32(6)[:, 129:258], psb[6], 1, 3, 1)]
                        epilogue(h, acc_list, 128, 4, lambda j, qb=qb: qb * 512 + j * 128, defer=pending)
                        pend_at[0] = cur_i[0] + 6

                LA = 2
                pending = []
                pend_at = [None]
                cur_i = [0]
                for i in range(0, len(steps) + LA, 2):
                    cur_i[0] = i
                    for k2 in (i, i + 1):
                        if k2 < len(steps):
                            emit_S(steps[k2])
                    if pending and pend_at[0] is not None and i >= pend_at[0]:
                        pending.pop(0)()
                        pend_at[0] = None
                    for k2 in (i - LA, i - LA + 1):
                        if 0 <= k2 < len(steps):
                            emit_PV(steps[k2])
                while pending:
                    pending.pop(0)()

            q_proj(0)
            for h in range(8):
                if h + 1 < 8:
                    q_proj(h + 1)
                attn_prompt(h)

            KcT = region(O_R1, 32768, BF16, "p (h t) -> p h t", h=8)
            KcT_b = Buf("KcT")
            Vc = region(O_R1 + 33792, 16 * 8 * 130 * 2, BF16, "p (t h e) -> p t h e", t=16, h=8)
            Vc_b = Buf("Vc")
            kTs = region(O_TMP + 13568, 1024, BF16, "p (h t) -> p h t", h=8)
            kTs_b = Buf("kTs")
            sc.op("dve", lambda: E["dve"].tensor_copy(out=kTs, in_=kT[:, :, 2048:2112]), r=kT_b[16:18], w=[kTs_b])
            sc.alias(KcT_b, kT_b)
            sc.alias(Vc_b, v_b[:16])
            kst = [region(O_TMP + i * 2048, 2048, BF16) for i in range(2)]
            kst_b = [Buf("kst%d" % i) for i in range(2)]
            for b_ in kst_b:
                sc.alias(b_, qT_b)
            for s_ in range(2):
                for kt in range(16):
                    sc.dma("pool", [(Vc[:, kt, :, 0:128], cv[s_, kt * 128:(kt + 1) * 128, :].rearrange("p (h e) -> p h e", h=8))],
                           w=[Vc_b])
                for kt in range(16):
                    ks, ks_b = kst[kt % 2], kst_b[kt % 2]
                    sc.dma("pool", [(ks, ck[s_, kt * 128:(kt + 1) * 128, :])], w=[ks_b])
                    pb = kt % 2
                    pv = ps_bf(pb).rearrange("p (c t) -> p c t", c=8)
                    for c in range(8):
                        sc.op("pe", lambda c=c, pv=pv: E["pe"].transpose(out=pv[:, c, :], in_=ks[:, c * 128:(c + 1) * 128], identity=identb),
                              r=[ks_b, cB], w=[psb[pb]], signal=(c == 7))
                    sc.op("dve", lambda pv=pv: E["dve"].tensor_copy(out=KcT[:, :, kt * 128:(kt + 1) * 128], in_=pv),
                          r=[psb[pb]], w=[KcT_b])
                tnew = 16 + s_
                c0n = 2048 + 32 * s_
                def s_stage(h, c):
                    sb = 2 + c
                    qsl = qTs[c * 64:(c + 1) * 64, h, 32 * s_:32 * s_ + 32]
                    for kt in range(16):
                        sc.op("pe", lambda kt=kt: E["pe"].matmul(
                            ps_f32(sb)[:, kt * 32:(kt + 1) * 32], lhsT=KcT[c * 64:(c + 1) * 64, h, kt * 128:(kt + 1) * 128], rhs=qsl,
                            start=(kt == 0), stop=True, skip_group_check=True), r=[KcT_b, qTs_b], w=[psb[sb]], signal=False)
                    for hl in range(2):
                        sc.op("pe", lambda hl=hl: E["pe"].matmul(
                            ps_f32(sb)[:, 480:512], lhsT=identb, rhs=bias_t[:, h, 1, hl, 0:32], start=False, stop=True,
                            skip_group_check=True), r=[cB], w=[psb[sb]], signal=(hl == 1))
                    nb = 6
                    ncol = ((h % 2) * 2 + c) * 32
                    sc.op("pe", lambda: E["pe"].matmul(
                        ps_f32(nb)[0:32, ncol:ncol + 32], lhsT=kTs[c * 64:(c + 1) * 64, h, 32 * s_:32 * s_ + 32], rhs=qsl,
                        start=True, stop=True, skip_group_check=True), r=[kTs_b, qTs_b], w=[psb[nb]], signal=False)
                    for hl in range(2):
                        sc.op("pe", lambda hl=hl: E["pe"].matmul(
                            ps_f32(nb)[0:32, ncol:ncol + 32], lhsT=identb[0:32, 0:32], rhs=bias_t[0:32, h, 0, hl, 0:32],
                            start=False, stop=True, skip_group_check=True), r=[cB], w=[psb[nb]], signal=(hl == 1))
                    sc.op("act", lambda: E["act"].activation(out=PT[sb][:, :512], in_=ps_f32(sb)[:, :512], func=AF.Exp),
                          r=[psb[sb]], w=[PT_b[sb]])
                    sc.op("act", lambda: E["act"].activation(out=PT[c][0:32, 0:32], in_=ps_f32(nb)[0:32, ncol:ncol + 32], func=AF.Exp),
                          r=[psb[nb]], w=[PT_b[c]])

                def pv_stage(h, c):
                    sb = 2 + c
                    for kt in range(16):
                        sc.op("pe", lambda kt=kt: E["pe"].matmul(
                            ps_f32(4 + c)[0:32, 0:129], lhsT=PT[sb][:, kt * 32:(kt + 1) * 32], rhs=Vc[:, kt, h, 0:129],
                            start=(kt == 0), stop=False), r=[PT_b[sb], Vc_b], w=[psb[4 + c]], signal=False)
                    sc.op("pe", lambda: E["pe"].matmul(
                        ps_f32(4 + c)[0:32, 0:129], lhsT=PT[c][0:32, 0:32], rhs=sv_aug[0:32, s_, h, 0:129],
                        start=False, stop=True), r=[PT_b[c], v_b[tnew]], w=[psb[4 + c]], signal=True)
                    if c == 1:
                        acc_list = [(ps_f32(4)[0:32, 0:129], psb[4], 0, 0, 1), (ps_f32(5)[0:32, 0:129], psb[5], 1, 0, 1)]
                        epilogue(h, acc_list, 32, 1, lambda j, c0n=c0n: c0n)

                hc = [(h, c) for h in range(8) for c in range(2)]
                s_stage(*hc[0])
                for i in range(len(hc)):
                    if i + 1 < len(hc):
                        s_stage(*hc[i + 1])
                    pv_stage(*hc[i])

        def load_w(dst, dst_b, w_ap, c0, ncols, kc0=0, nkc=8):
            src = w_ap[kc0 * 128:(kc0 + nkc) * 128, c0:c0 + ncols].rearrange("(c p) n -> p c n", p=128)
            pairs = []
            for k0 in range(0, nkc, 2):
                k1 = min(nkc, k0 + 2)
                pairs.append((dst[:, k0:k1, 0:ncols], src[:, k0:k1, :]))
            sc.dma("pool", pairs, w=[dst_b])

        if phase_limit >= 4:
            x2 = region(O_R1, 65536, F32, "p (t d) -> p t d", t=16)
            x2s = region(O_TMP + 10560, 4096, F32)
            x2_b = [Buf("x2_%d" % i) for i in range(17)]
            for b_ in x2_b[:16]:
                sc.alias(b_, [KcT_b, Vc_b, kTs_b] + kT_b + v_b)
            sc.alias(x2_b[16], [qTs_b, kTs_b] + PT_b + kst_b + qT_b)
            for tt in range(16):
                sc.dma("sp", [(x2[:, tt, :], xp[tt * 128:(tt + 1) * 128, :])], w=[x2_b[tt]])
            sc.dma("sp", [(x2s[0:64, :], xs[0:64, :])], w=[x2_b[16]])
            with nc.allow_non_contiguous_dma(reason="tiny per-channel columns"):
                sc.dma("sp", [(colv2[:, C2_BG:C2_BG + 24], b_gate.rearrange("o (c p) -> p (o c)", p=128))], w=[colv_b])
            wo = region(O_TMP2, 16384, BF16, "p (c n) -> p c n", c=8)
            wo_b = Buf("wo")
            sc.alias(wo_b, [ep_b, scr_b])
            mT = region(O_TMP, 8192, BF16, "p (c t) -> p c t", c=8)
            mT_b = Buf("mT")
            gs = region(O_TMP + 8192, 2048, F32)
            gs_b = Buf("gs")
            sc.alias(mT_b, qT_b + kst_b + PT_b)
            sc.alias(gs_b, qT_b + kst_b + PT_b)

            def merge_branch(bi, srcT, srcT_bufs, wproj_ap):
                wp, wp_b = load_slab(wproj_ap, 0, 1024)
                wg, wg_b = load_slab(w_gate, bi * 1024, 1024)
                load_w(wo, wo_b, w_out, 0, 1024)
                cnt = [0]
                for (q0, n) in BQ:
                    for m in range(8):
                        pa, pg = cnt[0] % 2, 2 + cnt[0] % 2
                        cnt[0] += 1
                        for kc in range(8):
                            sc.op("pe", lambda kc=kc: E["pe"].matmul(ps_f32(pa)[:, :n], lhsT=wp[:, kc, m * 128:(m + 1) * 128],
                                                                    rhs=srcT[:, kc, q0:q0 + n], start=(kc == 0), stop=(kc == 7)),
                                  r=[wp_b] + srcT_bufs, w=[psb[pa]], signal=(kc == 7))
                        for kc in range(8):
                            sc.op("pe", lambda kc=kc: E["pe"].matmul(ps_f32(pg)[:, :n], lhsT=wg[:, kc, m * 128:(m + 1) * 128],
                                                                    rhs=hT[:, kc, q0:q0 + n], start=(kc == 0), stop=(kc == 7)),
                                  r=[wg_b] + hT_b, w=[psb[pg]], signal=(kc == 7))
                        sc.op("act", lambda: E["act"].activation(out=gs[:, :n], in_=ps_f32(pg)[:, :n], func=AF.Sigmoid,
                                                                bias=colv2[:, C2_BG + bi * 8 + m:C2_BG + bi * 8 + m + 1]),
                              r=[psb[pg], colv_b], w=[gs_b])
                        sc.op("dve", lambda: E["dve"].tensor_tensor(out=mT[:, m, :n], in0=gs[:, :n], in1=ps_f32(pa)[:, :n], op=ALU.mult),
                              r=[gs_b, psb[pa]], w=[mT_b])
                    ntile = (n + 127) // 128
                    for ti in range(ntile):
                        rows = min(128, n - ti * 128)
                        if q0 < 2048:
                            tt = q0 // 128 + ti
                            xt_ = x2[:, tt, :]
                        else:
                            tt = 16
                            xt_ = x2s
                        for half in range(2):
                            pb = 4 + (2 * ti + half) % 4
                            for kc in range(8):
                                sc.op("pe", lambda kc=kc: E["pe"].matmul(
                                    ps_f32(pb)[:rows, :], lhsT=mT[:, kc, ti * 128:ti * 128 + rows], rhs=wo[:, kc, half * 512:(half + 1) * 512],
                                    start=(kc == 0), stop=(kc == 7)), r=[mT_b, wo_b], w=[psb[pb]], signal=(kc == 7))
                            sc.op("dve", lambda: E["dve"].tensor_tensor(
                                out=xt_[:rows, half * 512:(half + 1) * 512], in0=xt_[:rows, half * 512:(half + 1) * 512],
                                in1=ps_f32(pb)[:rows, :], op=ALU.add), r=[psb[pb]], w=[x2_b[tt]])

            merge_branch(0, a_outT, aT_b, w_proj_a)

        if phase_limit >= 5:
            c_outT = region(O_A, 33792, BF16, "p (c t) -> p c t", c=8)
            cT_b = [Buf("cT%d" % i) for i in range(4)]
            for b_ in cT_b + stg_b:
                sc.alias(b_, aT_b)
            grow2 = region(O_A + 20480, 4096, F32)
            grow2_b = Buf("grow2")
            mst = region(O_A + 24576, 4096, F32)
            mst_b = Buf("mst")
            mxn = region(O_A + 28672, 2048, BF16)
            mxn_b = Buf("mxn")
            junkm = region(O_A + 30720, 2048, BF16)
            ssm = region(O_A + 32768, 3 * 4 * 4, F32, "p (k t) -> p k t", k=3)
            mjs_b = Buf("mjs")
            for b_ in (grow2_b, mst_b, mxn_b, mjs_b):
                sc.alias(b_, aT_b)
            memT = region(O_TMP2, 4096, BF16, "p (c t) -> p c t", c=8)
            memT_b = [Buf("memT%d" % i) for i in range(2)]
            for b_ in memT_b:
                sc.alias(b_, [wo_b])
            memkT = region(O_TMP, 4096, BF16, "p (c t) -> p c t", c=8)
            memkT_b = [Buf("memkT%d" % i) for i in range(2)]
            memv = region(O_TMP + 4096, 4128, BF16, "p (t h e) -> p t h e", t=2, h=4)
            memv_b = [Buf("memv%d" % i) for i in range(2)]
            qcTs = region(O_TMP + 8224, 1024, BF16, "p (h c t) -> p h c t", h=4, c=2)
            qcTs_b = Buf("qcTs")
            for b_ in memkT_b + memv_b + [qcTs_b]:
                sc.alias(b_, [mT_b, gs_b])
            sc.dma("sp", [(grow2, norm_mem.partition_broadcast(128))], w=[grow2_b])
            sc.op("pool", lambda: E["pool"].memset(memv[:, :, :, 256:257], 1.0), w=memv_b)
            MT = [(0, 128), (128, 128)]
            for mt in range(2):
                sc.dma("sp", [(mst, mem[mt * 128:(mt + 1) * 128, :])], w=[mst_b])
                sc.op("act", lambda: E["act"].activation(out=junkm, in_=mst, func=AF.Square, accum_out=ssm[:, 0, mt:mt + 1]),
                      r=[mst_b], w=[mjs_b])
                sc.op("act", lambda: E["act"].activation(out=ssm[:, 1, mt:mt + 1], in_=ssm[:, 0, mt:mt + 1], func=AF.Ln, scale=1.0 / D, bias=EPS),
                      r=[mjs_b], w=[mjs_b])
                sc.op("act", lambda: E["act"].activation(out=ssm[:, 2, mt:mt + 1], in_=ssm[:, 1, mt:mt + 1], func=AF.Exp, scale=-0.5),
                      r=[mjs_b], w=[mjs_b])
                sc.op("dve", lambda: E["dve"].scalar_tensor_tensor(out=mxn, in0=mst, scalar=ssm[:, 2, mt:mt + 1], in1=grow2,
                                                                   op0=ALU.mult, op1=ALU.mult), r=[mst_b, mjs_b, grow2_b], w=[mxn_b])
                transpose_to_T(mxn, mxn_b, 128, memT, memT_b[mt], mt * 128, mt % 2, "act")
            wmk_sl = load_slab(w_mem_kv, 0, 1024)
            wmv_sl = load_slab(w_mem_kv, 1024, 1024)
            kv_phase(memT, memT_b, MT, wmk_sl, wmv_sl,
                     lambda tt: nmk[tt * 128:(tt + 1) * 128, :], lambda tt: nmv[tt * 128:(tt + 1) * 128, :],
                     memkT, memkT_b, lambda tt: memv[:, tt], memv_b, 2, 256)

            for b_ in cT_b:
                sc.alias(b_, stg_b)
            qcT = region(O_TMP2, 8448, BF16, "p (c t) -> p c t", c=2)
            qcT_b = Buf("qcT")
            sc.alias(qcT_b, memT_b)
            PTc = [region(O_TMP2 + 8448 + i * 1024, 1024, BF16) for i in range(4)]
            PTc_b = [Buf("PTc%d" % i) for i in range(4)]
            cn = [region(O_TMP2 + 12544 + i * 512, 512, BF16) for i in range(2)]
            cn_b = [Buf("cn%d" % i) for i in range(2)]
            recc = region(O_TMP2 + 13568, 32, F32)
            recc_b = Buf("recc")
            cks = region(O_TMP2 + 13600, 2048, BF16)
            cks_b = Buf("cks")
            for b_ in PTc_b + cn_b + [recc_b, cks_b]:
                sc.alias(b_, [wo_b])
            wqc, wqc_b = load_slab(w_in, 5120, 1024)
            ccount = [0]

            def c_epilogue(h, accbank, nr, c0):
                i = ccount[0] % 2
                ccount[0] += 1
                sc.op("dve", lambda: E["dve"].reciprocal(out=recc[:nr, i:i + 1], in_=ps_f32(accbank)[:nr, 256:257]), r=[psb[accbank]], w=[recc_b])
                sc.op("dve", lambda: E["dve"].tensor_scalar(cn[i][:nr, :], ps_f32(accbank)[:nr, 0:256], recc[:nr, i:i + 1], None, op0=ALU.mult),
                      r=[psb[accbank], recc_b], w=[cn_b[i]])
                tb = 2 + i
                pv = ps_bf(tb).rearrange("p (c t) -> p c t", c=8)
                for dcx in range(2):
                    sc.op("pe", lambda dcx=dcx: E["pe"].transpose(out=pv[:, dcx, :nr], in_=cn[i][:nr, dcx * 128:(dcx + 1) * 128],
                                                                 identity=identb[:nr, :nr]), r=[cn_b[i], cB], w=[psb[tb]], signal=(dcx == 1))
                sc.op("act", lambda: E["act"].copy(out=c_outT[:, 2 * h:2 * h + 2, c0:c0 + nr], in_=pv[:, 0:2, :nr]), r=[psb[tb]], w=[cT_b[h]])

            for h in range(4):
                for dc in range(2):
                    for bi, (q0, n) in enumerate(BQ):
                        tb = 2 + (bi % 2)
                        for kc in range(8):
                            sc.op("pe", lambda kc=kc: E["pe"].matmul(
                                ps_f32(tb)[:, :n], lhsT=wqc[:, kc, h * 256 + dc * 128:h * 256 + (dc + 1) * 128], rhs=hT[:, kc, q0:q0 + n],
                                start=(kc == 0), stop=(kc == 7)), r=[wqc_b] + hT_b, w=[psb[tb]], signal=(kc == 7))
                        if bi < 4:
                            sc.op("act", lambda: E["act"].activation(out=qcT[:, dc, q0:q0 + n], in_=ps_f32(tb)[:, :n], func=AF.Copy, scale=0.0625),
                                  r=[psb[tb]], w=[qcT_b])
                        else:
                            sc.op("act", lambda: E["act"].activation(out=qcTs[:, h, dc, :], in_=ps_f32(tb)[:, :n], func=AF.Copy, scale=0.0625),
                                  r=[psb[tb]], w=[qcTs_b])
                for qb in range(4):
                    q0 = qb * 512
                    for mt in range(2):
                        sb = mt
                        for dc in range(2):
                            sc.op("pe", lambda dc=dc: E["pe"].matmul(
                                ps_f32(sb)[:, :512], lhsT=memkT[:, 2 * h + dc, mt * 128:(mt + 1) * 128], rhs=qcT[:, dc, q0:q0 + 512],
                                start=(dc == 0), stop=(dc == 1)), r=[memkT_b[mt], qcT_b], w=[psb[sb]], signal=(dc == 1))
                        pi = 2 * (qb % 2) + mt
                        sc.op("act", lambda: E["act"].activation(out=PTc[pi][:, :512], in_=ps_f32(sb)[:, :512], func=AF.Exp),
                              r=[psb[sb]], w=[PTc_b[pi]])
                        for j in range(4):
                            sc.op("pe", lambda j=j: E["pe"].matmul(
                                ps_f32(4 + j)[:, 0:257], lhsT=PTc[pi][:, j * 128:(j + 1) * 128], rhs=memv[:, mt, h, 0:257],
                                start=(mt == 0), stop=(mt == 1)), r=[PTc_b[pi], memv_b[mt]], w=[psb[4 + j]], signal=True)
                    for j in range(4):
                        c_epilogue(h, 4 + j, 128, q0 + j * 128)

            for s_ in range(2):
                for mt in range(2):
                    sc.dma("pool", [(memv[:, mt, :, 0:256], cmv[s_, mt * 128:(mt + 1) * 128, :].rearrange("p (h e) -> p h e", h=4))],
                           w=[memv_b[mt]])
                    sc.dma("pool", [(cks, cmk[s_, mt * 128:(mt + 1) * 128, :])], w=[cks_b])
                    pv = ps_bf(mt).rearrange("p (c t) -> p c t", c=8)
                    for c in range(8):
                        sc.op("pe", lambda c=c, pv=pv: E["pe"].transpose(out=pv[:, c, :], in_=cks[:, c * 128:(c + 1) * 128], identity=identb),
                              r=[cks_b, cB], w=[psb[mt]], signal=(c == 7))
                    sc.op("dve", lambda pv=pv: E["dve"].tensor_copy(out=memkT[:, :, mt * 128:(mt + 1) * 128], in_=pv), r=[psb[mt]], w=[memkT_b[mt]])
                for h in range(4):
                    sb = h % 2
                    first = True
                    for mt in range(2):
                        for dc in range(2):
                            sc.op("pe", lambda mt=mt, dc=dc, first=first: E["pe"].matmul(
                                ps_f32(sb)[:, mt * 32:(mt + 1) * 32], lhsT=memkT[:, 2 * h + dc, mt * 128:(mt + 1) * 128],
                                rhs=qcTs[:, h, dc, 32 * s_:32 * s_ + 32], start=first, stop=(dc == 1), skip_group_check=True),
                                r=memkT_b + [qcTs_b], w=[psb[sb]], signal=(mt == 1 and dc == 1))
                            first = False
                    pi = h % 4
                    sc.op("act", lambda: E["act"].activation(out=PTc[pi][:, :64], in_=ps_f32(sb)[:, :64], func=AF.Exp), r=[psb[sb]], w=[PTc_b[pi]])
                    ab = 4 + h
                    for mt in range(2):
                        sc.op("pe", lambda mt=mt: E["pe"].matmul(
                            ps_f32(ab)[0:32, 0:257], lhsT=PTc[pi][:, mt * 32:(mt + 1) * 32], rhs=memv[:, mt, h, 0:257],
                            start=(mt == 0), stop=(mt == 1)), r=[PTc_b[pi]] + memv_b, w=[psb[ab]], signal=(mt == 1))
                    c_epilogue(h, ab, 32, 2048 + 32 * s_)

            if DEBUG == "cT":
                sc.dma("sp", [(dbg[:, c * 2112:(c + 1) * 2112], c_outT[:, c, :]) for c in range(8)], r=cT_b)
            if phase_limit >= 6:
                merge_branch(2, c_outT, cT_b, w_proj_c)

        if phase_limit >= 7:
            b_outT = region(O_A, 33792, BF16, "p (c t) -> p c t", c=8)
            bT_b = [Buf("bT%d" % i) for i in range(8)]
            for b_ in bT_b:
                sc.alias(b_, cT_b + stg_b + [grow2_b, mst_b, mxn_b, mjs_b])
            colL = calloc(80 * 4, F32)
            colL_b = Buf("colL")
            with nc.allow_non_contiguous_dma(reason="tiny per-channel columns"):
                sc.dma("sp", [(colL[:, 0:8], b_rg_a.rearrange("o (c p) -> p (o c)", p=128)),
                              (colL[:, 8:16], b_rg_x.rearrange("o (c p) -> p (o c)", p=128)),
                              (colL[:, 16:24], conv_b.rearrange("o (c p) -> p (o c)", p=128)),
                              (colL[:, 24:56].rearrange("p (j c) -> p j c", j=4), conv_w.rearrange("j (c p) -> p j c", p=128)),
                              (colL[:, 72:80], rg_lambda.rearrange("o (c p) -> p (o c)", p=128))], w=[colL_b])
            sc.op("dve", lambda: E["dve"].tensor_scalar(colL[:, 0:16], colL[:, 0:16], -1.0, None, op0=ALU.mult), r=[colL_b], w=[colL_b])
            sc.op("act", lambda: E["act"].activation(out=colL[:, 72:80], in_=colL[:, 72:80], func=AF.Exp, scale=-1.0), r=[colL_b], w=[colL_b])
            sc.op("act", lambda: E["act"].activation(out=colL[:, 72:80], in_=colL[:, 72:80], func=AF.Ln, bias=1.0), r=[colL_b], w=[colL_b])
            sc.op("dve", lambda: E["dve"].tensor_scalar(colL[:, 56:64], colL[:, 72:80], -8.0, None, op0=ALU.mult), r=[colL_b], w=[colL_b])
            sc.op("dve", lambda: E["dve"].tensor_scalar(colL[:, 64:72], colL[:, 72:80], -16.0, None, op0=ALU.mult), r=[colL_b], w=[colL_b])

            lru_old = [wo_b, mT_b, gs_b, qcT_b, recc_b, cks_b, qcTs_b] + PTc_b + cn_b + memkT_b + memv_b + memT_b
            def lbuf(name):
                b_ = Buf(name)
                sc.alias(b_, lru_old)
                return b_
            wrga = region(O_TMP2, 2048, BF16, "p (n j) -> p n j", n=8)
            wrgx = region(O_TMP2 + 2048, 2048, BF16, "p (n j) -> p n j", n=8)
            wrg_b = lbuf("wrg")
            sc.dma("pool", [(wrga, w_rg_a.rearrange("n i j -> i n j")), (wrgx, w_rg_x.rearrange("n i j -> i n j"))], w=[wrg_b])
            LW = 256
            def tset(i):
                base = (O_TMP2 + 4096) if i == 0 else O_TMP
                o = [base]
                def mk(nbytes, dt, name):
                    ap = region(o[0], nbytes, dt)
                    o[0] += nbytes
                    return ap, lbuf(name + str(i))
                d = {}
                d["xpad"] = mk(1040, F32, "xpad")
                d["xc"] = mk(1024, F32, "xc")
                d["xcb"] = mk(512, BF16, "xcb")
                d["tRr"] = mk(1024, F32, "tRr")
                d["tA"] = mk(1024, F32, "tA")
                d["tA2"] = mk(1024, F32, "tA2")
                d["tI"] = mk(1024, F32, "tI")
                d["hh"] = mk(1024, F32, "hh")
                d["gbs"] = mk(1024, F32, "gbs")
                d["tG"] = mk(1024, F32, "tG")
                return d
            TS = [tset(0), tset(1)]
            hst = region(O_TMP + 9744, 16, F32); hst_b = lbuf("hst")
            xpad_b = [TS[0]["xpad"][1], TS[1]["xpad"][1]]
            xc_b, xcb_b, tA_b, tA2_b = TS[0]["xc"][1], TS[0]["xcb"][1], TS[0]["tA"][1], TS[0]["tA2"][1]
            tI_b, hh_b, gbs_b, tG_b, tRr_b = TS[1]["tI"][1], TS[1]["hh"][1], TS[1]["gbs"][1], TS[1]["tG"][1], TS[1]["tRr"][1]
            lru_all_b = [v_[1] for d_ in TS for v_ in d_.values()] + [hst_b]

            wxb, wxb_b = load_slab(w_in, 3072, 1024)
            wgb, wgb_b = load_slab(w_in, 4096, 1024)
            LB = [(i * LW, LW, -1) for i in range(2048 // LW)] + [(2048, 32, 0), (2080, 32, 1)]
            NPB = 2048 // LW
            blocks = [(n, li) for n in range(8) for li in range(len(LB))]

            def sig3(buf_ap, buf_b, src_ap, src_bufs, w, scale, bias):
                kw = {} if bias is None else {"bias": bias}
                sc.op("act", lambda: E["act"].activation(out=buf_ap[:, :w], in_=src_ap, func=AF.Exp, scale=-scale, **kw),
                      r=src_bufs + [colL_b], w=[buf_b])
                sc.op("act", lambda: E["act"].activation(out=buf_ap[:, :w], in_=buf_ap[:, :w], func=AF.Ln, bias=1.0), r=[buf_b], w=[buf_b])
                sc.op("act", lambda: E["act"].activation(out=buf_ap[:, :w], in_=buf_ap[:, :w], func=AF.Exp, scale=-1.0), r=[buf_b], w=[buf_b])

            def lru_stage1(k_):
                n, li = blocks[k_]
                q0, w, sq = LB[li]
                S_, P_ = TS[k_ % 2], TS[(k_ + 1) % 2]
                cur, cur_b = S_["xpad"]
                prv, prv_b = P_["xpad"]
                xc, xc_b_ = S_["xc"]
                xcb, xcb_b_ = S_["xcb"]
                px, pg, pa, pi_ = k_ % 2, 2 + k_ % 2, 4 + k_ % 2, 6 + k_ % 2
                for kc in range(8):
                    sc.op("pe", lambda kc=kc: E["pe"].matmul(ps_f32(px)[:, :w], lhsT=wxb[:, kc, n * 128:(n + 1) * 128], rhs=hT[:, kc, q0:q0 + w],
                                                            start=(kc == 0), stop=(kc == 7)), r=[wxb_b] + hT_b, w=[psb[px]], signal=(kc == 7))
                for kc in range(8):
                    sc.op("pe", lambda kc=kc: E["pe"].matmul(ps_f32(pg)[:, :w], lhsT=wgb[:, kc, n * 128:(n + 1) * 128], rhs=hT[:, kc, q0:q0 + w],
                                                            start=(kc == 0), stop=(kc == 7)), r=[wgb_b] + hT_b, w=[psb[pg]], signal=(kc == 7))
                if li == 0:
                    sc.op("dve", lambda: E["dve"].memset(cur[:, 0:3], 0.0), w=[cur_b])
                elif sq < 0:
                    sc.op("dve", lambda: E["dve"].tensor_copy(out=cur[:, 0:3], in_=prv[:, LW:LW + 3]), r=[prv_b], w=[cur_b])
                else:
                    with nc.allow_non_contiguous_dma(reason="tiny conv state"):
                        sc.dma("sp", [(cur[:, 0:3], sconv[sq, :, n * 128:(n + 1) * 128].rearrange("j c -> c j"))], w=[cur_b])
                sc.op("dve", lambda: E["dve"].tensor_copy(out=cur[:, 3:3 + w], in_=ps_f32(px)[:, :w]), r=[psb[px]], w=[cur_b])
                if li == NPB - 1 or sq >= 0:
                    dst = nconv if sq < 0 else nconvs[sq]
                    with nc.allow_non_contiguous_dma(reason="tiny conv state"):
                        sc.dma("sp", [(dst[:, n * 128:(n + 1) * 128].rearrange("j c -> c j"), cur[:, w:w + 3])], r=[cur_b])
                sc.op("dve", lambda: E["dve"].tensor_scalar(xc[:, :w], cur[:, 3:3 + w], colL[:, 24 + 3 * 8 + n:24 + 3 * 8 + n + 1],
                                                            colL[:, 16 + n:17 + n], op0=ALU.mult, op1=ALU.add), r=[cur_b, colL_b], w=[xc_b_])
                for j in range(3):
                    sc.op("dve", lambda j=j: E["dve"].scalar_tensor_tensor(out=xc[:, :w], in0=cur[:, j:j + w],
                                                                          scalar=colL[:, 24 + j * 8 + n:24 + j * 8 + n + 1], in1=xc[:, :w],
                                                                          op0=ALU.mult, op1=ALU.add), r=[cur_b, colL_b, xc_b_], w=[xc_b_])
                sc.op("dve", lambda: E["dve"].tensor_copy(out=xcb[:, :w], in_=xc[:, :w]), r=[xc_b_], w=[xcb_b_])
                sc.op("pe", lambda: E["pe"].matmul(ps_f32(pa)[:, :w], lhsT=wrga[:, n, :], rhs=xcb[:, :w], start=True, stop=True),
                      r=[wrg_b, xcb_b_], w=[psb[pa]])
                sc.op("pe", lambda: E["pe"].matmul(ps_f32(pi_)[:, :w], lhsT=wrgx[:, n, :], rhs=xcb[:, :w], start=True, stop=True),
                      r=[wrg_b, xcb_b_], w=[psb[pi_]])

            def lru_stage2(k_):
                n, li = blocks[k_]
                q0, w, sq = LB[li]
                S_ = TS[k_ % 2]
                xc, xc_b_ = S_["xc"]
                tRr, tRr_b_ = S_["tRr"]
                tA, tA_b_ = S_["tA"]
                tA2, tA2_b_ = S_["tA2"]
                tI, tI_b_ = S_["tI"]
                hh, hh_b_ = S_["hh"]
                gbs, gbs_b_ = S_["gbs"]
                tG, tG_b_ = S_["tG"]
                pg, pa, pi_ = 2 + k_ % 2, 4 + k_ % 2, 6 + k_ % 2
                sc.op("act", lambda: E["act"].copy(out=gbs[:, :w], in_=ps_f32(pg)[:, :w]), r=[psb[pg]], w=[gbs_b_])
                sc.op("act", lambda: E["act"].activation(out=tG[:, :w], in_=ps_f32(pg)[:, :w], func=AF.Square), r=[psb[pg]], w=[tG_b_])
                sc.op("dve", lambda: E["dve"].tensor_scalar(tG[:, :w], tG[:, :w], 0.044715, 1.0, op0=ALU.mult, op1=ALU.add), r=[tG_b_], w=[tG_b_])
                sc.op("dve", lambda: E["dve"].tensor_tensor(out=tG[:, :w], in0=tG[:, :w], in1=gbs[:, :w], op=ALU.mult), r=[tG_b_, gbs_b_], w=[tG_b_])
                sig3(tRr, tRr_b_, ps_f32(pa)[:, :w], [psb[pa]], w, 1.0, colL[:, n:n + 1])
                sc.op("act", lambda: E["act"].activation(out=tA[:, :w], in_=tRr[:, :w], func=AF.Exp, scale=colL[:, 56 + n:57 + n]),
                      r=[tRr_b_, colL_b], w=[tA_b_])
                sc.op("act", lambda: E["act"].activation(out=tA2[:, :w], in_=tRr[:, :w], func=AF.Exp, scale=colL[:, 64 + n:65 + n]),
                      r=[tRr_b_, colL_b], w=[tA2_b_])
                sc.op("act", lambda: E["act"].activation(out=tA2[:, :w], in_=tA2[:, :w], func=AF.Ln, scale=-1.0, bias=1.0), r=[tA2_b_], w=[tA2_b_])
                sc.op("act", lambda: E["act"].activation(out=tA2[:, :w], in_=tA2[:, :w], func=AF.Exp, scale=0.5), r=[tA2_b_], w=[tA2_b_])
                sig3(tI, tI_b_, ps_f32(pi_)[:, :w], [psb[pi_]], w, 1.0, colL[:, 8 + n:9 + n])
                sig3(tG, tG_b_, tG[:, :w], [tG_b_], w, 1.5957691216057308, None)
                sc.op("dve", lambda: E["dve"].tensor_tensor(out=tI[:, :w], in0=tI[:, :w], in1=tA2[:, :w], op=ALU.mult), r=[tI_b_, tA2_b_], w=[tI_b_])
                sc.op("dve", lambda: E["dve"].tensor_tensor(out=tI[:, :w], in0=tI[:, :w], in1=xc[:, :w], op=ALU.mult), r=[tI_b_, xc_b_], w=[tI_b_])
                if li == 0:
                    init = 0.0
                    rr = []
                elif sq < 0:
                    init = hst[:, 0:1]
                    rr = [hst_b]
                else:
                    with nc.allow_non_contiguous_dma(reason="tiny lru state"):
                        sc.dma("sp", [(hst[:, 1 + sq:2 + sq], slru[sq:sq + 1, n * 128:(n + 1) * 128].rearrange("o c -> c o"))], w=[hst_b])
                    init = hst[:, 1 + sq:2 + sq]
                    rr = [hst_b]
                sc.op("dve", lambda: E["dve"].tensor_tensor_scan(out=hh[:, :w], data0=tA[:, :w], data1=tI[:, :w], initial=init,
                                                                 op0=ALU.mult, op1=ALU.add), r=[tA_b_, tI_b_] + rr, w=[hh_b_])
                if sq < 0 and li < NPB - 1:
                    sc.op("dve", lambda: E["dve"].tensor_copy(out=hst[:, 0:1], in_=hh[:, w - 1:w]), r=[hh_b_], w=[hst_b])
                if li == NPB - 1 or sq >= 0:
                    dst = nlru[0:1, n * 128:(n + 1) * 128] if sq < 0 else nlrus[sq:sq + 1, n * 128:(n + 1) * 128]
                    with nc.allow_non_contiguous_dma(reason="tiny lru state"):
                        sc.dma("sp", [(dst.rearrange("o c -> c o"), hh[:, w - 1:w])], r=[hh_b_])
                sc.op("dve", lambda: E["dve"].tensor_tensor(out=tG[:, :w], in0=tG[:, :w], in1=gbs[:, :w], op=ALU.mult), r=[tG_b_, gbs_b_], w=[tG_b_])
                sc.op("dve", lambda: E["dve"].tensor_tensor(out=b_outT[:, n, q0:q0 + w], in0=tG[:, :w], in1=hh[:, :w], op=ALU.mult),
                      r=[tG_b_, hh_b_], w=[bT_b[n]])

            lru_stage1(0)
            for k_ in range(len(blocks)):
                if k_ + 1 < len(blocks):
                    lru_stage1(k_ + 1)
                lru_stage2(k_)

            if DEBUG == "bT":
                sc.dma("sp", [(dbg[:, c * 2112:(c + 1) * 2112], b_outT[:, c, :]) for c in range(8)], r=bT_b)
            if phase_limit >= 8:
                merge_branch(1, b_outT, bT_b, w_proj_b)

        if phase_limit >= 9:
            T17 = [(i * 128, 128, x2[:, i, :], x2_b[i]) for i in range(16)] + [(2048, 64, x2s, x2_b[16])]
            fin_old = [wo_b, wrg_b] + lru_all_b
            grow3 = region(O_TMP2, 4096, F32); grow3_b = Buf("grow3"); sc.alias(grow3_b, fin_old)
            xn3 = [region(O_TMP2 + 4096 + i * 2048, 2048, BF16) for i in range(2)]
            xn3_b = [Buf("xn3_%d" % i) for i in range(2)]
            junk3 = region(O_TMP2 + 8192, 2048, BF16)
            ss3 = region(O_TMP2 + 10240, 3 * 17 * 4, F32, "p (k t) -> p k t", k=3)
            js3_b = Buf("js3")
            for b_ in xn3_b + [js3_b]:
                sc.alias(b_, fin_old)
            h2T = hT
            h2T_b = [Buf("h2T%d" % i) for i in range(17)]
            for b_ in h2T_b:
                sc.alias(b_, hT_b)
            sc.dma("sp", [(grow3, norm_ffn.partition_broadcast(128))], w=[grow3_b])

            def rms2(st, st_b, rows, ssi, g_ap, g_b, out_ap, out_b, junk_ap, ss_ap, js_b):
                sc.op("act", lambda: E["act"].activation(out=junk_ap[:rows], in_=st[:rows], func=AF.Square,
                                                        accum_out=ss_ap[:rows, 0, ssi:ssi + 1]), r=[st_b], w=[js_b])
                sc.op("act", lambda: E["act"].activation(out=ss_ap[:rows, 1, ssi:ssi + 1], in_=ss_ap[:rows, 0, ssi:ssi + 1],
                                                        func=AF.Ln, scale=1.0 / D, bias=EPS), r=[js_b], w=[js_b])
                sc.op("act", lambda: E["act"].activation(out=ss_ap[:rows, 2, ssi:ssi + 1], in_=ss_ap[:rows, 1, ssi:ssi + 1],
                                                        func=AF.Exp, scale=-0.5), r=[js_b], w=[js_b])
                sc.op("dve", lambda: E["dve"].scalar_tensor_tensor(
                    out=out_ap[:rows], in0=st[:rows], scalar=ss_ap[:rows, 2, ssi:ssi + 1], in1=g_ap[:rows],
                    op0=ALU.mult, op1=ALU.mult), r=[st_b, js_b, g_b], w=[out_b])

            for ti, (t0, rows, xt_, xb_) in enumerate(T17):
                rms2(xt_, xb_, rows, ti, grow3, grow3_b, xn3[ti % 2], xn3_b[ti % 2], junk3, ss3, js3_b)
                transpose_to_T(xn3[ti % 2], xn3_b[ti % 2], rows, h2T, h2T_b[ti], t0, ti % 2, "act")

            actT = region(O_A, 33792, BF16, "p (c t) -> p c t", c=8)
            actT_b = Buf("actT")
            sc.alias(actT_b, bT_b)
            wo2 = region(O_TMP2, 16384, BF16, "p (c n) -> p c n", c=8)
            wo2_b = Buf("wo2")
            sc.alias(wo2_b, [grow3_b, js3_b] + xn3_b)
            tS = [region(O_TMP + i * 2048, 2048, F32) for i in range(2)]
            tS_b = [Buf("tS%d" % i) for i in range(2)]
            for b_ in tS_b:
                sc.alias(b_, [mT_b, gs_b] + lru_all_b)
            fc = [0]
            for (c0, cnt) in ((0, 8), (8, 8), (16, 6)):
                wgs, wgs_b = load_slab(w_ffn_in, c0 * 128, cnt * 128)
                wus, wus_b = load_slab(w_ffn_in, DFF + c0 * 128, cnt * 128)
                load_w(wo2, wo2_b, w_ffn_out, 0, 1024, kc0=c0, nkc=cnt)
                for (q0, n) in BQ:
                    for j in range(cnt):
                        k_ = fc[0]
                        fc[0] += 1
                        pg, pu = k_ % 2, 2 + k_ % 2
                        for kc in range(8):
                            sc.op("pe", lambda kc=kc: E["pe"].matmul(ps_f32(pg)[:, :n], lhsT=wgs[:, kc, j * 128:(j + 1) * 128],
                                                                    rhs=h2T[:, kc, q0:q0 + n], start=(kc == 0), stop=(kc == 7)),
                                  r=[wgs_b] + h2T_b, w=[psb[pg]], signal=(kc == 7))
                        for kc in range(8):
                            sc.op("pe", lambda kc=kc: E["pe"].matmul(ps_f32(pu)[:, :n], lhsT=wus[:, kc, j * 128:(j + 1) * 128],
                                                                    rhs=h2T[:, kc, q0:q0 + n], start=(kc == 0), stop=(kc == 7)),
                                  r=[wus_b] + h2T_b, w=[psb[pu]], signal=(kc == 7))
                        sc.op("act", lambda: E["act"].activation(out=tS[k_ % 2][:, :n], in_=ps_f32(pg)[:, :n], func=AF.Silu),
                              r=[psb[pg]], w=[tS_b[k_ % 2]])
                        sc.op("dve", lambda: E["dve"].tensor_tensor(out=actT[:, j, q0:q0 + n], in0=tS[k_ % 2][:, :n], in1=ps_f32(pu)[:, :n],
                                                                    op=ALU.mult), r=[tS_b[k_ % 2], psb[pu]], w=[actT_b])
                    ntile = (n + 127) // 128
                    for ti in range(ntile):
                        rows = min(128, n - ti * 128)
                        if q0 < 2048:
                            tt = q0 // 128 + ti
                            xt_ = x2[:, tt, :]
                        else:
                            tt = 16
                            xt_ = x2s
                        for half in range(2):
                            pb = 4 + (2 * ti + half) % 4
                            for j in range(cnt):
                                sc.op("pe", lambda j=j: E["pe"].matmul(
                                    ps_f32(pb)[:rows, :], lhsT=actT[:, j, q0 + ti * 128:q0 + ti * 128 + rows],
                                    rhs=wo2[:, j, half * 512:(half + 1) * 512], start=(j == 0), stop=(j == cnt - 1)),
                                    r=[actT_b, wo2_b], w=[psb[pb]], signal=(j == cnt - 1))
                            sc.op("dve", lambda: E["dve"].tensor_tensor(
                                out=xt_[:rows, half * 512:(half + 1) * 512], in0=xt_[:rows, half * 512:(half + 1) * 512],
                                in1=ps_f32(pb)[:rows, :], op=ALU.add), r=[psb[pb]], w=[x2_b[tt]])

            growF = region(O_A, 4096, F32); growF_b = Buf("growF"); sc.alias(growF_b, [actT_b])
            yst = [region(O_A + 4096 + i * 4096, 4096, F32) for i in range(2)]
            yst_b = [Buf("yst%d" % i) for i in range(2)]
            junkF = region(O_A + 12288, 2048, BF16)
            ssF = region(O_A + 14336, 3 * 17 * 4, F32, "p (k t) -> p k t", k=3)
            jsF_b = Buf("jsF")
            for b_ in yst_b + [jsF_b]:
                sc.alias(b_, [actT_b])
            sc.dma("sp", [(growF, norm_final.partition_broadcast(128))], w=[growF_b])
            for ti, (t0, rows, xt_, xb_) in enumerate(T17):
                rms2(xt_, xb_, rows, ti, growF, growF_b, yst[ti % 2], yst_b[ti % 2], junkF, ssF, jsF_b)
                dst = yp[t0:t0 + rows, :] if ti < 16 else ys[0:64, :]
                sc.dma("sp", [(dst, yst[ti % 2][:rows])], r=[yst_b[ti % 2]])

        if DEBUG == "x2":
            sc.dma("sp", [(dbg2[tt * 128:(tt + 1) * 128, :], x2[:, tt, :]) for tt in range(16)] + [(dbg2[2048:2112, :], x2s[0:64, :])],
                   r=x2_b)
        if DEBUG == "aT" and phase_limit >= 3:
            sc.dma("sp", [(dbg[:, c * 2112:(c + 1) * 2112], a_outT[:, c, :]) for c in range(8)], r=aT_b)
        if DEBUG == "hT":
            sc.dma("sp", [(dbg[:, c * 2112:(c + 1) * 2112], hT[:, c, :]) for c in range(8)], r=hT_b)
        sc.finish()
    return nc


_NC_CACHE = {}
_DBG = None


def _get_nc():
    lim = int(os.environ.get("MK_PHASE_LIMIT", "99"))
    if lim not in _NC_CACHE:
        _NC_CACHE[lim] = build_program(lim)
    return _NC_CACHE[lim]


def kernel(x_prompt, x_sample, mem_prompt, cache_k, cache_v, state_conv, state_lru, cache_mem_k, cache_mem_v,
           rel_table, norm_mix, w_in, lambda_q1, lambda_k1, lambda_q2, lambda_k2, subln_g, conv_w, conv_b,
           w_rg_a, b_rg_a, w_rg_x, b_rg_x, rg_lambda, norm_mem, w_mem_kv, w_proj_a, w_proj_b, w_proj_c,
           w_gate, b_gate, w_out, norm_ffn, w_ffn_in, w_ffn_out, norm_final):
    f = lambda a: np.ascontiguousarray(np.asarray(a, dtype=np.float32))
    nc = _get_nc()
    shared = {
        "rel_table": f(rel_table), "boh": _bias_onehot(), "ident": np.eye(128, dtype=np.float32), "aident": np.ascontiguousarray(np.eye(128, dtype=np.float32)[::-1]), "norm_mix": f(norm_mix), "w_in": f(w_in)[0],
        "lq1": f(lambda_q1), "lk1": f(lambda_k1), "lq2": f(lambda_q2), "lk2": f(lambda_k2),
        "subln_g": f(subln_g), "conv_w": f(conv_w)[0], "conv_b": f(conv_b), "w_rg_a": f(w_rg_a)[0],
        "b_rg_a": f(b_rg_a), "w_rg_x": f(w_rg_x)[0], "b_rg_x": f(b_rg_x), "rg_lambda": f(rg_lambda),
        "norm_mem": f(norm_mem), "w_mem_kv": f(w_mem_kv)[0], "w_proj_a": f(w_proj_a)[0],
        "w_proj_b": f(w_proj_b)[0], "w_proj_c": f(w_proj_c)[0], "w_gate": f(w_gate)[0], "b_gate": f(b_gate),
        "w_out": f(w_out)[0], "norm_ffn": f(norm_ffn), "w_ffn_in": f(w_ffn_in)[0], "w_ffn_out": f(w_ffn_out)[0],
        "norm_final": f(norm_final).reshape(1, D),
    }
    x_prompt = f(x_prompt); x_sample = f(x_sample); mem_prompt = f(mem_prompt)
    cache_k = f(cache_k); cache_v = f(cache_v); state_conv = f(state_conv); state_lru = f(state_lru)
    cache_mem_k = f(cache_mem_k); cache_mem_v = f(cache_mem_v)
    in_maps = []
    for c in range(NCORES):
        m = dict(shared)
        m["xp"] = x_prompt[c]
        m["xs"] = x_sample[2 * c:2 * c + 2].reshape(64, D)
        m["mem"] = mem_prompt[c]
        m["ck"] = cache_k[0, 2 * c:2 * c + 2].reshape(2, S, D)
        m["cv"] = cache_v[0, 2 * c:2 * c + 2].reshape(2, S, D)
        m["sconv"] = state_conv[0, 2 * c:2 * c + 2]
        m["slru"] = state_lru[0, 2 * c:2 * c + 2]
        m["cmk"] = cache_mem_k[0, 2 * c:2 * c + 2].reshape(2, 256, D)
        m["cmv"] = cache_mem_v[0, 2 * c:2 * c + 2].reshape(2, 256, D)
        in_maps.append(m)
    res = run_bass_kernel_spmd(nc, in_maps, core_ids=list(range(NCORES)))
    R = res.results
    global _DBG
    _DBG = R[0].get("debug_out") if isinstance(R[0], dict) else None
    global _DBG2
    _DBG2 = R[0].get("debug_x2") if isinstance(R[0], dict) else None
    cat = lambda k: np.stack([np.asarray(R[c][k], dtype=np.float32) for c in range(NCORES)])
    y_prompt = cat("yp")
    y_sample = cat("ys").reshape(16, 32, D)
    new_k_p = cat("nk").reshape(1, 8, S, 8, 2, 64)
    new_v_p = cat("out_v").reshape(1, 8, S, 8, 128)
    new_conv_p = cat("nconv").reshape(1, 8, 3, D)
    new_lru_p = cat("nlru").reshape(1, 8, D)
    new_mk = cat("nmk").reshape(1, 8, 256, 4, 256)
    new_mv = cat("nmv").reshape(1, 8, 256, 4, 256)
    new_k_s = cat("nks").reshape(1, 16, 32, 8, 2, 64)
    new_v_s = cat("out_vs").reshape(1, 16, 32, 8, 128)
    new_conv_s = cat("nconvs").reshape(1, 16, 3, D)
    new_lru_s = cat("nlrus").reshape(1, 16, D)
    return (y_prompt, y_sample, new_k_p, new_v_p, new_conv_p, new_lru_p, new_mk, new_mv,
            new_k_s, new_v_s, new_conv_s, new_lru_s)
```

```python
import os
from contextlib import ExitStack
import numpy as np
import concourse.bass as bass
import concourse.mybir as mybir
from concourse.bass_utils import run_bass_kernel_spmd

F32 = mybir.dt.float32
BF16 = mybir.dt.bfloat16
U8 = mybir.dt.uint8
AF = mybir.ActivationFunctionType
ALU = mybir.AluOpType

D = 1024
S = 2048
T = 2112
NCORES = 8
DFF = 2816
EPS = 1e-6
LAMBDA_INIT = 0.2
NEG = -30000.0


class Buf:
    __slots__ = ("name", "w", "r", "dsem", "dval")

    def __init__(self, name):
        self.name = name
        self.w = []
        self.r = []
        self.dsem = None
        self.dval = 0


class Sched:
    ENG = ("pe", "act", "dve", "pool", "sp")

    def __init__(self, nc, stack):
        self.nc = nc
        self.stack = stack
        self.e = {"pe": nc.tensor, "act": nc.scalar, "dve": nc.vector, "pool": nc.gpsimd, "sp": nc.sync}
        self.sem = {k: stack.enter_context(nc.semaphore("s_" + k)) for k in self.ENG}
        self.cnt = {k: 0 for k in self.ENG}
        self.pending = {k: False for k in self.ENG}
        self.seen = {k: {} for k in self.ENG}
        self.dma_sems = []
        self.all_dma = []

    def _wait(self, eng, tok):
        kind = tok[0]
        if kind == "e":
            _, src, idx = tok
            if src == eng and src == "pe":
                return
            key = src
            if self.seen[eng].get(key, 0) >= idx:
                return
            self.e[eng].wait_ge(self.sem[src], idx)
            self.seen[eng][key] = idx
        else:
            _, buf, val = tok
            key = ("d", id(buf))
            if self.seen[eng].get(key, 0) >= val:
                return
            self.e[eng].wait_ge(buf.dsem, val)
            self.seen[eng][key] = val

    def _deps(self, eng, r, w):
        toks = []
        for b in r:
            toks += b.w
        for b in w:
            toks += b.w + b.r
        for t in toks:
            self._wait(eng, t)

    def op(self, eng, fn, r=(), w=(), signal=True):
        self._deps(eng, r, w)
        inst = fn()
        if signal:
            self.cnt[eng] += 1
            inst.then_inc(self.sem[eng], 1)
            tok = ("e", eng, self.cnt[eng])
            self.pending[eng] = False
        else:
            tok = ("e", eng, self.cnt[eng] + 1)
            self.pending[eng] = True
        for b in r:
            b.r.append(tok)
        for b in w:
            b.w = [tok]
            b.r = []
        return inst

    def dma(self, q, pairs, r=(), w=(), **kw):
        self._deps(q, r, w)
        owner = w[0] if w else r[0]
        if owner.dsem is None:
            owner.dsem = self.stack.enter_context(self.nc.semaphore("d_" + owner.name))
        for (o, i) in pairs:
            self.e[q].dma_start(out=o, in_=i, **kw).then_inc(owner.dsem, 16)
            owner.dval += 16
        tok = ("d", owner, owner.dval)
        for b in r:
            b.r.append(tok)
        for b in w:
            b.w = [tok]
            b.r = []
        self.all_dma.append(tok)

    def alias(self, new, olds):
        for o in olds:
            new.r += o.w + o.r

    def finish(self):
        for tok in self.all_dma:
            self._wait("sp", tok)


def _rel_bucket_np(rel):
    n = np.abs(rel)
    nf = np.maximum(n, 1).astype(np.float32)
    large = 8 + (np.log(nf / np.float32(8)) / np.float32(np.log(16.0)) * np.float32(8)).astype(np.int32)
    large = np.minimum(large, 15)
    return np.where(rel > 0, 16, 0) + np.where(n < 8, n, large)


def _bias_onehot():
    j = np.arange(384)
    rel = j - 255
    b = _rel_bucket_np(rel)
    oh = np.zeros((32, 384), np.float32)
    oh[b, j] = 1.0
    oh[15, :] -= 1.0
    oh[:, 383] = 0.0
    return oh


def build_program(phase_limit=99):
    nc = bass.Bass("TRN2", target_bir_lowering=False)

    def din(name, shape):
        return nc.dram_tensor(name, list(shape), F32, kind="ExternalInput")

    def dout(name, shape):
        return nc.dram_tensor(name, list(shape), F32, kind="ExternalOutput")

    xp = din("xp", [S, D]).ap()
    xs = din("xs", [64, D]).ap()
    mem = din("mem", [256, D]).ap()
    ck = din("ck", [2, S, D]).ap()
    cv = din("cv", [2, S, D]).ap()
    sconv = din("sconv", [2, 3, D]).ap()
    slru = din("slru", [2, D]).ap()
    cmk = din("cmk", [2, 256, D]).ap()
    cmv = din("cmv", [2, 256, D]).ap()
    rel_table = din("rel_table", [32, 8]).ap()
    boh = din("boh", [32, 384]).ap()
    ident_in = din("ident", [128, 128]).ap()
    aident_in = din("aident", [128, 128]).ap()
    norm_mix = din("norm_mix", [1, D]).ap()
    w_in = din("w_in", [D, 6144]).ap()
    lq1 = din("lq1", [1, 64]).ap()
    lk1 = din("lk1", [1, 64]).ap()
    lq2 = din("lq2", [1, 64]).ap()
    lk2 = din("lk2", [1, 64]).ap()
    subln_g = din("subln_g", [1, 128]).ap()
    conv_w = din("conv_w", [4, D]).ap()
    conv_b = din("conv_b", [1, D]).ap()
    w_rg_a = din("w_rg_a", [8, 128, 128]).ap()
    b_rg_a = din("b_rg_a", [1, D]).ap()
    w_rg_x = din("w_rg_x", [8, 128, 128]).ap()
    b_rg_x = din("b_rg_x", [1, D]).ap()
    rg_lambda = din("rg_lambda", [1, D]).ap()
    norm_mem = din("norm_mem", [1, D]).ap()
    w_mem_kv = din("w_mem_kv", [D, 2048]).ap()
    w_proj_a = din("w_proj_a", [D, D]).ap()
    w_proj_b = din("w_proj_b", [D, D]).ap()
    w_proj_c = din("w_proj_c", [D, D]).ap()
    w_gate = din("w_gate", [D, 3072]).ap()
    b_gate = din("b_gate", [1, 3072]).ap()
    w_out = din("w_out", [D, D]).ap()
    norm_ffn = din("norm_ffn", [1, D]).ap()
    w_ffn_in = din("w_ffn_in", [D, 2 * DFF]).ap()
    w_ffn_out = din("w_ffn_out", [DFF, D]).ap()
    norm_final = din("norm_final", [1, D]).ap()

    yp = dout("yp", [S, D]).ap()
    ys = dout("ys", [64, D]).ap()
    nk = dout("nk", [S, D]).ap()
    nv = dout("out_v", [S, D]).ap()
    nconv = dout("nconv", [3, D]).ap()
    nlru = dout("nlru", [1, D]).ap()
    nmk = dout("nmk", [256, D]).ap()
    nmv = dout("nmv", [256, D]).ap()
    nks = dout("nks", [64, D]).ap()
    nvs = dout("out_vs", [64, D]).ap()
    nconvs = dout("nconvs", [2, 3, D]).ap()
    nlrus = dout("nlrus", [2, D]).ap()
    tsc_h = nc.dram_tensor("tsc", [8, 384], F32, kind="Internal")
    DEBUG = os.environ.get("MK_DEBUG", "")
    dbg = nc.dram_tensor("debug_out", [128, 16896], BF16, kind="ExternalOutput").ap() if DEBUG else None
    dbg2 = nc.dram_tensor("debug_x2", [2112, 1024], F32, kind="ExternalOutput").ap() if DEBUG else None

    stack = ExitStack()
    with stack:
        arena = stack.enter_context(nc.sbuf_tensor("arena", [128, 212800], U8))
        psum = [stack.enter_context(nc.psum_tensor("ps%d" % i, [128, 512], F32)) for i in range(8)]
        stack.enter_context(nc.Block())
        sc = Sched(nc, stack)
        E = sc.e

        def region(off, nbytes, dt, pattern=None, **kw):
            ap = arena[:, off:off + nbytes].bitcast(dt)
            if pattern:
                ap = ap.rearrange(pattern, **kw)
            return ap

        O_CONST = 0
        O_SLAB = 14336
        O_HT = O_SLAB + 3 * 16384
        O_R1 = O_HT + 33792
        O_A = O_R1 + 67072
        O_TMP = O_A + 33792
        TMP_SZ = 212800 - O_TMP
        assert TMP_SZ >= 14656, TMP_SZ

        psb = [Buf("psum%d" % i) for i in range(8)]

        def ps_f32(i):
            return psum[i][:]

        def ps_bf(i):
            return psum[i][:].bitcast(BF16)

        co = [O_CONST]

        def calloc(nbytes, dt, pattern=None, **kw):
            off = co[0]
            co[0] += (nbytes + 31) // 32 * 32
            assert co[0] <= O_SLAB
            return region(off, nbytes, dt, pattern, **kw)

        identb = calloc(256, BF16)
        identf = calloc(512, F32)
        bias_t = calloc(8 * 2 * 2 * 256, BF16, "p (h k s q) -> p h k s q", h=8, k=2, s=2)
        colv = calloc(64 * 4, F32)
        cB = Buf("consts")

        C_NLAM, C_GSUB, C_SP4, C_ONE = 0, 1, 2, 10
        C_BA, C_BX, C_CB, C_CW = 11, 19, 27, 35
        colv2 = calloc(64 * 4, F32)
        C2_BG = 0
        colv_b = Buf("colv")

        sc.dma("sp", [(identf, ident_in)], w=[cB])
        sc.op("dve", lambda: E["dve"].tensor_copy(out=identb, in_=identf), r=[cB], w=[cB])


        hT = region(O_HT, 33792, BF16, "p (c t) -> p c t", c=8)
        hT_b = [Buf("hT%d" % i) for i in range(18)]
        TILES = [(i * 128, 128) for i in range(16)] + [(2048, 32), (2080, 32)]

        def tile_src(tt):
            t0, rows = TILES[tt]
            if tt < 16:
                return xp[t0:t0 + rows, :]
            return xs[(t0 - 2048):(t0 - 2048) + rows, :]

        grow = region(O_A, 4096, F32)
        grow_b = Buf("grow")
        sc.dma("sp", [(grow, norm_mix.partition_broadcast(128))], w=[grow_b])

        xst = [region(O_R1 + i * 4096, 4096, F32) for i in range(3)]
        xst_b = [Buf("xst%d" % i) for i in range(3)]
        xn = [region(O_R1 + 12288 + i * 2048, 2048, BF16) for i in range(2)]
        xn_b = [Buf("xn%d" % i) for i in range(2)]
        junk = region(O_R1 + 16384, 2048, BF16)
        junk_b = Buf("junk")
        ssb = region(O_R1 + 18432, 18 * 4 * 3, F32, "p (k t) -> p k t", k=3)
        ss_b = Buf("ss")

        def rms_tile(src_ap, rows, st, st_b, ssi, g_ap, out_bf, out_b):
            sc.op("act", lambda: E["act"].activation(out=junk[:rows], in_=st[:rows], func=AF.Square,
                                                    accum_out=ssb[:rows, 0, ssi:ssi + 1]),
                  r=[st_b], w=[junk_b, ss_b])
            sc.op("act", lambda: E["act"].activation(out=ssb[:rows, 1, ssi:ssi + 1], in_=ssb[:rows, 0, ssi:ssi + 1],
                                                    func=AF.Ln, scale=1.0 / D, bias=EPS), r=[ss_b], w=[ss_b])
            sc.op("act", lambda: E["act"].activation(out=ssb[:rows, 2, ssi:ssi + 1], in_=ssb[:rows, 1, ssi:ssi + 1],
                                                    func=AF.Exp, scale=-0.5), r=[ss_b], w=[ss_b])
            sc.op("dve", lambda: E["dve"].scalar_tensor_tensor(
                out=out_bf[:rows], in0=st[:rows], scalar=ssb[:rows, 2, ssi:ssi + 1], in1=g_ap[:rows],
                op0=ALU.mult, op1=ALU.mult), r=[st_b, ss_b, grow_b], w=[out_b])

        def transpose_to_T(src_bf, src_b, rows, dstT, dst_b, t0, pbank, evac_eng):
            pv = ps_bf(pbank).rearrange("p (c t) -> p c t", c=8)
            for c in range(8):
                sc.op("pe", lambda c=c: E["pe"].transpose(out=pv[:, c, :rows], in_=src_bf[:rows, c * 128:(c + 1) * 128],
                                                          identity=identb[:rows, :rows]),
                      r=[src_b, cB], w=[psb[pbank]], signal=(c == 7))
            if evac_eng == "act":
                sc.op("act", lambda: E["act"].copy(out=dstT[:, :, t0:t0 + rows], in_=pv[:, :, :rows]),
                      r=[psb[pbank]], w=[dst_b])
            else:
                sc.op("dve", lambda: E["dve"].tensor_copy(out=dstT[:, :, t0:t0 + rows], in_=pv[:, :, :rows]),
                      r=[psb[pbank]], w=[dst_b])

        def p1_a(tt):
            t0, rows = TILES[tt]
            st, st_b = xst[tt % 3], xst_b[tt % 3]
            sc.dma("sp", [(st[:rows], tile_src(tt))], w=[st_b])
            rms_tile(None, rows, st, st_b, tt, grow, xn[tt % 2], xn_b[tt % 2])

        def p1_b(tt):
            t0, rows = TILES[tt]
            transpose_to_T(xn[tt % 2], xn_b[tt % 2], rows, hT, hT_b[tt], t0, tt % 2, "act")

        p1_a(0)
        for tt in range(18):
            if tt + 1 < 18:
                p1_a(tt + 1)
            p1_b(tt)

        slab = [region(O_SLAB + i * 16384, 16384, BF16, "p (c n) -> p c n", c=8) for i in range(2)]
        slab_b = [Buf("slab%d" % i) for i in range(2)]
        slab_i = [0]
        O_TMP2 = O_SLAB + 2 * 16384

        def load_slab(w_ap, c0, ncols, kc0=0, nkc=8):
            i = slab_i[0] % 2
            slab_i[0] += 1
            src = w_ap[kc0 * 128:(kc0 + nkc) * 128, c0:c0 + ncols].rearrange("(c p) n -> p c n", p=128)
            pairs = []
            step = 2
            for k0 in range(0, nkc, step):
                k1 = min(nkc, k0 + step)
                pairs.append((slab[i][:, k0:k1, 0:ncols], src[:, k0:k1, :]))
            sc.dma("pool", pairs, w=[slab_b[i]])
            return slab[i], slab_b[i]

        stg = [region(O_A + 4096 + i * 4096, 4096, F32) for i in range(4)]
        stg_b = [Buf("stg%d" % i) for i in range(4)]
        stg_i = [0]

        def kv_phase(srcT, srcT_b, tiles, wk_sl, wv_sl, krows, vrows, kT_dst, kT_dst_b, vdst, vdst_b, vh, ve):
            for which, (wsl, wsl_b) in enumerate((wk_sl, wv_sl)):
                for tt, (t0, rows) in enumerate(tiles):
                    sg, sg_b = stg[stg_i[0] % 4], stg_b[stg_i[0] % 4]
                    stg_i[0] += 1
                    for half in range(2):
                        pb = (2 * tt + half) % 4
                        for kc in range(8):
                            sc.op("pe", lambda kc=kc, half=half, pb=pb: E["pe"].matmul(
                                ps_f32(pb)[:rows, :], lhsT=srcT[:, kc, t0:t0 + rows], rhs=wsl[:, kc, half * 512:(half + 1) * 512],
                                start=(kc == 0), stop=(kc == 7)),
                                r=[srcT_b[tt], wsl_b], w=[psb[pb]], signal=(kc == 7))
                        sc.op("act", lambda half=half, pb=pb: E["act"].copy(
                            out=sg[:rows, half * 512:(half + 1) * 512], in_=ps_f32(pb)[:rows, :]),
                            r=[psb[pb]], w=[sg_b])
                        if which == 1:
                            for hv in range(vh):
                                sc.op("dve", lambda half=half, hv=hv: E["dve"].tensor_copy(
                                    out=vdst(tt)[:rows, half * vh + hv, 0:ve],
                                    in_=sg[:rows, half * 512 + hv * ve:half * 512 + (hv + 1) * ve]),
                                    r=[sg_b], w=[vdst_b[tt]])
                    if which == 0:
                        sc.dma("sp", [(krows(tt), sg[:rows])], r=[sg_b])
                        for half in range(2):
                            pb = 4 + (2 * tt + half) % 4
                            pv = ps_f32(pb).rearrange("p (c t) -> p c t", c=4)
                            for c in range(4):
                                hh = half * 4 + c
                                sc.op("pe", lambda c=c, hh=hh, pv=pv: E["pe"].transpose(
                                    out=pv[:, c, :rows], in_=sg[:rows, hh * 128:(hh + 1) * 128], identity=identf[:rows, :rows]),
                                    r=[sg_b, cB], w=[psb[pb]], signal=(c == 3))
                            sc.op("dve", lambda half=half, pv=pv: E["dve"].tensor_copy(
                                out=kT_dst[:, half * 4:(half + 1) * 4, t0:t0 + rows], in_=pv[:, :, :rows]),
                                r=[psb[pb]], w=[kT_dst_b[tt]])
                    else:
                        sc.dma("sp", [(vrows(tt), sg[:rows])], r=[sg_b])

        kT = region(O_R1, 33792, BF16, "p (c t) -> p c t", c=8)
        kT_b = [Buf("kT%d" % i) for i in range(18)]
        v_aug = region(O_R1 + 33792, 16 * 8 * 130 * 2, BF16, "p (t h e) -> p t h e", t=16, h=8)
        sv_aug = calloc(2 * 8 * 130 * 2, BF16, "p (t h e) -> p t h e", t=2, h=8)
        v_b = [Buf("v%d" % i) for i in range(18)]
        for b_ in kT_b + v_b:
            sc.alias(b_, xst_b + xn_b + [junk_b, ss_b])
        sc.op("pool", lambda: E["pool"].memset(v_aug[:, :, :, 128:129], 1.0), w=v_b[:16])
        sc.op("pool", lambda: E["pool"].memset(sv_aug[:, :, :, 128:129], 1.0), w=v_b[16:])

        def out_rows(dst_p, dst_s, tt):
            t0, rows = TILES[tt]
            if tt < 16:
                return dst_p[t0:t0 + rows, :]
            return dst_s[t0 - 2048:t0 - 2048 + rows, :]

        wk_sl = load_slab(w_in, 1024, 1024)
        wv_sl = load_slab(w_in, 2048, 1024)
        kv_phase(hT, hT_b, TILES, wk_sl, wv_sl,
                 lambda tt: out_rows(nk, nks, tt), lambda tt: out_rows(nv, nvs, tt),
                 kT, kT_b, lambda tt: (v_aug[:, tt] if tt < 16 else sv_aug[:, tt - 16]), v_b, 4, 128)

        if phase_limit == 210:
            sc.finish()
            return nc
        scr = region(O_TMP2, 16384, F32)
        scr_b = Buf("scr")
        lam4 = scr[:, 0:256].rearrange("p (k d) -> p k d", k=4)
        sc.dma("sp", [(lam4[:, 0, :], lq1.partition_broadcast(128)), (lam4[:, 1, :], lk1.partition_broadcast(128)),
                      (lam4[:, 2, :], lq2.partition_broadcast(128)), (lam4[:, 3, :], lk2.partition_broadcast(128))],
               w=[scr_b])
        lt = scr[:, 256:512]
        sc.op("dve", lambda: E["dve"].tensor_tensor(out=lt[:, 0:64], in0=lam4[:, 0, :], in1=lam4[:, 1, :], op=ALU.mult), r=[scr_b], w=[scr_b])
        sc.op("dve", lambda: E["dve"].tensor_tensor(out=lt[:, 64:128], in0=lam4[:, 2, :], in1=lam4[:, 3, :], op=ALU.mult), r=[scr_b], w=[scr_b])
        sc.op("dve", lambda: E["dve"].reduce_sum(out=lt[:, 128:130], in_=lt[:, 0:128].rearrange("p (k d) -> p k d", k=2),
                                                 axis=mybir.AxisListType.X), r=[scr_b], w=[scr_b])
        sc.op("act", lambda: E["act"].activation(out=lt[:, 130:132], in_=lt[:, 128:130], func=AF.Exp), r=[scr_b], w=[scr_b])
        sc.op("dve", lambda: E["dve"].scalar_tensor_tensor(out=colv[:, C_NLAM:C_NLAM + 1], in0=lt[:, 131:132], scalar=-LAMBDA_INIT,
                                                           in1=lt[:, 130:131], op0=ALU.add, op1=ALU.subtract), r=[scr_b], w=[colv_b])
        with nc.allow_non_contiguous_dma(reason="tiny per-channel columns"):
            sc.dma("sp", [(lt[:, 132:133], subln_g.rearrange("o e -> e o"))], w=[scr_b])
        sc.op("dve", lambda: E["dve"].tensor_scalar(colv[:, C_GSUB:C_GSUB + 1], lt[:, 132:133], 1.0 - LAMBDA_INIT, None, op0=ALU.mult),
              r=[scr_b], w=[colv_b])

        if phase_limit != 200:
            rt = scr[0:32, 512:520]
            bo = scr[0:32, 1024:1408]
            sc.dma("sp", [(rt, rel_table), (bo, boh)], w=[scr_b])
            sc.op("pe", lambda: E["pe"].matmul(ps_f32(7)[0:8, 0:384], lhsT=rt, rhs=bo, start=True, stop=True), r=[scr_b], w=[psb[7]])
            tms = scr[0:8, 1536:1920]
            sc.op("dve", lambda: E["dve"].tensor_copy(out=tms, in_=ps_f32(7)[0:8, 0:384]), r=[psb[7]], w=[scr_b])
            tsc_b = Buf("tsc")
            sc.dma("sp", [(tsc_h.ap(), tms)], r=[scr_b], w=[tsc_b])
            if phase_limit == 201:
                sc.finish()
                return nc
            btf = scr[:, 2048:4096].rearrange("p (h k q) -> p h k q", h=8, k=2)
            ghk = region(O_A + 20480, 8192, F32, "p (h k q) -> p h k q", h=8, k=2)
            aid = region(O_A + 28672, 512, F32)
            ghk_b = Buf("ghk")
            pairs = [(aid, aident_in)]
            for h in range(8):
                for k_, base in ((0, 128), (1, 0)):
                    pairs.append((ghk[:, h, k_, :], bass.AP(tsc_h, h * 384 + base, [[1, 128], [1, 128]])))
            sc.dma("sp", pairs, r=[tsc_b], w=[ghk_b])
            for h in range(8):
                for k_ in range(2):
                    sc.op("pe", lambda h=h, k_=k_: E["pe"].matmul(ps_f32(7)[:, k_ * 128:(k_ + 1) * 128], lhsT=ghk[:, h, k_, :], rhs=aid,
                                                               start=True, stop=True), r=[ghk_b], w=[psb[7]])
                sc.op("dve", lambda h=h: E["dve"].tensor_copy(out=btf[:, h, :, :], in_=ps_f32(7)[:, 0:256].rearrange("p (k q) -> p k q", k=2)),
                      r=[psb[7]], w=[scr_b])
            sc.op("pool", lambda: E["pool"].memset(btf[64:128, :, 0, 0:64], NEG), r=[scr_b], w=[scr_b])
            ebt = bias_t.bitcast(F32) if False else None
        ebt = region(O_CONST + 256 + 512, 8192, F32, "p (h k q) -> p h k q", h=8, k=2)
        sc.op("act", lambda: E["act"].activation(out=ebt, in_=btf, func=AF.Exp), r=[scr_b], w=[cB])

        if phase_limit >= 3:
            a_outT = region(O_A, 33792, BF16, "p (c t) -> p c t", c=8)
            aT_b = [Buf("aT%d" % i) for i in range(8)]
            for b_ in aT_b:
                sc.alias(b_, stg_b + [grow_b])
            qT = [region(O_TMP + i * 4224, 4224, BF16) for i in range(2)]
            qT_b = [Buf("qT%d" % i) for i in range(2)]
            PT = [region(O_TMP + 8448 + i * 1024, 1024, BF16) for i in range(4)]
            PT_b = [Buf("PT%d" % i) for i in range(4)]
            qTs = region(O_TMP + 12544, 1024, BF16, "p (h t) -> p h t", h=8)
            qTs_b = Buf("qTs")
            accS = region(O_TMP2, 2 * 4 * 129 * 4, F32, "p (c j e) -> p c j e", c=2, j=4)
            accS2 = region(O_TMP2, 2 * 4 * 129 * 4, F32, "p (c x) -> p c x", c=2)
            t0s = region(O_TMP2 + 4160, 2048, F32, "p (j e) -> p j e", j=4)
            t1s = region(O_TMP2 + 6208, 2048, F32, "p (j e) -> p j e", j=4)
            tns = region(O_TMP2 + 8256, 1024, BF16, "p (j e) -> p j e", j=4)
            rec = region(O_TMP2 + 9280, 32, F32, "p (c j) -> p c j", c=2)
            ssq = region(O_TMP2 + 9312, 48, F32, "p (k j) -> p k j", k=3)
            junk2 = region(O_TMP2 + 9376, 256, BF16)
            ep_b = Buf("ep")
            sc.alias(ep_b, [scr_b])
            BQ = [(0, 512), (512, 512), (1024, 512), (1536, 512), (2048, 64)]
            wq_sl, wq_b = load_slab(w_in, 0, 1024)

            def q_proj(h):
                slot = h % 2
                for bi, (q0, n) in enumerate(BQ):
                    for kc in range(8):
                        sc.op("pe", lambda kc=kc: E["pe"].matmul(
                            ps_f32(7)[:, :n], lhsT=wq_sl[:, kc, h * 128:(h + 1) * 128], rhs=hT[:, kc, q0:q0 + n],
                            start=(kc == 0), stop=(kc == 7)), r=[wq_b] + hT_b, w=[psb[7]], signal=(kc == 7))
                    if bi < 4:
                        sc.op("act", lambda: E["act"].activation(out=qT[slot][:, q0:q0 + n], in_=ps_f32(7)[:, :n],
                                                                func=AF.Copy, scale=0.125), r=[psb[7]], w=[qT_b[slot]])
                    else:
                        sc.op("act", lambda: E["act"].activation(out=qTs[:, h, :], in_=ps_f32(7)[:, :n],
                                                                func=AF.Copy, scale=0.125), r=[psb[7]], w=[qTs_b])

            def epilogue(h, acc_list, nr, nj, dst_cols, defer=None):
                for (ap_, bb, c, j0, n) in acc_list:
                    sc.op("dve", lambda ap_=ap_, c=c, j0=j0, n=n: E["dve"].tensor_copy(
                        out=accS2[:nr, c, j0 * 129:(j0 + n) * 129], in_=ap_), r=[bb], w=[ep_b])
                sc.op("dve", lambda: E["dve"].reciprocal(out=rec[:nr, :, :nj], in_=accS[:nr, :, :nj, 128]), r=[ep_b], w=[ep_b])
                sc.op("dve", lambda: E["dve"].tensor_scalar(rec[:nr, 1, :nj], rec[:nr, 1, :nj], colv[:nr, C_NLAM:C_NLAM + 1], None,
                                                            op0=ALU.mult), r=[ep_b, colv_b], w=[ep_b])
                sc.op("dve", lambda: E["dve"].tensor_tensor(out=t0s[:nr, :nj, :], in0=accS[:nr, 0, :nj, 0:128],
                                                            in1=rec[:nr, 0, :nj].unsqueeze(2).to_broadcast([nr, nj, 128]), op=ALU.mult),
                      r=[ep_b], w=[ep_b])
                sc.op("dve", lambda: E["dve"].tensor_tensor(out=t1s[:nr, :nj, :], in0=accS[:nr, 1, :nj, 0:128],
                                                            in1=rec[:nr, 1, :nj].unsqueeze(2).to_broadcast([nr, nj, 128]), op=ALU.mult),
                      r=[ep_b], w=[ep_b])
                sc.op("dve", lambda: E["dve"].tensor_tensor(out=t0s[:nr, :nj, :], in0=t0s[:nr, :nj, :], in1=t1s[:nr, :nj, :], op=ALU.add),
                      r=[ep_b], w=[ep_b])
                for j in range(nj):
                    sc.op("act", lambda j=j: E["act"].activation(out=junk2[:nr, :], in_=t0s[:nr, j, :], func=AF.Square,
                                                                accum_out=ssq[:nr, 0, j:j + 1]), r=[ep_b], w=[ep_b])
                sc.op("act", lambda: E["act"].activation(out=ssq[:nr, 1, :nj], in_=ssq[:nr, 0, :nj], func=AF.Ln, scale=1.0 / 128, bias=EPS),
                      r=[ep_b], w=[ep_b])
                sc.op("act", lambda: E["act"].activation(out=ssq[:nr, 2, :nj], in_=ssq[:nr, 1, :nj], func=AF.Exp, scale=-0.5),
                      r=[ep_b], w=[ep_b])
                sc.op("dve", lambda: E["dve"].tensor_tensor(out=tns[:nr, :nj, :], in0=t0s[:nr, :nj, :],
                                                            in1=ssq[:nr, 2, :nj].unsqueeze(2).to_broadcast([nr, nj, 128]), op=ALU.mult),
                      r=[ep_b], w=[ep_b])
                def part2():
                    pv = ps_bf(7).rearrange("p (j t) -> p j t", j=8)
                    for j in range(nj):
                        sc.op("pe", lambda j=j: E["pe"].transpose(out=pv[:, j, :nr], in_=tns[:nr, j, :], identity=identb[:nr, :nr]),
                              r=[ep_b, cB], w=[psb[7]], signal=(j == nj - 1))
                    for j in range(nj):
                        c0 = dst_cols(j)
                        sc.op("dve", lambda j=j, c0=c0: E["dve"].tensor_scalar(a_outT[:, h, c0:c0 + nr], pv[:, j, :nr],
                                                                              colv[:, C_GSUB:C_GSUB + 1], None, op0=ALU.mult),
                              r=[psb[7], colv_b], w=[aT_b[h]])
                if defer is None:
                    part2()
                else:
                    defer.append(part2)

            def attn_prompt(h):
                slot = h % 2
                steps = []
                for qb in range(4):
                    last_kt = 4 * qb + 3
                    for kt in range(0, last_kt + 1):
                        j0 = max(0, kt - 4 * qb)
                        for c in range(2):
                            near = []
                            for j in range(j0, 4):
                                d = (4 * qb + j) - kt
                                if d == 0:
                                    near.append((j, 0))
                                elif d == 1:
                                    near.append((j, 1))
                            steps.append(dict(qb=qb, kt=kt, c=c, j0=j0, ncols=512 - 128 * j0, qs=qb * 512 + 128 * j0, near=near,
                                              sb=len(steps) % 4, last=(kt == last_kt and c == 1)))
                started = {}

                def emit_S(st):
                    sb, c, kt, j0, ncols, qs, near = st["sb"], st["c"], st["kt"], st["j0"], st["ncols"], st["qs"], st["near"]
                    sc.op("pe", lambda: E["pe"].matmul(
                        ps_f32(sb)[:, :ncols], lhsT=kT[c * 64:(c + 1) * 64, h, kt * 128:(kt + 1) * 128],
                        rhs=qT[slot][c * 64:(c + 1) * 64, qs:qs + ncols], start=True, stop=True),
                        r=[kT_b[kt], qT_b[slot]], w=[psb[sb]], signal=True)
                    sc.op("act", lambda: E["act"].activation(out=PT[sb][:, :ncols], in_=ps_f32(sb)[:, :ncols], func=AF.Exp),
                          r=[psb[sb]], w=[PT_b[sb]])
                    for (j, kind) in near:
                        sc.op("dve", lambda j=j, kind=kind: E["dve"].tensor_tensor(
                            out=PT[sb][:, (j - j0) * 128:(j - j0 + 1) * 128], in0=PT[sb][:, (j - j0) * 128:(j - j0 + 1) * 128],
                            in1=ebt[:, h, kind, :], op=ALU.mult), r=[cB], w=[PT_b[sb]])

                def emit_PV(st):
                    sb, c, kt, j0, qb = st["sb"], st["c"], st["kt"], st["j0"], st["qb"]
                    stt = started.setdefault(qb, set())
                    for j in range(j0, 4):
                        if j < 3:
                            bank, col = 4 + c, j * 129
                        else:
                            bank, col = 6, c * 129
                        first = bank not in stt
                        stt.add(bank)
                        fin = (kt == 4 * qb + j)
                        sc.op("pe", lambda j=j, bank=bank, col=col, first=first, fin=fin: E["pe"].matmul(
                            ps_f32(bank)[:, col:col + 129], lhsT=PT[sb][:, (j - j0) * 128:(j - j0 + 1) * 128],
                            rhs=v_aug[:, kt, h, 0:129], start=first, stop=fin, skip_group_check=True),
                            r=[PT_b[sb], v_b[kt]], w=[psb[bank]], signal=(j == 3))
                    if st["last"]:
                        acc_list = [(ps_f32(4)[:, 0:387], psb[4], 0, 0, 3), (ps_f32(5)[:, 0:387], psb[5], 1, 0, 3),
                                    (ps_f32(6)[:, 0:129], psb[6], 0, 3, 1), (ps_f32(6)[:, 129:258], psb[6], 1, 3, 1)]
                        epilogue(h, acc_list, 128, 4, lambda j, qb=qb: qb * 512 + j * 128, defer=pending)
                        pend_at[0] = cur_i[0] + 6

                LA = 2
                pending = []
                pend_at = [None]
                cur_i = [0]
                for i in range(0, len(steps) + LA, 2):
                    cur_i[0] = i
                    for k2 in (i, i + 1):
                        if k2 < len(steps):
                            emit_S(steps[k2])
                    if pending and pend_at[0] is not None and i >= pend_at[0]:
                        pending.pop(0)()
                        pend_at[0] = None
                    for k2 in (i - LA, i - LA + 1):
                        if 0 <= k2 < len(steps):
                            emit_PV(steps[k2])
                while pending:
                    pending.pop(0)()

            q_proj(0)
            for h in range(8):
                if h + 1 < 8:
                    q_proj(h + 1)
                attn_prompt(h)

            KcT = region(O_R1, 32768, BF16, "p (h t) -> p h t", h=8)
            KcT_b = Buf("KcT")
            Vc = region(O_R1 + 33792, 16 * 8 * 130 * 2, BF16, "p (t h e) -> p t h e", t=16, h=8)
            Vc_b = Buf("Vc")
            kTs = region(O_TMP + 13568, 1024, BF16, "p (h t) -> p h t", h=8)
            kTs_b = Buf("kTs")
            sc.op("dve", lambda: E["dve"].tensor_copy(out=kTs, in_=kT[:, :, 2048:2112]), r=kT_b[16:18], w=[kTs_b])
            sc.alias(KcT_b, kT_b)
            sc.alias(Vc_b, v_b[:16])
            kst = [region(O_TMP + i * 2048, 2048, BF16) for i in range(2)]
            kst_b = [Buf("kst%d" % i) for i in range(2)]
            for b_ in kst_b:
                sc.alias(b_, qT_b)
            for s_ in range(2):
                for kt in range(16):
                    sc.dma("pool", [(Vc[:, kt, :, 0:128], cv[s_, kt * 128:(kt + 1) * 128, :].rearrange("p (h e) -> p h e", h=8))],
                           w=[Vc_b])
                for kt in range(16):
                    ks, ks_b = kst[kt % 2], kst_b[kt % 2]
                    sc.dma("pool", [(ks, ck[s_, kt * 128:(kt + 1) * 128, :])], w=[ks_b])
                    pb = kt % 2
                    pv = ps_bf(pb).rearrange("p (c t) -> p c t", c=8)
                    for c in range(8):
                        sc.op("pe", lambda c=c, pv=pv: E["pe"].transpose(out=pv[:, c, :], in_=ks[:, c * 128:(c + 1) * 128], identity=identb),
                              r=[ks_b, cB], w=[psb[pb]], signal=(c == 7))
                    sc.op("dve", lambda pv=pv: E["dve"].tensor_copy(out=KcT[:, :, kt * 128:(kt + 1) * 128], in_=pv),
                          r=[psb[pb]], w=[KcT_b])
                tnew = 16 + s_
                c0n = 2048 + 32 * s_
                def s_stage(h, c):
                    sb = 2 + c
                    qsl = qTs[c * 64:(c + 1) * 64, h, 32 * s_:32 * s_ + 32]
                    for kt in range(16):
                        sc.op("pe", lambda kt=kt: E["pe"].matmul(
                            ps_f32(sb)[:, kt * 32:(kt + 1) * 32], lhsT=KcT[c * 64:(c + 1) * 64, h, kt * 128:(kt + 1) * 128], rhs=qsl,
                            start=(kt == 0), stop=True, skip_group_check=True), r=[KcT_b, qTs_b], w=[psb[sb]], signal=(kt == 15))
                    nb = 6
                    ncol = ((h % 2) * 2 + c) * 32
                    sc.op("pe", lambda: E["pe"].matmul(
                        ps_f32(nb)[0:32, ncol:ncol + 32], lhsT=kTs[c * 64:(c + 1) * 64, h, 32 * s_:32 * s_ + 32], rhs=qsl,
                        start=True, stop=True, skip_group_check=True), r=[kTs_b, qTs_b], w=[psb[nb]], signal=True)
                    sc.op("act", lambda: E["act"].activation(out=PT[sb][:, :512], in_=ps_f32(sb)[:, :512], func=AF.Exp),
                          r=[psb[sb]], w=[PT_b[sb]])
                    sc.op("act", lambda: E["act"].activation(out=PT[c][0:32, 0:32], in_=ps_f32(nb)[0:32, ncol:ncol + 32], func=AF.Exp),
                          r=[psb[nb]], w=[PT_b[c]])
                    sc.op("dve", lambda: E["dve"].tensor_tensor(out=PT[sb][:, 480:512], in0=PT[sb][:, 480:512], in1=ebt[:, h, 1, 0:32],
                                                                op=ALU.mult), r=[cB], w=[PT_b[sb]])
                    sc.op("dve", lambda: E["dve"].tensor_tensor(out=PT[c][0:32, 0:32], in0=PT[c][0:32, 0:32], in1=ebt[0:32, h, 0, 0:32],
                                                                op=ALU.mult), r=[cB], w=[PT_b[c]])

                def pv_stage(h, c):
                    sb = 2 + c
                    for kt in range(16):
                        sc.op("pe", lambda kt=kt: E["pe"].matmul(
                            ps_f32(4 + c)[0:32, 0:129], lhsT=PT[sb][:, kt * 32:(kt + 1) * 32], rhs=Vc[:, kt, h, 0:129],
                            start=(kt == 0), stop=False), r=[PT_b[sb], Vc_b], w=[psb[4 + c]], signal=False)
                    sc.op("pe", lambda: E["pe"].matmul(
                        ps_f32(4 + c)[0:32, 0:129], lhsT=PT[c][0:32, 0:32], rhs=sv_aug[0:32, s_, h, 0:129],
                        start=False, stop=True), r=[PT_b[c], v_b[tnew]], w=[psb[4 + c]], signal=True)
                    if c == 1:
                        acc_list = [(ps_f32(4)[0:32, 0:129], psb[4], 0, 0, 1), (ps_f32(5)[0:32, 0:129], psb[5], 1, 0, 1)]
                        epilogue(h, acc_list, 32, 1, lambda j, c0n=c0n: c0n)

                hc = [(h, c) for h in range(8) for c in range(2)]
                s_stage(*hc[0])
                for i in range(len(hc)):
                    if i + 1 < len(hc):
                        s_stage(*hc[i + 1])
                    pv_stage(*hc[i])

        def load_w(dst, dst_b, w_ap, c0, ncols, kc0=0, nkc=8):
            src = w_ap[kc0 * 128:(kc0 + nkc) * 128, c0:c0 + ncols].rearrange("(c p) n -> p c n", p=128)
            pairs = []
            for k0 in range(0, nkc, 2):
                k1 = min(nkc, k0 + 2)
                pairs.append((dst[:, k0:k1, 0:ncols], src[:, k0:k1, :]))
            sc.dma("pool", pairs, w=[dst_b])

        if phase_limit >= 4:
            x2 = region(O_R1, 65536, F32, "p (t d) -> p t d", t=16)
            x2s = region(O_TMP + 10560, 4096, F32)
            x2_b = [Buf("x2_%d" % i) for i in range(17)]
            for b_ in x2_b[:16]:
                sc.alias(b_, [KcT_b, Vc_b, kTs_b] + kT_b + v_b)
            sc.alias(x2_b[16], [qTs_b, kTs_b] + PT_b + kst_b + qT_b)
            for tt in range(16):
                sc.dma("sp", [(x2[:, tt, :], xp[tt * 128:(tt + 1) * 128, :])], w=[x2_b[tt]])
            sc.dma("sp", [(x2s[0:64, :], xs[0:64, :])], w=[x2_b[16]])
            with nc.allow_non_contiguous_dma(reason="tiny per-channel columns"):
                sc.dma("sp", [(colv2[:, C2_BG:C2_BG + 24], b_gate.rearrange("o (c p) -> p (o c)", p=128))], w=[colv_b])
            wo = region(O_TMP2, 16384, BF16, "p (c n) -> p c n", c=8)
            wo_b = Buf("wo")
            sc.alias(wo_b, [ep_b, scr_b])
            mT = region(O_TMP, 8192, BF16, "p (c t) -> p c t", c=8)
            mT_b = Buf("mT")
            gs = region(O_TMP + 8192, 2048, F32)
            gs_b = Buf("gs")
            sc.alias(mT_b, qT_b + kst_b + PT_b)
            sc.alias(gs_b, qT_b + kst_b + PT_b)

            def merge_branch(bi, srcT, srcT_bufs, wproj_ap):
                wp, wp_b = load_slab(wproj_ap, 0, 1024)
                wg, wg_b = load_slab(w_gate, bi * 1024, 1024)
                load_w(wo, wo_b, w_out, 0, 1024)
                cnt = [0]
                for (q0, n) in BQ:
                    for m in range(8):
                        pa, pg = cnt[0] % 2, 2 + cnt[0] % 2
                        cnt[0] += 1
                        for kc in range(8):
                            sc.op("pe", lambda kc=kc: E["pe"].matmul(ps_f32(pa)[:, :n], lhsT=wp[:, kc, m * 128:(m + 1) * 128],
                                                                    rhs=srcT[:, kc, q0:q0 + n], start=(kc == 0), stop=(kc == 7)),
                                  r=[wp_b] + srcT_bufs, w=[psb[pa]], signal=(kc == 7))
                        for kc in range(8):
                            sc.op("pe", lambda kc=kc: E["pe"].matmul(ps_f32(pg)[:, :n], lhsT=wg[:, kc, m * 128:(m + 1) * 128],
                                                                    rhs=hT[:, kc, q0:q0 + n], start=(kc == 0), stop=(kc == 7)),
                                  r=[wg_b] + hT_b, w=[psb[pg]], signal=(kc == 7))
                        sc.op("act", lambda: E["act"].activation(out=gs[:, :n], in_=ps_f32(pg)[:, :n], func=AF.Sigmoid,
                                                                bias=colv2[:, C2_BG + bi * 8 + m:C2_BG + bi * 8 + m + 1]),
                              r=[psb[pg], colv_b], w=[gs_b])
                        sc.op("dve", lambda: E["dve"].tensor_tensor(out=mT[:, m, :n], in0=gs[:, :n], in1=ps_f32(pa)[:, :n], op=ALU.mult),
                              r=[gs_b, psb[pa]], w=[mT_b])
                    ntile = (n + 127) // 128
                    for ti in range(ntile):
                        rows = min(128, n - ti * 128)
                        if q0 < 2048:
                            tt = q0 // 128 + ti
                            xt_ = x2[:, tt, :]
                        else:
                            tt = 16
                            xt_ = x2s
                        for half in range(2):
                            pb = 4 + (2 * ti + half) % 4
                            for kc in range(8):
                                sc.op("pe", lambda kc=kc: E["pe"].matmul(
                                    ps_f32(pb)[:rows, :], lhsT=mT[:, kc, ti * 128:ti * 128 + rows], rhs=wo[:, kc, half * 512:(half + 1) * 512],
                                    start=(kc == 0), stop=(kc == 7)), r=[mT_b, wo_b], w=[psb[pb]], signal=(kc == 7))
                            sc.op("dve", lambda: E["dve"].tensor_tensor(
                                out=xt_[:rows, half * 512:(half + 1) * 512], in0=xt_[:rows, half * 512:(half + 1) * 512],
                                in1=ps_f32(pb)[:rows, :], op=ALU.add), r=[psb[pb]], w=[x2_b[tt]])

            merge_branch(0, a_outT, aT_b, w_proj_a)

        if phase_limit >= 5:
            c_outT = region(O_A, 33792, BF16, "p (c t) -> p c t", c=8)
            cT_b = [Buf("cT%d" % i) for i in range(4)]
            for b_ in cT_b + stg_b:
                sc.alias(b_, aT_b)
            grow2 = region(O_A + 20480, 4096, F32)
            grow2_b = Buf("grow2")
            mst = region(O_A + 24576, 4096, F32)
            mst_b = Buf("mst")
            mxn = region(O_A + 28672, 2048, BF16)
            mxn_b = Buf("mxn")
            junkm = region(O_A + 30720, 2048, BF16)
            ssm = region(O_A + 32768, 3 * 4 * 4, F32, "p (k t) -> p k t", k=3)
            mjs_b = Buf("mjs")
            for b_ in (grow2_b, mst_b, mxn_b, mjs_b):
                sc.alias(b_, aT_b)
            memT = region(O_TMP2, 4096, BF16, "p (c t) -> p c t", c=8)
            memT_b = [Buf("memT%d" % i) for i in range(2)]
            for b_ in memT_b:
                sc.alias(b_, [wo_b])
            memkT = region(O_TMP, 4096, BF16, "p (c t) -> p c t", c=8)
            memkT_b = [Buf("memkT%d" % i) for i in range(2)]
            memv = region(O_TMP + 4096, 4128, BF16, "p (t h e) -> p t h e", t=2, h=4)
            memv_b = [Buf("memv%d" % i) for i in range(2)]
            qcTs = region(O_TMP + 8224, 1024, BF16, "p (h c t) -> p h c t", h=4, c=2)
            qcTs_b = Buf("qcTs")
            for b_ in memkT_b + memv_b + [qcTs_b]:
                sc.alias(b_, [mT_b, gs_b])
            sc.dma("sp", [(grow2, norm_mem.partition_broadcast(128))], w=[grow2_b])
            sc.op("pool", lambda: E["pool"].memset(memv[:, :, :, 256:257], 1.0), w=memv_b)
            MT = [(0, 128), (128, 128)]
            for mt in range(2):
                sc.dma("sp", [(mst, mem[mt * 128:(mt + 1) * 128, :])], w=[mst_b])
                sc.op("act", lambda: E["act"].activation(out=junkm, in_=mst, func=AF.Square, accum_out=ssm[:, 0, mt:mt + 1]),
                      r=[mst_b], w=[mjs_b])
                sc.op("act", lambda: E["act"].activation(out=ssm[:, 1, mt:mt + 1], in_=ssm[:, 0, mt:mt + 1], func=AF.Ln, scale=1.0 / D, bias=EPS),
                      r=[mjs_b], w=[mjs_b])
                sc.op("act", lambda: E["act"].activation(out=ssm[:, 2, mt:mt + 1], in_=ssm[:, 1, mt:mt + 1], func=AF.Exp, scale=-0.5),
                      r=[mjs_b], w=[mjs_b])
                sc.op("dve", lambda: E["dve"].scalar_tensor_tensor(out=mxn, in0=mst, scalar=ssm[:, 2, mt:mt + 1], in1=grow2,
                                                                   op0=ALU.mult, op1=ALU.mult), r=[mst_b, mjs_b, grow2_b], w=[mxn_b])
                transpose_to_T(mxn, mxn_b, 128, memT, memT_b[mt], mt * 128, mt % 2, "act")
            wmk_sl = load_slab(w_mem_kv, 0, 1024)
            wmv_sl = load_slab(w_mem_kv, 1024, 1024)
            kv_phase(memT, memT_b, MT, wmk_sl, wmv_sl,
                     lambda tt: nmk[tt * 128:(tt + 1) * 128, :], lambda tt: nmv[tt * 128:(tt + 1) * 128, :],
                     memkT, memkT_b, lambda tt: memv[:, tt], memv_b, 2, 256)

            for b_ in cT_b:
                sc.alias(b_, stg_b)
            qcT = region(O_TMP2, 8448, BF16, "p (c t) -> p c t", c=2)
            qcT_b = Buf("qcT")
            sc.alias(qcT_b, memT_b)
            PTc = [region(O_TMP2 + 8448 + i * 1024, 1024, BF16) for i in range(4)]
            PTc_b = [Buf("PTc%d" % i) for i in range(4)]
            cn = [region(O_TMP2 + 12544 + i * 512, 512, BF16) for i in range(2)]
            cn_b = [Buf("cn%d" % i) for i in range(2)]
            recc = region(O_TMP2 + 13568, 32, F32)
            recc_b = Buf("recc")
            cks = region(O_TMP2 + 13600, 2048, BF16)
            cks_b = Buf("cks")
            for b_ in PTc_b + cn_b + [recc_b, cks_b]:
                sc.alias(b_, [wo_b])
            wqc, wqc_b = load_slab(w_in, 5120, 1024)
            ccount = [0]

            def c_epilogue(h, accbank, nr, c0):
                i = ccount[0] % 2
                ccount[0] += 1
                sc.op("dve", lambda: E["dve"].reciprocal(out=recc[:nr, i:i + 1], in_=ps_f32(accbank)[:nr, 256:257]), r=[psb[accbank]], w=[recc_b])
                sc.op("dve", lambda: E["dve"].tensor_scalar(cn[i][:nr, :], ps_f32(accbank)[:nr, 0:256], recc[:nr, i:i + 1], None, op0=ALU.mult),
                      r=[psb[accbank], recc_b], w=[cn_b[i]])
                tb = 2 + i
                pv = ps_bf(tb).rearrange("p (c t) -> p c t", c=8)
                for dcx in range(2):
                    sc.op("pe", lambda dcx=dcx: E["pe"].transpose(out=pv[:, dcx, :nr], in_=cn[i][:nr, dcx * 128:(dcx + 1) * 128],
                                                                 identity=identb[:nr, :nr]), r=[cn_b[i], cB], w=[psb[tb]], signal=(dcx == 1))
                sc.op("act", lambda: E["act"].copy(out=c_outT[:, 2 * h:2 * h + 2, c0:c0 + nr], in_=pv[:, 0:2, :nr]), r=[psb[tb]], w=[cT_b[h]])

            for h in range(4):
                for dc in range(2):
                    for bi, (q0, n) in enumerate(BQ):
                        tb = 2 + (bi % 2)
                        for kc in range(8):
                            sc.op("pe", lambda kc=kc: E["pe"].matmul(
                                ps_f32(tb)[:, :n], lhsT=wqc[:, kc, h * 256 + dc * 128:h * 256 + (dc + 1) * 128], rhs=hT[:, kc, q0:q0 + n],
                                start=(kc == 0), stop=(kc == 7)), r=[wqc_b] + hT_b, w=[psb[tb]], signal=(kc == 7))
                        if bi < 4:
                            sc.op("act", lambda: E["act"].activation(out=qcT[:, dc, q0:q0 + n], in_=ps_f32(tb)[:, :n], func=AF.Copy, scale=0.0625),
                                  r=[psb[tb]], w=[qcT_b])
                        else:
                            sc.op("act", lambda: E["act"].activation(out=qcTs[:, h, dc, :], in_=ps_f32(tb)[:, :n], func=AF.Copy, scale=0.0625),
                                  r=[psb[tb]], w=[qcTs_b])
                for qb in range(4):
                    q0 = qb * 512
                    for mt in range(2):
                        sb = mt
                        for dc in range(2):
                            sc.op("pe", lambda dc=dc: E["pe"].matmul(
                                ps_f32(sb)[:, :512], lhsT=memkT[:, 2 * h + dc, mt * 128:(mt + 1) * 128], rhs=qcT[:, dc, q0:q0 + 512],
                                start=(dc == 0), stop=(dc == 1)), r=[memkT_b[mt], qcT_b], w=[psb[sb]], signal=(dc == 1))
                        pi = 2 * (qb % 2) + mt
                        sc.op("act", lambda: E["act"].activation(out=PTc[pi][:, :512], in_=ps_f32(sb)[:, :512], func=AF.Exp),
                              r=[psb[sb]], w=[PTc_b[pi]])
                        for j in range(4):
                            sc.op("pe", lambda j=j: E["pe"].matmul(
                                ps_f32(4 + j)[:, 0:257], lhsT=PTc[pi][:, j * 128:(j + 1) * 128], rhs=memv[:, mt, h, 0:257],
                                start=(mt == 0), stop=(mt == 1)), r=[PTc_b[pi], memv_b[mt]], w=[psb[4 + j]], signal=True)
                    for j in range(4):
                        c_epilogue(h, 4 + j, 128, q0 + j * 128)

            for s_ in range(2):
                for mt in range(2):
                    sc.dma("pool", [(memv[:, mt, :, 0:256], cmv[s_, mt * 128:(mt + 1) * 128, :].rearrange("p (h e) -> p h e", h=4))],
                           w=[memv_b[mt]])
                    sc.dma("pool", [(cks, cmk[s_, mt * 128:(mt + 1) * 128, :])], w=[cks_b])
                    pv = ps_bf(mt).rearrange("p (c t) -> p c t", c=8)
                    for c in range(8):
                        sc.op("pe", lambda c=c, pv=pv: E["pe"].transpose(out=pv[:, c, :], in_=cks[:, c * 128:(c + 1) * 128], identity=identb),
                              r=[cks_b, cB], w=[psb[mt]], signal=(c == 7))
                    sc.op("dve", lambda pv=pv: E["dve"].tensor_copy(out=memkT[:, :, mt * 128:(mt + 1) * 128], in_=pv), r=[psb[mt]], w=[memkT_b[mt]])
                for h in range(4):
                    sb = h % 2
                    first = True
                    for mt in range(2):
                        for dc in range(2):
                            sc.op("pe", lambda mt=mt, dc=dc, first=first: E["pe"].matmul(
                                ps_f32(sb)[:, mt * 32:(mt + 1) * 32], lhsT=memkT[:, 2 * h + dc, mt * 128:(mt + 1) * 128],
                                rhs=qcTs[:, h, dc, 32 * s_:32 * s_ + 32], start=first, stop=(dc == 1), skip_group_check=True),
                                r=memkT_b + [qcTs_b], w=[psb[sb]], signal=(mt == 1 and dc == 1))
                            first = False
                    pi = h % 4
                    sc.op("act", lambda: E["act"].activation(out=PTc[pi][:, :64], in_=ps_f32(sb)[:, :64], func=AF.Exp), r=[psb[sb]], w=[PTc_b[pi]])
                    ab = 4 + h
                    for mt in range(2):
                        sc.op("pe", lambda mt=mt: E["pe"].matmul(
                            ps_f32(ab)[0:32, 0:257], lhsT=PTc[pi][:, mt * 32:(mt + 1) * 32], rhs=memv[:, mt, h, 0:257],
                            start=(mt == 0), stop=(mt == 1)), r=[PTc_b[pi]] + memv_b, w=[psb[ab]], signal=(mt == 1))
                    c_epilogue(h, ab, 32, 2048 + 32 * s_)

            if DEBUG == "cT":
                sc.dma("sp", [(dbg[:, c * 2112:(c + 1) * 2112], c_outT[:, c, :]) for c in range(8)], r=cT_b)
            if phase_limit >= 6:
                merge_branch(2, c_outT, cT_b, w_proj_c)

        if phase_limit >= 7:
            b_outT = region(O_A, 33792, BF16, "p (c t) -> p c t", c=8)
            bT_b = [Buf("bT%d" % i) for i in range(8)]
            for b_ in bT_b:
                sc.alias(b_, cT_b + stg_b + [grow2_b, mst_b, mxn_b, mjs_b])
            colL = calloc(80 * 4, F32)
            colL_b = Buf("colL")
            with nc.allow_non_contiguous_dma(reason="tiny per-channel columns"):
                sc.dma("sp", [(colL[:, 0:8], b_rg_a.rearrange("o (c p) -> p (o c)", p=128)),
                              (colL[:, 8:16], b_rg_x.rearrange("o (c p) -> p (o c)", p=128)),
                              (colL[:, 16:24], conv_b.rearrange("o (c p) -> p (o c)", p=128)),
                              (colL[:, 24:56].rearrange("p (j c) -> p j c", j=4), conv_w.rearrange("j (c p) -> p j c", p=128)),
                              (colL[:, 72:80], rg_lambda.rearrange("o (c p) -> p (o c)", p=128))], w=[colL_b])
            sc.op("dve", lambda: E["dve"].tensor_scalar(colL[:, 0:16], colL[:, 0:16], -1.0, None, op0=ALU.mult), r=[colL_b], w=[colL_b])
            sc.op("act", lambda: E["act"].activation(out=colL[:, 72:80], in_=colL[:, 72:80], func=AF.Exp, scale=-1.0), r=[colL_b], w=[colL_b])
            sc.op("act", lambda: E["act"].activation(out=colL[:, 72:80], in_=colL[:, 72:80], func=AF.Ln, bias=1.0), r=[colL_b], w=[colL_b])
            sc.op("dve", lambda: E["dve"].tensor_scalar(colL[:, 56:64], colL[:, 72:80], -8.0, None, op0=ALU.mult), r=[colL_b], w=[colL_b])
            sc.op("dve", lambda: E["dve"].tensor_scalar(colL[:, 64:72], colL[:, 72:80], -16.0, None, op0=ALU.mult), r=[colL_b], w=[colL_b])

            lru_old = [wo_b, mT_b, gs_b, qcT_b, recc_b, cks_b, qcTs_b] + PTc_b + cn_b + memkT_b + memv_b + memT_b
            def lbuf(name):
                b_ = Buf(name)
                sc.alias(b_, lru_old)
                return b_
            wrga = region(O_TMP2, 2048, BF16, "p (n j) -> p n j", n=8)
            wrgx = region(O_TMP2 + 2048, 2048, BF16, "p (n j) -> p n j", n=8)
            wrg_b = lbuf("wrg")
            sc.dma("pool", [(wrga, w_rg_a.rearrange("n i j -> i n j")), (wrgx, w_rg_x.rearrange("n i j -> i n j"))], w=[wrg_b])
            LW = 256
            def tset(i):
                base = (O_TMP2 + 4096) if i == 0 else O_TMP
                o = [base]
                def mk(nbytes, dt, name):
                    ap = region(o[0], nbytes, dt)
                    o[0] += nbytes
                    return ap, lbuf(name + str(i))
                d = {}
                d["xpad"] = mk(1040, F32, "xpad")
                d["xc"] = mk(1024, F32, "xc")
                d["xcb"] = mk(512, BF16, "xcb")
                d["tRr"] = mk(1024, F32, "tRr")
                d["tA"] = mk(1024, F32, "tA")
                d["tA2"] = mk(1024, F32, "tA2")
                d["tI"] = mk(1024, F32, "tI")
                d["hh"] = mk(1024, F32, "hh")
                d["gbs"] = mk(1024, F32, "gbs")
                d["tG"] = mk(1024, F32, "tG")
                return d
            TS = [tset(0), tset(1)]
            hst = region(O_TMP + 9744, 16, F32); hst_b = lbuf("hst")
            xpad_b = [TS[0]["xpad"][1], TS[1]["xpad"][1]]
            xc_b, xcb_b, tA_b, tA2_b = TS[0]["xc"][1], TS[0]["xcb"][1], TS[0]["tA"][1], TS[0]["tA2"][1]
            tI_b, hh_b, gbs_b, tG_b, tRr_b = TS[1]["tI"][1], TS[1]["hh"][1], TS[1]["gbs"][1], TS[1]["tG"][1], TS[1]["tRr"][1]
            lru_all_b = [v_[1] for d_ in TS for v_ in d_.values()] + [hst_b]

            wxb, wxb_b = load_slab(w_in, 3072, 1024)
            wgb, wgb_b = load_slab(w_in, 4096, 1024)
            LB = [(i * LW, LW, -1) for i in range(2048 // LW)] + [(2048, 32, 0), (2080, 32, 1)]
            NPB = 2048 // LW
            blocks = [(n, li) for n in range(8) for li in range(len(LB))]

            def sig3(buf_ap, buf_b, src_ap, src_bufs, w, scale, bias):
                kw = {} if bias is None else {"bias": bias}
                sc.op("act", lambda: E["act"].activation(out=buf_ap[:, :w], in_=src_ap, func=AF.Exp, scale=-scale, **kw),
                      r=src_bufs + [colL_b], w=[buf_b])
                sc.op("act", lambda: E["act"].activation(out=buf_ap[:, :w], in_=buf_ap[:, :w], func=AF.Ln, bias=1.0), r=[buf_b], w=[buf_b])
                sc.op("act", lambda: E["act"].activation(out=buf_ap[:, :w], in_=buf_ap[:, :w], func=AF.Exp, scale=-1.0), r=[buf_b], w=[buf_b])

            def lru_stage1(k_):
                n, li = blocks[k_]
                q0, w, sq = LB[li]
                S_, P_ = TS[k_ % 2], TS[(k_ + 1) % 2]
                cur, cur_b = S_["xpad"]
                prv, prv_b = P_["xpad"]
                xc, xc_b_ = S_["xc"]
                xcb, xcb_b_ = S_["xcb"]
                px, pg, pa, pi_ = k_ % 2, 2 + k_ % 2, 4 + k_ % 2, 6 + k_ % 2
                for kc in range(8):
                    sc.op("pe", lambda kc=kc: E["pe"].matmul(ps_f32(px)[:, :w], lhsT=wxb[:, kc, n * 128:(n + 1) * 128], rhs=hT[:, kc, q0:q0 + w],
                                                            start=(kc == 0), stop=(kc == 7)), r=[wxb_b] + hT_b, w=[psb[px]], signal=(kc == 7))
                for kc in range(8):
                    sc.op("pe", lambda kc=kc: E["pe"].matmul(ps_f32(pg)[:, :w], lhsT=wgb[:, kc, n * 128:(n + 1) * 128], rhs=hT[:, kc, q0:q0 + w],
                                                            start=(kc == 0), stop=(kc == 7)), r=[wgb_b] + hT_b, w=[psb[pg]], signal=(kc == 7))
                if li == 0:
                    sc.op("dve", lambda: E["dve"].memset(cur[:, 0:3], 0.0), w=[cur_b])
                elif sq < 0:
                    sc.op("dve", lambda: E["dve"].tensor_copy(out=cur[:, 0:3], in_=prv[:, LW:LW + 3]), r=[prv_b], w=[cur_b])
                else:
                    with nc.allow_non_contiguous_dma(reason="tiny conv state"):
                        sc.dma("sp", [(cur[:, 0:3], sconv[sq, :, n * 128:(n + 1) * 128].rearrange("j c -> c j"))], w=[cur_b])
                sc.op("dve", lambda: E["dve"].tensor_copy(out=cur[:, 3:3 + w], in_=ps_f32(px)[:, :w]), r=[psb[px]], w=[cur_b])
                if li == NPB - 1 or sq >= 0:
                    dst = nconv if sq < 0 else nconvs[sq]
                    with nc.allow_non_contiguous_dma(reason="tiny conv state"):
                        sc.dma("sp", [(dst[:, n * 128:(n + 1) * 128].rearrange("j c -> c j"), cur[:, w:w + 3])], r=[cur_b])
                sc.op("dve", lambda: E["dve"].tensor_scalar(xc[:, :w], cur[:, 3:3 + w], colL[:, 24 + 3 * 8 + n:24 + 3 * 8 + n + 1],
                                                            colL[:, 16 + n:17 + n], op0=ALU.mult, op1=ALU.add), r=[cur_b, colL_b], w=[xc_b_])
                for j in range(3):
                    sc.op("dve", lambda j=j: E["dve"].scalar_tensor_tensor(out=xc[:, :w], in0=cur[:, j:j + w],
                                                                          scalar=colL[:, 24 + j * 8 + n:24 + j * 8 + n + 1], in1=xc[:, :w],
                                                                          op0=ALU.mult, op1=ALU.add), r=[cur_b, colL_b, xc_b_], w=[xc_b_])
                sc.op("dve", lambda: E["dve"].tensor_copy(out=xcb[:, :w], in_=xc[:, :w]), r=[xc_b_], w=[xcb_b_])
                sc.op("pe", lambda: E["pe"].matmul(ps_f32(pa)[:, :w], lhsT=wrga[:, n, :], rhs=xcb[:, :w], start=True, stop=True),
                      r=[wrg_b, xcb_b_], w=[psb[pa]])
                sc.op("pe", lambda: E["pe"].matmul(ps_f32(pi_)[:, :w], lhsT=wrgx[:, n, :], rhs=xcb[:, :w], start=True, stop=True),
                      r=[wrg_b, xcb_b_], w=[psb[pi_]])

            def lru_stage2(k_):
                n, li = blocks[k_]
                q0, w, sq = LB[li]
                S_ = TS[k_ % 2]
                xc, xc_b_ = S_["xc"]
                tRr, tRr_b_ = S_["tRr"]
                tA, tA_b_ = S_["tA"]
                tA2, tA2_b_ = S_["tA2"]
                tI, tI_b_ = S_["tI"]
                hh, hh_b_ = S_["hh"]
                gbs, gbs_b_ = S_["gbs"]
                tG, tG_b_ = S_["tG"]
                pg, pa, pi_ = 2 + k_ % 2, 4 + k_ % 2, 6 + k_ % 2
                sc.op("act", lambda: E["act"].copy(out=gbs[:, :w], in_=ps_f32(pg)[:, :w]), r=[psb[pg]], w=[gbs_b_])
                sc.op("act", lambda: E["act"].activation(out=tG[:, :w], in_=ps_f32(pg)[:, :w], func=AF.Square), r=[psb[pg]], w=[tG_b_])
                sc.op("dve", lambda: E["dve"].tensor_scalar(tG[:, :w], tG[:, :w], 0.044715, 1.0, op0=ALU.mult, op1=ALU.add), r=[tG_b_], w=[tG_b_])
                sc.op("dve", lambda: E["dve"].tensor_tensor(out=tG[:, :w], in0=tG[:, :w], in1=gbs[:, :w], op=ALU.mult), r=[tG_b_, gbs_b_], w=[tG_b_])
                sig3(tRr, tRr_b_, ps_f32(pa)[:, :w], [psb[pa]], w, 1.0, colL[:, n:n + 1])
                sc.op("act", lambda: E["act"].activation(out=tA[:, :w], in_=tRr[:, :w], func=AF.Exp, scale=colL[:, 56 + n:57 + n]),
                      r=[tRr_b_, colL_b], w=[tA_b_])
                sc.op("act", lambda: E["act"].activation(out=tA2[:, :w], in_=tRr[:, :w], func=AF.Exp, scale=colL[:, 64 + n:65 + n]),
                      r=[tRr_b_, colL_b], w=[tA2_b_])
                sc.op("act", lambda: E["act"].activation(out=tA2[:, :w], in_=tA2[:, :w], func=AF.Ln, scale=-1.0, bias=1.0), r=[tA2_b_], w=[tA2_b_])
                sc.op("act", lambda: E["act"].activation(out=tA2[:, :w], in_=tA2[:, :w], func=AF.Exp, scale=0.5), r=[tA2_b_], w=[tA2_b_])
                sig3(tI, tI_b_, ps_f32(pi_)[:, :w], [psb[pi_]], w, 1.0, colL[:, 8 + n:9 + n])
                sig3(tG, tG_b_, tG[:, :w], [tG_b_], w, 1.5957691216057308, None)
                sc.op("dve", lambda: E["dve"].tensor_tensor(out=tI[:, :w], in0=tI[:, :w], in1=tA2[:, :w], op=ALU.mult), r=[tI_b_, tA2_b_], w=[tI_b_])
                sc.op("dve", lambda: E["dve"].tensor_tensor(out=tI[:, :w], in0=tI[:, :w], in1=xc[:, :w], op=ALU.mult), r=[tI_b_, xc_b_], w=[tI_b_])
                if li == 0:
                    init = 0.0
                    rr = []
                elif sq < 0:
                    init = hst[:, 0:1]
                    rr = [hst_b]
                else:
                    with nc.allow_non_contiguous_dma(reason="tiny lru state"):
                        sc.dma("sp", [(hst[:, 1 + sq:2 + sq], slru[sq:sq + 1, n * 128:(n + 1) * 128].rearrange("o c -> c o"))], w=[hst_b])
                    init = hst[:, 1 + sq:2 + sq]
                    rr = [hst_b]
                sc.op("dve", lambda: E["dve"].tensor_tensor_scan(out=hh[:, :w], data0=tA[:, :w], data1=tI[:, :w], initial=init,
                                                                 op0=ALU.mult, op1=ALU.add), r=[tA_b_, tI_b_] + rr, w=[hh_b_])
                if sq < 0 and li < NPB - 1:
                    sc.op("dve", lambda: E["dve"].tensor_copy(out=hst[:, 0:1], in_=hh[:, w - 1:w]), r=[hh_b_], w=[hst_b])
                if li == NPB - 1 or sq >= 0:
                    dst = nlru[0:1, n * 128:(n + 1) * 128] if sq < 0 else nlrus[sq:sq + 1, n * 128:(n + 1) * 128]
                    with nc.allow_non_contiguous_dma(reason="tiny lru state"):
                        sc.dma("sp", [(dst.rearrange("o c -> c o"), hh[:, w - 1:w])], r=[hh_b_])
                sc.op("dve", lambda: E["dve"].tensor_tensor(out=tG[:, :w], in0=tG[:, :w], in1=gbs[:, :w], op=ALU.mult), r=[tG_b_, gbs_b_], w=[tG_b_])
                sc.op("dve", lambda: E["dve"].tensor_tensor(out=b_outT[:, n, q0:q0 + w], in0=tG[:, :w], in1=hh[:, :w], op=ALU.mult),
                      r=[tG_b_, hh_b_], w=[bT_b[n]])

            lru_stage1(0)
            for k_ in range(len(blocks)):
                if k_ + 1 < len(blocks):
                    lru_stage1(k_ + 1)
                lru_stage2(k_)

            if DEBUG == "bT":
                sc.dma("sp", [(dbg[:, c * 2112:(c + 1) * 2112], b_outT[:, c, :]) for c in range(8)], r=bT_b)
            if phase_limit >= 8:
                merge_branch(1, b_outT, bT_b, w_proj_b)

        if phase_limit >= 9:
            T17 = [(i * 128, 128, x2[:, i, :], x2_b[i]) for i in range(16)] + [(2048, 64, x2s, x2_b[16])]
            fin_old = [wo_b, wrg_b] + lru_all_b
            grow3 = region(O_TMP2, 4096, F32); grow3_b = Buf("grow3"); sc.alias(grow3_b, fin_old)
            xn3 = [region(O_TMP2 + 4096 + i * 2048, 2048, BF16) for i in range(2)]
            xn3_b = [Buf("xn3_%d" % i) for i in range(2)]
            junk3 = region(O_TMP2 + 8192, 2048, BF16)
            ss3 = region(O_TMP2 + 10240, 3 * 17 * 4, F32, "p (k t) -> p k t", k=3)
            js3_b = Buf("js3")
            for b_ in xn3_b + [js3_b]:
                sc.alias(b_, fin_old)
            h2T = hT
            h2T_b = [Buf("h2T%d" % i) for i in range(17)]
            for b_ in h2T_b:
                sc.alias(b_, hT_b)
            sc.dma("sp", [(grow3, norm_ffn.partition_broadcast(128))], w=[grow3_b])

            def rms2(st, st_b, rows, ssi, g_ap, g_b, out_ap, out_b, junk_ap, ss_ap, js_b):
                sc.op("act", lambda: E["act"].activation(out=junk_ap[:rows], in_=st[:rows], func=AF.Square,
                                                        accum_out=ss_ap[:rows, 0, ssi:ssi + 1]), r=[st_b], w=[js_b])
                sc.op("act", lambda: E["act"].activation(out=ss_ap[:rows, 1, ssi:ssi + 1], in_=ss_ap[:rows, 0, ssi:ssi + 1],
                                                        func=AF.Ln, scale=1.0 / D, bias=EPS), r=[js_b], w=[js_b])
                sc.op("act", lambda: E["act"].activation(out=ss_ap[:rows, 2, ssi:ssi + 1], in_=ss_ap[:rows, 1, ssi:ssi + 1],
                                                        func=AF.Exp, scale=-0.5), r=[js_b], w=[js_b])
                sc.op("dve", lambda: E["dve"].scalar_tensor_tensor(
                    out=out_ap[:rows], in0=st[:rows], scalar=ss_ap[:rows, 2, ssi:ssi + 1], in1=g_ap[:rows],
                    op0=ALU.mult, op1=ALU.mult), r=[st_b, js_b, g_b], w=[out_b])

            def p9_a(ti):
                t0, rows, xt_, xb_ = T17[ti]
                rms2(xt_, xb_, rows, ti, grow3, grow3_b, xn3[ti % 2], xn3_b[ti % 2], junk3, ss3, js3_b)

            def p9_b(ti):
                t0, rows, xt_, xb_ = T17[ti]
                transpose_to_T(xn3[ti % 2], xn3_b[ti % 2], rows, h2T, h2T_b[ti], t0, ti % 2, "act")

            p9_a(0)
            for ti in range(17):
                if ti + 1 < 17:
                    p9_a(ti + 1)
                p9_b(ti)

            actT = region(O_A, 33792, BF16, "p (c t) -> p c t", c=8)
            actT_b = Buf("actT")
            sc.alias(actT_b, bT_b)
            wo2 = region(O_TMP2, 16384, BF16, "p (c n) -> p c n", c=8)
            wo2_b = Buf("wo2")
            sc.alias(wo2_b, [grow3_b, js3_b] + xn3_b)
            tS = [region(O_TMP + i * 2048, 2048, F32) for i in range(2)]
            tS_b = [Buf("tS%d" % i) for i in range(2)]
            for b_ in tS_b:
                sc.alias(b_, [mT_b, gs_b] + lru_all_b)
            fc = [0]
            for (c0, cnt) in ((0, 8), (8, 8), (16, 6)):
                wgs, wgs_b = load_slab(w_ffn_in, c0 * 128, cnt * 128)
                wus, wus_b = load_slab(w_ffn_in, DFF + c0 * 128, cnt * 128)
                load_w(wo2, wo2_b, w_ffn_out, 0, 1024, kc0=c0, nkc=cnt)
                for (q0, n) in BQ:
                    for j in range(cnt):
                        k_ = fc[0]
                        fc[0] += 1
                        pg, pu = k_ % 2, 2 + k_ % 2
                        for kc in range(8):
                            sc.op("pe", lambda kc=kc: E["pe"].matmul(ps_f32(pg)[:, :n], lhsT=wgs[:, kc, j * 128:(j + 1) * 128],
                                                                    rhs=h2T[:, kc, q0:q0 + n], start=(kc == 0), stop=(kc == 7)),
                                  r=[wgs_b] + h2T_b, w=[psb[pg]], signal=(kc == 7))
                        for kc in range(8):
                            sc.op("pe", lambda kc=kc: E["pe"].matmul(ps_f32(pu)[:, :n], lhsT=wus[:, kc, j * 128:(j + 1) * 128],
                                                                    rhs=h2T[:, kc, q0:q0 + n], start=(kc == 0), stop=(kc == 7)),
                                  r=[wus_b] + h2T_b, w=[psb[pu]], signal=(kc == 7))
                        sc.op("act", lambda: E["act"].activation(out=tS[k_ % 2][:, :n], in_=ps_f32(pg)[:, :n], func=AF.Silu),
                              r=[psb[pg]], w=[tS_b[k_ % 2]])
                        sc.op("dve", lambda: E["dve"].tensor_tensor(out=actT[:, j, q0:q0 + n], in0=tS[k_ % 2][:, :n], in1=ps_f32(pu)[:, :n],
                                                                    op=ALU.mult), r=[tS_b[k_ % 2], psb[pu]], w=[actT_b])
                    ntile = (n + 127) // 128
                    for ti in range(ntile):
                        rows = min(128, n - ti * 128)
                        if q0 < 2048:
                            tt = q0 // 128 + ti
                            xt_ = x2[:, tt, :]
                        else:
                            tt = 16
                            xt_ = x2s
                        for half in range(2):
                            pb = 4 + (2 * ti + half) % 4
                            for j in range(cnt):
                                sc.op("pe", lambda j=j: E["pe"].matmul(
                                    ps_f32(pb)[:rows, :], lhsT=actT[:, j, q0 + ti * 128:q0 + ti * 128 + rows],
                                    rhs=wo2[:, j, half * 512:(half + 1) * 512], start=(j == 0), stop=(j == cnt - 1)),
                                    r=[actT_b, wo2_b], w=[psb[pb]], signal=(j == cnt - 1))
                            sc.op("dve", lambda: E["dve"].tensor_tensor(
                                out=xt_[:rows, half * 512:(half + 1) * 512], in0=xt_[:rows, half * 512:(half + 1) * 512],
                                in1=ps_f32(pb)[:rows, :], op=ALU.add), r=[psb[pb]], w=[x2_b[tt]])

            growF = region(O_A, 4096, F32); growF_b = Buf("growF"); sc.alias(growF_b, [actT_b])
            yst = [region(O_A + 4096 + i * 4096, 4096, F32) for i in range(2)]
            yst_b = [Buf("yst%d" % i) for i in range(2)]
            junkF = region(O_A + 12288, 2048, BF16)
            ssF = region(O_A + 14336, 3 * 17 * 4, F32, "p (k t) -> p k t", k=3)
            jsF_b = Buf("jsF")
            for b_ in yst_b + [jsF_b]:
                sc.alias(b_, [actT_b])
            sc.dma("sp", [(growF, norm_final.partition_broadcast(128))], w=[growF_b])
            for ti, (t0, rows, xt_, xb_) in enumerate(T17):
                rms2(xt_, xb_, rows, ti, growF, growF_b, yst[ti % 2], yst_b[ti % 2], junkF, ssF, jsF_b)
                dst = yp[t0:t0 + rows, :] if ti < 16 else ys[0:64, :]
                sc.dma("sp", [(dst, yst[ti % 2][:rows])], r=[yst_b[ti % 2]])

        if DEBUG == "x2":
            sc.dma("sp", [(dbg2[tt * 128:(tt + 1) * 128, :], x2[:, tt, :]) for tt in range(16)] + [(dbg2[2048:2112, :], x2s[0:64, :])],
                   r=x2_b)
        if DEBUG == "aT" and phase_limit >= 3:
            sc.dma("sp", [(dbg[:, c * 2112:(c + 1) * 2112], a_outT[:, c, :]) for c in range(8)], r=aT_b)
        if DEBUG == "hT":
            sc.dma("sp", [(dbg[:, c * 2112:(c + 1) * 2112], hT[:, c, :]) for c in range(8)], r=hT_b)
        sc.finish()
    return nc


_NC_CACHE = {}
_DBG = None


def _get_nc():
    lim = int(os.environ.get("MK_PHASE_LIMIT", "99"))
    if lim not in _NC_CACHE:
        _NC_CACHE[lim] = build_program(lim)
    return _NC_CACHE[lim]


def kernel(x_prompt, x_sample, mem_prompt, cache_k, cache_v, state_conv, state_lru, cache_mem_k, cache_mem_v,
           rel_table, norm_mix, w_in, lambda_q1, lambda_k1, lambda_q2, lambda_k2, subln_g, conv_w, conv_b,
           w_rg_a, b_rg_a, w_rg_x, b_rg_x, rg_lambda, norm_mem, w_mem_kv, w_proj_a, w_proj_b, w_proj_c,
           w_gate, b_gate, w_out, norm_ffn, w_ffn_in, w_ffn_out, norm_final):
    f = lambda a: np.ascontiguousarray(np.asarray(a, dtype=np.float32))
    nc = _get_nc()
    shared = {
        "rel_table": f(rel_table), "boh": _bias_onehot(), "ident": np.eye(128, dtype=np.float32), "aident": np.ascontiguousarray(np.eye(128, dtype=np.float32)[::-1]), "norm_mix": f(norm_mix), "w_in": f(w_in)[0],
        "lq1": f(lambda_q1), "lk1": f(lambda_k1), "lq2": f(lambda_q2), "lk2": f(lambda_k2),
        "subln_g": f(subln_g), "conv_w": f(conv_w)[0], "conv_b": f(conv_b), "w_rg_a": f(w_rg_a)[0],
        "b_rg_a": f(b_rg_a), "w_rg_x": f(w_rg_x)[0], "b_rg_x": f(b_rg_x), "rg_lambda": f(rg_lambda),
        "norm_mem": f(norm_mem), "w_mem_kv": f(w_mem_kv)[0], "w_proj_a": f(w_proj_a)[0],
        "w_proj_b": f(w_proj_b)[0], "w_proj_c": f(w_proj_c)[0], "w_gate": f(w_gate)[0], "b_gate": f(b_gate),
        "w_out": f(w_out)[0], "norm_ffn": f(norm_ffn), "w_ffn_in": f(w_ffn_in)[0], "w_ffn_out": f(w_ffn_out)[0],
        "norm_final": f(norm_final).reshape(1, D),
    }
    x_prompt = f(x_prompt); x_sample = f(x_sample); mem_prompt = f(mem_prompt)
    cache_k = f(cache_k); cache_v = f(cache_v); state_conv = f(state_conv); state_lru = f(state_lru)
    cache_mem_k = f(cache_mem_k); cache_mem_v = f(cache_mem_v)
    in_maps = []
    for c in range(NCORES):
        m = dict(shared)
        m["xp"] = x_prompt[c]
        m["xs"] = x_sample[2 * c:2 * c + 2].reshape(64, D)
        m["mem"] = mem_prompt[c]
        m["ck"] = cache_k[0, 2 * c:2 * c + 2].reshape(2, S, D)
        m["cv"] = cache_v[0, 2 * c:2 * c + 2].reshape(2, S, D)
        m["sconv"] = state_conv[0, 2 * c:2 * c + 2]
        m["slru"] = state_lru[0, 2 * c:2 * c + 2]
        m["cmk"] = cache_mem_k[0, 2 * c:2 * c + 2].reshape(2, 256, D)
        m["cmv"] = cache_mem_v[0, 2 * c:2 * c + 2].reshape(2, 256, D)
        in_maps.append(m)
    res = run_bass_kernel_spmd(nc, in_maps, core_ids=list(range(NCORES)))
    R = res.results
    global _DBG
    _DBG = R[0].get("debug_out") if isinstance(R[0], dict) else None
    global _DBG2
    _DBG2 = R[0].get("debug_x2") if isinstance(R[0], dict) else None
    cat = lambda k: np.stack([np.asarray(R[c][k], dtype=np.float32) for c in range(NCORES)])
    y_prompt = cat("yp")
    y_sample = cat("ys").reshape(16, 32, D)
    new_k_p = cat("nk").reshape(1, 8, S, 8, 2, 64)
    new_v_p = cat("out_v").reshape(1, 8, S, 8, 128)
    new_conv_p = cat("nconv").reshape(1, 8, 3, D)
    new_lru_p = cat("nlru").reshape(1, 8, D)
    new_mk = cat("nmk").reshape(1, 8, 256, 4, 256)
    new_mv = cat("nmv").reshape(1, 8, 256, 4, 256)
    new_k_s = cat("nks").reshape(1, 16, 32, 8, 2, 64)
    new_v_s = cat("out_vs").reshape(1, 16, 32, 8, 128)
    new_conv_s = cat("nconvs").reshape(1, 16, 3, D)
    new_lru_s = cat("nlrus").reshape(1, 16, D)
    return (y_prompt, y_sample, new_k_p, new_v_p, new_conv_p, new_lru_p, new_mk, new_mv,
            new_k_s, new_v_s, new_conv_s, new_lru_s)
```
